# Optimizing a Trainium2 kernel written in Bass

```python
import math
import jax
import jax.numpy as jnp
from jax import lax
import numpy as np

D_MODEL = 1024
BATCH = 4
SEQ = 8192
DEPTH = 1

CTX_LEN = 256
GRID_W = 64
EPS = 1e-6
N_MOD = 9
D_FF = 2816
GLA_HEADS = 4
GLA_DK = D_MODEL // 8
GLA_DV = D_MODEL // 4
GLA_RANK = 16
GLA_TAU = 16.0
GLA_CHUNK = 64
DIFF_HEADS = 8
DIFF_DH = D_MODEL // 16
Q_BLOCK = 128
ROPE_BASE = 10000.0
GLA_QK = GLA_HEADS * GLA_DK
GLA_V = GLA_HEADS * GLA_DV
DIFF_QK = DIFF_HEADS * 2 * DIFF_DH
DIFF_V = DIFF_HEADS * 2 * DIFF_DH
IN_SPLITS = (GLA_QK, GLA_QK, GLA_V, GLA_V, 2 * GLA_RANK, DIFF_QK, DIFF_QK, DIFF_V, 2 * D_MODEL)
IN_COLS = 2 * GLA_QK + 2 * GLA_V + 2 * GLA_RANK + 2 * DIFF_QK + DIFF_V + 2 * D_MODEL

kernel_name = 'hybrid_gla_diffattn_macaron_dit'


def _rmsnorm(x, g):
    xf = x.astype(jnp.float32)
    y = xf * lax.rsqrt(jnp.mean(xf * xf, axis=-1, keepdims=True) + EPS)
    return (y * g.astype(jnp.float32)).astype(x.dtype)


def _modnorm(h, g, shift, scale):
    return _rmsnorm(h, g) * (1.0 + scale) + shift


def _swiglu(h, w1, w3, w2):
    return (jax.nn.silu(h @ w1) * (h @ w3)) @ w2


def _split_cols(p):
    idx = np.cumsum(IN_SPLITS)[:-1].tolist()
    return jnp.split(p, idx, axis=-1)


def _axial_rope(t, row, col):
    half = DIFF_DH // 2
    quarter = half // 2
    inv = ROPE_BASE ** (-np.arange(quarter, dtype=np.float32) / quarter)

    def rot(u, pos):
        ang = pos.astype(jnp.float32)[:, None] * inv[None, :]
        cos = jnp.cos(ang)[None, :, None, None, :].astype(u.dtype)
        sin = jnp.sin(ang)[None, :, None, None, :].astype(u.dtype)
        u1, u2 = u[..., :quarter], u[..., quarter:]
        return jnp.concatenate([u1 * cos - u2 * sin, u2 * cos + u1 * sin], axis=-1)

    return jnp.concatenate([rot(t[..., :half], row), rot(t[..., half:], col)], axis=-1)


def _gla_scan(q, k, v, logf, s0):
    b_, L, h, _ = q.shape
    dv = v.shape[-1]
    n = L // GLA_CHUNK

    def chunks(t):
        return t.astype(jnp.float32).reshape(b_, n, GLA_CHUNK, h, t.shape[-1]).transpose(1, 0, 3, 2, 4)

    tril = jnp.tril(jnp.ones((GLA_CHUNK, GLA_CHUNK), dtype=bool))[:, :, None]

    def step(state, inp):
        qc, kc, vc, gc = inp
        cum = jnp.cumsum(gc, axis=2)
        o_inter = jnp.einsum('bhtd,bhde->bhte', qc * jnp.exp(cum), state)
        rel = jnp.where(tril, cum[:, :, :, None, :] - cum[:, :, None, :, :], -jnp.inf)
        attn = jnp.einsum('bhtd,bhsd,bhtsd->bhts', qc, kc, jnp.exp(rel))
        o_intra = jnp.einsum('bhts,bhse->bhte', attn, vc)
        last = cum[:, :, -1:, :]
        new_state = (jnp.exp(last[:, :, 0, :])[..., None] * state
                     + jnp.einsum('bhsd,bhse->bhde', kc * jnp.exp(last - cum), vc))
        return new_state, o_inter + o_intra

    state, o = lax.scan(step, s0, (chunks(q), chunks(k), chunks(v), chunks(logf)))
    o = o.transpose(1, 0, 3, 2, 4).reshape(b_, L, h, dv)
    return o, state


def _gla_direction(q, k, v, lr, w2, bias, s0, reverse):
    logf = jax.nn.log_sigmoid((lr @ w2 + bias).astype(jnp.float32)) / GLA_TAU
    logf = logf.reshape(q.shape)
    if reverse:
        q, k, v, logf = jnp.flip(q, 1), jnp.flip(k, 1), jnp.flip(v, 1), jnp.flip(logf, 1)
    o, s = _gla_scan(q, k, v, logf, s0)
    if reverse:
        o = jnp.flip(o, 1)
    return o.astype(v.dtype), s


def _gla_heads(q, k, v, L):
    b_ = q.shape[0]
    return (q.reshape(b_, L, GLA_HEADS, GLA_DK) * (GLA_DK ** -0.5),
            k.reshape(b_, L, GLA_HEADS, GLA_DK),
            v.reshape(b_, L, GLA_HEADS, GLA_DV))


def _gla_head_out(o, gate, norm_g, w_branch):
    b_, L = o.shape[:2]
    y = _rmsnorm(o, norm_g.reshape(GLA_HEADS, GLA_DV)).reshape(b_, L, GLA_V)
    return (y * jax.nn.silu(gate)) @ w_branch


def _diff_block(q, k, v, lam):
    s = jnp.einsum('bqhcd,bkhcd->bhcqk', q, k).astype(jnp.float32) * (DIFF_DH ** -0.5)
    p = jax.nn.softmax(s, axis=-1)
    a = p[:, :, 0] - lam * p[:, :, 1]
    return jnp.einsum('bhqk,bkhe->bqhe', a.astype(v.dtype), v)


def _diff_latent(q, k, v, lam):
    b_, L = q.shape[:2]
    nb = L // Q_BLOCK
    qb = q.reshape((b_, nb, Q_BLOCK) + q.shape[2:]).swapaxes(0, 1)
    o = lax.map(lambda qq: _diff_block(qq, k, v, lam), qb)
    return o.swapaxes(0, 1).reshape((b_, L) + o.shape[3:])


def _diff_head_out(o, norm_g, lam_init, w_branch):
    b_, L = o.shape[:2]
    y = _rmsnorm(o, norm_g.reshape(DIFF_HEADS, 2 * DIFF_DH)).reshape(b_, L, DIFF_V)
    return (y * (1.0 - lam_init)) @ w_branch


def setup_inputs(seed: int = 0) -> dict:
    key = jax.random.key(seed)
    ks = jax.random.split(key, 30)
    f32 = jnp.float32

    def nrm(k, shape, scale):
        return jax.random.normal(k, shape, f32) * scale

    def gain(k, shape):
        return 1.0 + 0.02 * jax.random.normal(k, shape, f32)

    L = DEPTH
    return {
        'x': nrm(ks[0], (BATCH, SEQ, D_MODEL), 1.0),
        'c': nrm(ks[1], (BATCH, D_MODEL), 1.0),
        'ctx': nrm(ks[2], (BATCH, CTX_LEN, D_MODEL), 1.0),
        'c_ctx': nrm(ks[3], (D_MODEL,), 1.0),
        'w_ada': nrm(ks[4], (L, D_MODEL, N_MOD * D_MODEL), 0.5 * D_MODEL ** -0.5),
        'b_ada': nrm(ks[5], (L, N_MOD * D_MODEL), 0.02),
        'ffn1_norm': gain(ks[6], (L, D_MODEL)),
        'ffn1_w1': nrm(ks[7], (L, D_MODEL, D_FF), D_MODEL ** -0.5),
        'ffn1_w3': nrm(ks[8], (L, D_MODEL, D_FF), D_MODEL ** -0.5),
        'ffn1_w2': nrm(ks[9], (L, D_FF, D_MODEL), D_FF ** -0.5),
        'mix_norm': gain(ks[10], (L, D_MODEL)),
        'w_in': nrm(ks[11], (L, D_MODEL, IN_COLS), D_MODEL ** -0.5),
        'gla_gate_w2': nrm(ks[12], (L, 2, GLA_RANK, GLA_QK), GLA_RANK ** -0.5),
        'gla_gate_b': nrm(ks[13], (L, 2, GLA_QK), 0.1),
        'gla_out_norm': gain(ks[14], (L, GLA_V)),
        'diff_lambda': nrm(ks[15], (L, 4, DIFF_DH), 0.1),
        'diff_out_norm': gain(ks[16], (L, DIFF_V)),
        'w_branch_gla': nrm(ks[17], (L, GLA_V, D_MODEL), GLA_V ** -0.5),
        'w_branch_diff': nrm(ks[18], (L, DIFF_V, D_MODEL), DIFF_V ** -0.5),
        'w_out': nrm(ks[19], (L, D_MODEL, D_MODEL), D_MODEL ** -0.5),
        'ffn2_norm': gain(ks[20], (L, D_MODEL)),
        'ffn2_w1': nrm(ks[21], (L, D_MODEL, D_FF), D_MODEL ** -0.5),
        'ffn2_w3': nrm(ks[22], (L, D_MODEL, D_FF), D_MODEL ** -0.5),
        'ffn2_w2': nrm(ks[23], (L, D_FF, D_MODEL), D_FF ** -0.5),
        'final_norm': gain(ks[24], (D_MODEL,)),
    }


def reference(x, c, ctx, c_ctx, w_ada, b_ada, ffn1_norm, ffn1_w1, ffn1_w3, ffn1_w2, mix_norm, w_in,
              gla_gate_w2, gla_gate_b, gla_out_norm, diff_lambda, diff_out_norm, w_branch_gla,
              w_branch_diff, w_out, ffn2_norm, ffn2_w1, ffn2_w3, ffn2_w2, final_norm):
    b_, seq, _ = x.shape
    n_ctx = ctx.shape[1]
    rows = seq // GRID_W
    t = jnp.arange(rows * GRID_W, dtype=jnp.int32)
    t_row, t_col = t // GRID_W, t % GRID_W
    s0 = jnp.zeros((b_, GLA_HEADS, GLA_DK, GLA_DV), jnp.float32)
    xl, xc = x, ctx
    for i in range(DEPTH):
        last = i == DEPTH - 1
        lam_init = 0.8 - 0.6 * math.exp(-0.3 * i)
        ml = jnp.split((jax.nn.silu(c) @ w_ada[i] + b_ada[i])[:, None, :], N_MOD, axis=-1)
        mc = jnp.split((jax.nn.silu(c_ctx) @ w_ada[i] + b_ada[i])[None, None, :], N_MOD, axis=-1)

        xl = xl + 0.5 * ml[2] * _swiglu(_modnorm(xl, ffn1_norm[i], ml[0], ml[1]), ffn1_w1[i], ffn1_w3[i], ffn1_w2[i])
        xc = xc + 0.5 * mc[2] * _swiglu(_modnorm(xc, ffn1_norm[i], mc[0], mc[1]), ffn1_w1[i], ffn1_w3[i], ffn1_w2[i])

        gq_l, gk_l, gv_l, gg_l, lr_l, dq_l, dk_l, dv_l, mg_l = _split_cols(
            _modnorm(xl, mix_norm[i], ml[3], ml[4]) @ w_in[i])
        gq_c, gk_c, gv_c, gg_c, lr_c, dq_c, dk_c, dv_c, mg_c = _split_cols(
            _modnorm(xc, mix_norm[i], mc[3], mc[4]) @ w_in[i])

        gq_c, gk_c, gv_c = _gla_heads(gq_c, gk_c, gv_c, n_ctx)
        gq_l, gk_l, gv_l = _gla_heads(gq_l, gk_l, gv_l, seq)
        oc_f, sc_f = _gla_direction(gq_c, gk_c, gv_c, lr_c[..., :GLA_RANK], gla_gate_w2[i, 0], gla_gate_b[i, 0], s0, False)
        oc_b, sc_b = _gla_direction(gq_c, gk_c, gv_c, lr_c[..., GLA_RANK:], gla_gate_w2[i, 1], gla_gate_b[i, 1], s0, True)
        ol_f, _ = _gla_direction(gq_l, gk_l, gv_l, lr_l[..., :GLA_RANK], gla_gate_w2[i, 0], gla_gate_b[i, 0], sc_f, False)
        ol_b, _ = _gla_direction(gq_l, gk_l, gv_l, lr_l[..., GLA_RANK:], gla_gate_w2[i, 1], gla_gate_b[i, 1], sc_b, True)
        y_gla_l = _gla_head_out(ol_f + ol_b, gg_l, gla_out_norm[i], w_branch_gla[i])

        lam = (jnp.exp(jnp.sum(diff_lambda[i, 0] * diff_lambda[i, 1]))
               - jnp.exp(jnp.sum(diff_lambda[i, 2] * diff_lambda[i, 3])) + lam_init)
        dq_l = _axial_rope(dq_l.reshape(b_, seq, DIFF_HEADS, 2, DIFF_DH), t_row, t_col)
        dk_l = _axial_rope(dk_l.reshape(b_, seq, DIFF_HEADS, 2, DIFF_DH), t_row, t_col)
        dv_l = dv_l.reshape(b_, seq, DIFF_HEADS, 2 * DIFF_DH)
        dk_c = dk_c.reshape(b_, n_ctx, DIFF_HEADS, 2, DIFF_DH)
        dv_c = dv_c.reshape(b_, n_ctx, DIFF_HEADS, 2 * DIFF_DH)
        k_all = jnp.concatenate([dk_c, dk_l], axis=1)
        v_all = jnp.concatenate([dv_c, dv_l], axis=1)
        y_diff_l = _diff_head_out(_diff_latent(dq_l, k_all, v_all, lam), diff_out_norm[i], lam_init, w_branch_diff[i])

        ga_l, gb_l = jnp.split(jax.nn.sigmoid(mg_l), 2, axis=-1)
        xl = xl + ml[5] * ((ga_l * y_gla_l + gb_l * y_diff_l) @ w_out[i])

        if not last:
            y_gla_c = _gla_head_out(oc_f + oc_b, gg_c, gla_out_norm[i], w_branch_gla[i])
            od_c = _diff_block(dq_c.reshape(b_, n_ctx, DIFF_HEADS, 2, DIFF_DH), dk_c, dv_c, lam)
            y_diff_c = _diff_head_out(od_c, diff_out_norm[i], lam_init, w_branch_diff[i])
            ga_c, gb_c = jnp.split(jax.nn.sigmoid(mg_c), 2, axis=-1)
            xc = xc + mc[5] * ((ga_c * y_gla_c + gb_c * y_diff_c) @ w_out[i])
            xc = xc + 0.5 * mc[8] * _swiglu(_modnorm(xc, ffn2_norm[i], mc[6], mc[7]), ffn2_w1[i], ffn2_w3[i], ffn2_w2[i])

        xl = xl + 0.5 * ml[8] * _swiglu(_modnorm(xl, ffn2_norm[i], ml[6], ml[7]), ffn2_w1[i], ffn2_w3[i], ffn2_w2[i])
    return _rmsnorm(xl, final_norm)
```

```python
import numpy as np
import concourse.bass as bass
import concourse.mybir as mybir
from concourse.bass_utils import run_bass_kernel_spmd

F32 = mybir.dt.float32
BF16 = mybir.dt.bfloat16
AF = mybir.ActivationFunctionType
ALU = mybir.AluOpType

D = 1024
NC = 8
FF = 2816
NF = 22
CTX = 256
EPS = 1e-6
GRID_W = 64
IN_COLS = 8224
O_GQ, O_GK, O_GV, O_GG, O_LR, O_DQ, O_DK, O_DV, O_MG = 0, 512, 1024, 2048, 3072, 3104, 4128, 5152, 6176
N_DMA_SEMS = 24
SKIP = set()


class Prog:
    def __init__(self, nc):
        self.nc = nc
        self.ops = []
        self.lastw = {}
        self.readers = {}
        self.ndma = 0
        self.nsw = 0
        self.dma_prev = {}
        self._guards = []
        self.barriers = []
        self.pending_bar = {}
        self.pool_dmas = []

    def barrier(self):
        ops = self.ops
        prior = set()
        seen_e = set()
        seen_s = set()
        for i in range(len(ops) - 1, -1, -1):
            o = ops[i]
            if o[3]:
                if o[4] not in seen_s:
                    seen_s.add(o[4]); prior.add(i)
            elif o[0] not in seen_e:
                seen_e.add(o[0]); prior.add(i)
        prior = sorted(prior)
        last = None
        for j in range(0, max(1, len(prior)), 3):
            idx = len(ops)
            ops.append(["sp", (lambda e: e.nop()), set(prior[j:j + 3]), False, None])
            last = idx
        self.pending_bar = {eng: last for eng in ("pe", "act", "dve", "pool")}

    def _deps(self, reads, writes):
        d = set()
        for r in reads:
            w = self.lastw.get(r)
            if w is not None:
                d.add(w)
        for w_ in writes:
            w = self.lastw.get(w_)
            if w is not None:
                d.add(w)
            for r in self.readers.get(w_, ()):
                d.add(r)
        return d

    def _commit(self, idx, reads, writes):
        for w_ in writes:
            self.lastw[w_] = idx
            self.readers[w_] = []
        for r in reads:
            if r not in writes:
                self.readers.setdefault(r, []).append(idx)

    def op(self, eng, fn, reads=(), writes=()):
        idx = len(self.ops)
        d = self._deps(reads, writes)
        if eng in self.pending_bar:
            d.add(self.pending_bar.pop(eng))
        self.ops.append([eng, fn, d, False, None])
        self._commit(idx, reads, writes)
        return idx

    def dma(self, eng, fn, reads=(), writes=()):
        idx = len(self.ops)
        d = self._deps(reads, writes)
        if eng in self.pending_bar:
            d.add(self.pending_bar.pop(eng))
        if eng == "pool":
            s = "sw%d" % (self.nsw % 8)
            self.nsw += 1
        else:
            s = "hw%d" % (self.ndma % N_DMA_SEMS)
            self.ndma += 1
        if s in self.dma_prev:
            d.add(self.dma_prev[s])
        self.dma_prev[s] = idx
        if eng == "pool":
            if len(self.pool_dmas) >= 12:
                d.add(self.pool_dmas[-12])
            self.pool_dmas.append(idx)
        self.ops.append([eng, fn, d, True, s])
        self._commit(idx, reads, writes)
        return idx

    def emit(self, final_wait_all_dma=True):
        nc = self.nc
        ops = self.ops
        for p in self.barriers:
            prior = set()
            seen_e = set()
            seen_s = set()
            for i in range(p - 1, -1, -1):
                o = ops[i]
                if o[3]:
                    if o[4] not in seen_s:
                        seen_s.add(o[4]); prior.add(i)
                elif o[0] not in seen_e:
                    seen_e.add(o[0]); prior.add(i)
                if len(seen_e) >= 4 and len(seen_s) >= N_DMA_SEMS + 8:
                    break
            done_e = set()
            for i in range(p, len(ops)):
                if ops[i][0] not in done_e:
                    done_e.add(ops[i][0])
                    ops[i][2] |= prior
                if len(done_e) >= 5:
                    break
        need = [False] * len(ops)
        for i, (eng, fn, deps, is_dma, s) in enumerate(ops):
            for dp in deps:
                pe = ops[dp]
                if pe[0] == "pe" and eng == "pe" and not pe[3] and not is_dma:
                    continue
                need[dp] = True
        for i, o in enumerate(ops):
            if o[3]:
                need[i] = True
        sems = {}

        def getsem(name):
            if name not in sems:
                g = nc.semaphore(name)
                sems[name] = g.__enter__()
                self._guards.append(g)
            return sems[name]

        cnt = {}
        tok = [None] * len(ops)
        for i, (eng, fn, deps, is_dma, s) in enumerate(ops):
            if not need[i]:
                continue
            name = ("dma" + s) if is_dma else ("e_" + eng)
            getsem(name)
            cnt[name] = cnt.get(name, 0) + (16 if is_dma else 1)
            tok[i] = (name, cnt[name])
        per = {k: [] for k in ("pe", "act", "dve", "pool", "sp")}
        for i, (eng, fn, deps, is_dma, s) in enumerate(ops):
            waits = []
            for dp in deps:
                pe = ops[dp]
                if pe[0] == "pe" and eng == "pe" and not pe[3] and not is_dma:
                    continue
                waits.append(tok[dp])
            per[eng].append((fn, waits, tok[i], 16 if is_dma else 1))
        final = [(n, c) for n, c in cnt.items() if n.startswith("dma")]
        with nc.Block() as block:
            def run(e, lst, fin=False):
                seen = {}
                for fn, waits, t, amt in lst:
                    for (sn, v) in waits:
                        if seen.get(sn, -1) >= v:
                            continue
                        seen[sn] = v
                        e.wait_ge(sems[sn], v)
                    ins = fn(e)
                    if t is not None:
                        ins.then_inc(sems[t[0]], amt)
                if fin:
                    for j, (sn, v) in enumerate(final):
                        e.wait_ge(sems[sn], v)
                        if j % 3 == 2:
                            e.nop()

            @block.tensor
            def _(e): run(e, per["pe"])

            @block.scalar
            def _(e): run(e, per["act"])

            @block.vector
            def _(e): run(e, per["dve"])

            @block.gpsimd
            def _(e): run(e, per["pool"])

            @block.sync
            def _(e): run(e, per["sp"], fin=True)


def build_nc(SEQ, dbg=False, stop_after=99):
    HALF = SEQ // 2
    TOK = CTX + SEQ
    OWN_END = CTX + HALF
    nc = bass.Bass("TRN2", target_bir_lowering=False)
    P = Prog(nc)

    def din(name, shape, dt=F32):
        return nc.dram_tensor(name, list(shape), dt, kind="ExternalInput").ap()

    def dscr(name, shape, dt):
        return nc.dram_tensor(name, list(shape), dt, kind=("ExternalOutput" if dbg else "Internal")).ap()

    xT = din("xT", [D, TOK])
    cT = din("cT", [128, 8, 2])
    w_ada = din("w_ada", [D, 9 * D])
    bT_ada = din("bT_ada", [128, 72])
    ngT = din("ngT", [128, 3, 8])
    fgT = din("fgT", [128, 8])
    gngT = din("gngT", [128, 8])
    dngT = din("dngT", [128, 8])
    f1w1 = din("f1w1", [D, FF]); f1w3 = din("f1w3", [D, FF]); f1w2 = din("f1w2", [FF, D])
    f2w1 = din("f2w1", [D, FF]); f2w3 = din("f2w3", [D, FF]); f2w2 = din("f2w2", [FF, D])
    w_in = din("w_in", [D, IN_COLS])
    gw2 = din("gw2", [2, 16, 512])
    gbT = din("gbT", [128, 2, 4])
    lamT = din("lamT", [128, 256])
    cosT = din("cosT", [128, SEQ])
    sinT = din("sinT", [128, SEQ])
    permM = din("permM", [128, 128])
    masks = din("masks", [128, 2, 128])
    ident = din("ident", [128, 128])
    wbg_d = din("wbg", [D, D]); wbd_d = din("wbd", [D, D]); wo_d = din("wo", [D, D])
    outT = nc.dram_tensor("outT", [D, HALF], F32, kind="ExternalOutput").ap()

    X1T = dscr("X1T", [D, HALF], F32)
    H2T = dscr("H2T", [D, TOK], BF16)
    GQT = dscr("GQT", [512, HALF], BF16)
    GKT = dscr("GKT", [512, TOK], BF16)
    GV = dscr("GV", [TOK, D], BF16)
    GGT = dscr("GGT", [D, HALF], BF16)
    SPT = dscr("SPT", [2, 512, TOK], F32)
    DQT = dscr("DQT", [D, HALF], BF16)
    DKT = dscr("DKT", [D, TOK], BF16)
    DV = dscr("DV", [TOK, D], BF16)
    MGT = dscr("MGT", [2 * D, HALF], BF16)
    YGT = dscr("YGT", [D, HALF], BF16)
    YDT = dscr("YDT", [D, HALF], BF16)
    X2T = dscr("X2T", [D, HALF], F32)

    def MM(out, lhsT, rhs, start, stop, R, W):
        P.op("pe", lambda e: e.matmul(out, lhsT, rhs, start=start, stop=stop), R, W)

    def TR(out, in_, idn, R, W):
        P.op("pe", lambda e: e.transpose(out, in_, idn), R, W)

    def ACT(out, in_, func, R, W, bias=None, scale=None):
        kw = {}
        if bias is not None:
            kw["bias"] = bias
        if scale is not None:
            kw["scale"] = scale
        P.op("act", lambda e: e.activation(out=out, in_=in_, func=func, **kw), R, W)

    def TS(eng, out, in0, s1, s2, op0, op1, R, W):
        if op1 is None:
            P.op(eng, lambda e: e.tensor_scalar(out=out, in0=in0, scalar1=s1, scalar2=None, op0=op0), R, W)
        else:
            P.op(eng, lambda e: e.tensor_scalar(out=out, in0=in0, scalar1=s1, scalar2=s2, op0=op0, op1=op1), R, W)

    def STT(out, in0, scalar, in1, op0, op1, R, W):
        P.op("dve", lambda e: e.scalar_tensor_tensor(out=out, in0=in0, scalar=scalar, in1=in1, op0=op0, op1=op1), R, W)

    def TT(eng, out, in0, in1, op, R, W):
        P.op(eng, lambda e: e.tensor_tensor(out=out, in0=in0, in1=in1, op=op), R, W)

    def RECIP(out, in_, R, W):
        P.op("dve", lambda e: e.reciprocal(out=out, in_=in_), R, W)

    def CP(eng, out, in_, R, W):
        if eng == "act":
            P.op("act", lambda e: e.activation(out=out, in_=in_, func=AF.Identity), R, W)
        else:
            P.op(eng, lambda e: e.tensor_copy(out=out, in_=in_), R, W)

    def MEMSET(eng, ap, v, W):
        P.op(eng, lambda e: e.memset(ap, v), (), W)

    def DMA(q, out, in_, R, W):
        P.dma(q, lambda e: e.dma_start(out=out, in_=in_), R, W)

    from contextlib import ExitStack
    es = ExitStack()

    def sb(name, shape, dt):
        return es.enter_context(nc.sbuf_tensor(name, list(shape), dt))

    def pst(name, shape, dt=F32):
        return es.enter_context(nc.psum_tensor(name, list(shape), dt))

    cons = ExitStack()

    def csb(name, shape, dt):
        return cons.enter_context(nc.sbuf_tensor(name, list(shape), dt))

    modt = csb("modt", [128, 72, 2], F32)
    gm = csb("gm", [128, 3, 8, 2], F32)
    shf = csb("shf", [128, 3, 8, 2], F32)
    gte = csb("gte", [128, 3, 8, 2], F32)
    ng = csb("ng", [128, 3, 8], F32)
    fg = csb("fg", [128, 8], F32)
    gng = csb("gng", [128, 8], F32)
    dng = csb("dng", [128, 8], F32)
    ones_f = csb("ones_f", [128, 512], F32)
    ones_b = csb("ones_b", [128, 128], BF16)
    id_b = csb("id_b", [128, 128], BF16)
    perm_b = csb("perm_b", [128, 128], BF16)
    mask_f = csb("mask_f", [128, 2, 128], F32)
    gw2s = csb("gw2s", [16, 2, 512], F32)
    gb = csb("gb", [128, 2, 4], F32)
    ngb = csb("ngb", [128, 2, 4], F32)
    lam_s = csb("lam_s", [128, 256], F32)
    lam_w = csb("lam_w", [128, 8], F32)
    eps_t = csb("eps_t", [128, 1], F32)

    DMA("sp", ng[:], ngT, (), ["ng"])
    DMA("sp", fg[:], fgT, (), ["fg"])
    DMA("sp", gng[:], gngT, (), ["gng"])
    DMA("sp", dng[:], dngT, (), ["dng"])
    DMA("sp", mask_f[:], masks, (), ["mask_f"])
    DMA("sp", gb[:], gbT, (), ["gb"])
    DMA("sp", lam_s[:], lamT, (), ["lam_s"])
    for d_ in range(2):
        DMA("sp", gw2s[:, d_, :], gw2[d_], (), [("gw2s", d_)])
    DMA("pool", id_b[:], ident, (), ["id_b"])
    DMA("pool", perm_b[:], permM, (), ["perm_b"])
    MEMSET("dve", ones_f[:], 1.0, ["ones_f"])
    MEMSET("dve", ones_b[:], 1.0, ["ones_b"])
    MEMSET("dve", eps_t[:], EPS, ["eps_t"])
    TS("dve", ngb[:], gb[:], -1.0, None, ALU.mult, None, ["gb"], ["ngb"])
    TT("dve", lam_s[:, 0:64], lam_s[:, 0:64], lam_s[:, 64:128], ALU.mult, ["lam_s"], ["lam_s"])
    TT("dve", lam_s[:, 128:192], lam_s[:, 128:192], lam_s[:, 192:256], ALU.mult, ["lam_s"], ["lam_s"])
    P.op("dve", lambda e: e.reduce_sum(out=lam_w[:, 0:1], in_=lam_s[:, 0:64], axis=mybir.AxisListType.X), ["lam_s"], ["lam_w"])
    P.op("dve", lambda e: e.reduce_sum(out=lam_w[:, 1:2], in_=lam_s[:, 128:192], axis=mybir.AxisListType.X), ["lam_s"], ["lam_w"])
    ACT(lam_w[:, 2:4], lam_w[:, 0:2], AF.Exp, ["lam_w"], ["lam_w"])
    TT("dve", lam_w[:, 4:5], lam_w[:, 3:4], lam_w[:, 2:3], ALU.subtract, ["lam_w"], ["lam_w"])
    TS("dve", lam_w[:, 4:5], lam_w[:, 4:5], -0.2, None, ALU.add, None, ["lam_w"], ["lam_w"])
    neglam = lam_w[:, 4:5]

    es = ExitStack()
    sc = sb("sc", [128, 8, 2], F32)
    bTa = sb("bTa", [128, 72], F32)
    wa = [sb("wa%d" % i, [128, 8, 1024], F32) for i in range(2)]
    ps_m = [pst("ps_m%d" % i, [128, 16]) for i in range(2)]
    DMA("sp", sc[:], cT, (), ["sc"])
    DMA("sp", bTa[:], bT_ada, (), ["bTa"])
    ACT(sc[:], sc[:], AF.Silu, ["sc"], ["sc"])
    w_ada_v = w_ada.rearrange("(kc p) n -> p kc n", p=128)
    for m in range(9):
        wb_ = wa[m % 2]
        for kc in range(8):
            DMA("sp", wb_[:, kc, :], w_ada_v[:, kc, m * 1024:(m + 1) * 1024], (), [("wa", m % 2, kc)])
        pm = ps_m[m % 2]
        for c in range(8):
            for kc in range(8):
                MM(pm[:, 2 * c:2 * c + 2], wb_[:, kc, c * 128:(c + 1) * 128], sc[:, kc, :], kc == 0, kc == 7,
                   [("wa", m % 2, kc), "sc"], [("ps_m", m % 2)])
        for j in range(2):
            TT("dve", modt[:, m * 8:(m + 1) * 8, j], pm[:, j:16:2], bTa[:, m * 8:(m + 1) * 8], ALU.add,
               [("ps_m", m % 2), "bTa"] + [("wa", m % 2, kc) for kc in range(8)], ["modt"])
    for i in range(3):
        for j in range(2):
            STT(gm[:, i, :, j], modt[:, (3 * i + 1) * 8:(3 * i + 2) * 8, j], 1.0, ng[:, i, :], ALU.add, ALU.mult,
                ["modt", "ng"], ["gm"])
            CP("dve", shf[:, i, :, j], modt[:, (3 * i) * 8:(3 * i + 1) * 8, j], ["modt"], ["shf"])
            TS("dve", gte[:, i, :, j], modt[:, (3 * i + 2) * 8:(3 * i + 3) * 8, j], (1.0 if i == 1 else 0.5), None,
               ALU.mult, None, ["modt"], ["gte"])
    es.close()

    def load_w_bf16(dst, src_v, nk, ncols, resname):
        step = 1024
        for kc in range(nk):
            for c0 in range(0, ncols, step):
                c1 = min(ncols, c0 + step)
                DMA("pool", dst[:, kc, c0:c1], src_v[:, kc, c0:c1], (), [(resname, kc)])

    def rms_rstd(key, rres, src_chunks, nch, W_, sq, ps_stat, rstd, inv_n, Rsrc):
        for c in range(nch):
            s_ = sq[c % 2]
            ACT(s_[:, :W_], src_chunks(c), AF.Square, Rsrc(c), [(key + "sq", c % 2)])
            MM(ps_stat[:, :W_], ones_f[:, 0:128], s_[:, :W_], c == 0, c == nch - 1,
               [(key + "sq", c % 2), "ones_f"], [key + "ps_stat"])
        ACT(rstd[:, :W_], ps_stat[:, :W_], AF.Sqrt, [key + "ps_stat", "eps_t"], [rres], bias=eps_t[:, 0:1], scale=inv_n)
        RECIP(rstd[:, :W_], rstd[:, :W_], [rres], [rres])

    def ffn_phase(tag, w1d, w3d, w2d, tiles, TW, src_T, li, post, src_deps):
        nonlocal es
        es = ExitStack()
        w1b = sb(tag + "w1b", [128, 8, FF], BF16)
        w3b = sb(tag + "w3b", [128, 8, FF], BF16)
        w2b = sb(tag + "w2b", [128, NF, D], BF16)
        load_w_bf16(w1b, w1d.rearrange("(kc p) n -> p kc n", p=128), 8, FF, tag + "w1b")
        load_w_bf16(w3b, w3d.rearrange("(kc p) n -> p kc n", p=128), 8, FF, tag + "w3b")
        load_w_bf16(w2b, w2d.rearrange("(kc p) n -> p kc n", p=128), NF, D, tag + "w2b")
        xt = [sb(tag + "xt%d" % i, [128, 8, TW], F32) for i in range(2)]
        hb = sb(tag + "hb", [128, 8, TW], BF16)
        ab = sb(tag + "ab", [128, NF, TW], BF16)
        sq = [sb(tag + "sq%d" % i, [128, TW], F32) for i in range(2)]
        rstd = sb(tag + "rstd", [128, TW], F32)
        rstd2 = sb(tag + "rstd2", [128, TW], F32)
        tmp = [sb(tag + "tmp%d" % i, [128, TW], F32) for i in range(2)]
        su = [sb(tag + "su%d" % i, [128, TW], F32) for i in range(2)]
        h2 = sb(tag + "h2", [128, 8, TW], BF16) if li == 0 else sb(tag + "h2", [128, 4, TW], F32)
        ps_u = [pst(tag + "ps_u%d" % i, [128, TW]) for i in range(2)]
        ps_v = [pst(tag + "ps_v%d" % i, [128, TW]) for i in range(2)]
        ps_y = [pst(tag + "ps_y%d" % i, [128, TW]) for i in range(2)]
        ps_stat = pst(tag + "ps_stat", [128, TW])
        src_v = src_T.rearrange("(c p) n -> p c n", p=128)

        def prologue(t):
            c0, W_, j, own = tiles[t]
            x_ = xt[t % 2]
            DMA("sp", x_[:, :, :W_], src_v[:, :, c0:c0 + W_], src_deps(t), [(tag + "xt", t % 2, c) for c in range(8)])
            rms_rstd(tag, tag + "arstd", lambda c: x_[:, c, :W_], 8, W_, sq, ps_stat, rstd, 1.0 / D,
                     lambda c: [(tag + "xt", t % 2, c)])
            for c in range(8):
                tm = tmp[c % 2]
                STT(tm[:, :W_], x_[:, c, :W_], gm[:, li, c, j:j + 1], rstd[:, :W_], ALU.mult, ALU.mult,
                    [(tag + "xt", t % 2, c), "gm", tag + "arstd"], [(tag + "tmp", c % 2)])
                ACT(hb[:, c, :W_], tm[:, :W_], AF.Identity, [(tag + "tmp", c % 2), "shf"], [(tag + "hb", c)],
                    bias=shf[:, li, c, j:j + 1])

        prologue(0)
        for t in range(len(tiles)):
            c0, W_, j, own = tiles[t]
            x_ = xt[t % 2]
            for f in range(NF):
                pu = ps_u[f % 2]; pv = ps_v[f % 2]
                for kc in range(8):
                    MM(pu[:, :W_], w1b[:, kc, f * 128:(f + 1) * 128], hb[:, kc, :W_], kc == 0, kc == 7,
                       [(tag + "w1b", kc), (tag + "hb", kc)], [(tag + "ps_u", f % 2)])
                for kc in range(8):
                    MM(pv[:, :W_], w3b[:, kc, f * 128:(f + 1) * 128], hb[:, kc, :W_], kc == 0, kc == 7,
                       [(tag + "w3b", kc), (tag + "hb", kc)], [(tag + "ps_v", f % 2)])
                s_ = su[f % 2]
                ACT(s_[:, :W_], pu[:, :W_], AF.Silu, [(tag + "ps_u", f % 2)], [(tag + "su", f % 2)])
                TT("dve", ab[:, f, :W_], s_[:, :W_], pv[:, :W_], ALU.mult,
                   [(tag + "su", f % 2), (tag + "ps_v", f % 2)], [(tag + "ab", f)])
            if t + 1 < len(tiles):
                prologue(t + 1)
            for c in range(8):
                py = ps_y[c % 2]
                for f in range(NF):
                    MM(py[:, :W_], w2b[:, f, c * 128:(c + 1) * 128], ab[:, f, :W_], f == 0, f == NF - 1,
                       [(tag + "w2b", f), (tag + "ab", f)], [(tag + "ps_y", c % 2)])
                STT(x_[:, c, :W_], py[:, :W_], gte[:, li, c, j:j + 1], x_[:, c, :W_], ALU.mult, ALU.add,
                    [(tag + "ps_y", c % 2), "gte", (tag + "xt", t % 2, c)], [(tag + "xt", t % 2, c)])
            post(t, x_, W_, sq, ps_stat, rstd2, tmp, h2, tag)
        es.close()

    def finish():
        P.emit()
        cons.close()
        return nc
    if stop_after < 1:
        return finish()
    P.barrier()
    TW1 = 256
    tiles1 = [(0, CTX, 1, None)]
    for i in range(SEQ // TW1):
        tiles1.append((CTX + i * TW1, TW1, 0, (i * TW1 if i * TW1 < HALF else None)))
    H2T_v = H2T.rearrange("(c p) n -> p c n", p=128)
    X1T_v = X1T.rearrange("(c p) n -> p c n", p=128)

    def post1(t, x_, W_, sq, ps_stat, rstd2, tmp, h2, tag):
        c0, _, j, own = tiles1[t]
        if own is not None:
            DMA("sp", X1T_v[:, :, own:own + W_], x_[:, :, :W_], [(tag + "xt", t % 2, c) for c in range(8)], [("X1T", own)])
        rms_rstd(tag, tag + "brstd", lambda c: x_[:, c, :W_], 8, W_, sq, ps_stat, rstd2, 1.0 / D,
                 lambda c: [(tag + "xt", t % 2, c)])
        for c in range(8):
            tm = tmp[c % 2]
            STT(tm[:, :W_], x_[:, c, :W_], gm[:, 1, c, j:j + 1], rstd2[:, :W_], ALU.mult, ALU.mult,
                [(tag + "xt", t % 2, c), "gm", tag + "brstd"], [(tag + "tmp", c % 2)])
            ACT(h2[:, c, :W_], tm[:, :W_], AF.Identity, [(tag + "tmp", c % 2), "shf"], [(tag + "h2", c)],
                bias=shf[:, 1, c, j:j + 1])
        DMA("sp", H2T_v[:, :, c0:c0 + W_], h2[:, :, :W_], [(tag + "h2", c) for c in range(8)], [("H2T", c0)])

    if "p1" not in SKIP:
        ffn_phase("f1", f1w1, f1w3, f1w2, tiles1, TW1, xT, 0, post1, lambda t: ())

    if stop_after < 2:
        return finish()
    P.barrier()
    TW2 = 256
    tiles2 = [(0, CTX, "ctx", None)]
    for i in range(SEQ // TW2):
        tiles2.append((CTX + i * TW2, TW2, ("own" if i * TW2 < HALF else "oth"), i * TW2))
    GKT_v = GKT.rearrange("(c p) n -> p c n", p=128)
    GQT_v = GQT.rearrange("(c p) n -> p c n", p=128)
    GGT_v = GGT.rearrange("(c p) n -> p c n", p=128)
    DQT_v = DQT.rearrange("(c p) n -> p c n", p=128)
    DKT_v = DKT.rearrange("(c p) n -> p c n", p=128)
    MGT_v = MGT.rearrange("(c p) n -> p c n", p=128)
    SPT_v = SPT.rearrange("d (c p) n -> d p c n", p=128)
    w_in_v = w_in.rearrange("(kc p) n -> p kc n", p=128)

    def proj_pass(pid):
        nonlocal es
        es = ExitStack()
        if pid == 0:
            segs = [(O_GK, 512), (O_GV, 1024), (O_LR, 32), (O_DK, 1024), (O_DV, 1024)]
        else:
            segs = [(O_GQ, 512), (O_GG, 1024), (O_DQ, 1024), (O_MG, 2048)]
        cmap = {}
        tot = 0
        for (s0, n) in segs:
            cmap[s0] = tot
            tot += n
        wtag = "winb%d" % pid
        winb = sb(wtag, [128, 8, tot], BF16)
        for (s0, n) in segs:
            if n == 32 and 'lr' in SKIP:
                continue
            for kc in range(8):
                for c0 in range(0, n, 1024):
                    c1 = min(n, c0 + 1024)
                    DMA("pool", winb[:, kc, cmap[s0] + c0:cmap[s0] + c1], w_in_v[:, kc, s0 + c0:s0 + c1], (), [(wtag, kc)])
        pt = "p2%d" % pid
        h2t = [sb(pt + "h2t%d" % i, [128, 8, TW2], BF16) for i in range(2)]
        ps_p = [pst(pt + "ps_p%d" % i, [128, 512]) for i in range(4)]
        pcount = [0]

        def proj_fm(h_, hi, col0, W_):
            k = pcount[0] % 4
            pcount[0] += 1
            pp = ps_p[k]
            for kc in range(8):
                MM(pp[:, :W_], winb[:, kc, col0:col0 + 128], h_[:, kc, :W_], kc == 0, kc == 7,
                   [(wtag, kc), (pt + "h2t", hi)], [(pt + "ps_p", k)])
            return pp, (pt + "ps_p", k)

        stg = {}
        names = (("gk", 4), ("dk", 8)) if pid == 0 else (("gq", 4), ("gg", 8), ("dq", 8), ("mg", 16))
        for nm, nchk in names:
            stg[nm] = sb("stg_" + nm, [128, nchk, TW2], BF16)
        qb = [sb(pt + "qb%d" % i, [128, TW2], BF16) for i in range(2)]
        r1 = [sb(pt + "r1_%d" % i, [128, TW2], F32) for i in range(2)]
        r2 = [sb(pt + "r2_%d" % i, [128, TW2], F32) for i in range(2)]
        cs = [sb(pt + "cs%d" % i, [128, 2, TW2], F32) for i in range(2)]
        ps_r = [pst(pt + "ps_r%d" % i, [128, TW2]) for i in range(2)]
        if pid == 0:
            stv = [sb("stv%d" % i, [128, TW2 // 128, 512], BF16) for i in range(2)]
            lrs = [sb("lrs%d" % i, [16, TW2], F32) for i in range(2)]
            spb = [sb("spb%d" % i, [128, 4, TW2], F32) for i in range(2)]
            ex = [sb("ex%d" % i, [128, TW2], F32) for i in range(2)]
            ps_l = [pst("ps_l%d" % i, [128, TW2]) for i in range(2)]
        ropec = [0]
        tcount = 0
        for (c0, W_, kind, l0) in tiles2:
            if pid == 1 and kind != "own":
                continue
            hi = tcount % 2
            tcount += 1
            h_ = h2t[hi]
            DMA("sp", h_[:, :, :W_], H2T_v[:, :, c0:c0 + W_], [("H2T", c0)], [(pt + "h2t", hi)])
            latent = kind != "ctx"
            if latent:
                DMA("sp", cs[hi][:, 0, :W_], cosT[:, l0:l0 + W_], (), [(pt + "cs", hi, 0)])
                DMA("sp", cs[hi][:, 1, :W_], sinT[:, l0:l0 + W_], (), [(pt + "cs", hi, 1)])

            def rope_evac(pp, pres, dst, dres):
                i = ropec[0] % 2
                ropec[0] += 1
                if not latent or 'norope' in SKIP:
                    CP("act", dst, pp[:, :W_], [pres], [dres])
                    return
                if 'rope_noperm' in SKIP:
                    if 'nocs' in SKIP:
                        TT("dve", r1[i][:, :W_], pp[:, :W_], ones_f[:, :W_], ALU.mult, [pres, "ones_f"], [(pt + "r1", i)])
                    elif 'cs_nodep' in SKIP:
                        TT("dve", r1[i][:, :W_], pp[:, :W_], cs[hi][:, 0, :W_], ALU.mult, [pres], [(pt + "r1", i)])
                    else:
                        TT("dve", r1[i][:, :W_], pp[:, :W_], cs[hi][:, 0, :W_], ALU.mult, [pres, (pt + "cs", hi, 0)], [(pt + "r1", i)])
                    CP("dve", dst, r1[i][:, :W_], [(pt + "r1", i)], [dres])
                    return
                CP("act", qb[i][:, :W_], pp[:, :W_], [pres], [(pt + "qb", i)])
                MM(ps_r[i][:, :W_], perm_b[:], qb[i][:, :W_], True, True, ["perm_b", (pt + "qb", i)], [(pt + "ps_r", i)])
                if 'rope_nodve' in SKIP:
                    CP("act", dst, ps_r[i][:, :W_], [(pt + "ps_r", i)], [dres])
                    return
                if 'r1pp' in SKIP:
                    TT("dve", r1[i][:, :W_], pp[:, :W_], cs[hi][:, 0, :W_], ALU.mult, [pres, (pt + "cs", hi, 0)], [(pt + "r1", i)])
                else:
                    TT("dve", r1[i][:, :W_], qb[i][:, :W_], cs[hi][:, 0, :W_], ALU.mult, [(pt + "qb", i), (pt + "cs", hi, 0)], [(pt + "r1", i)])
                TT("dve", r2[i][:, :W_], ps_r[i][:, :W_], cs[hi][:, 1, :W_], ALU.mult, [(pt + "ps_r", i), (pt + "cs", hi, 1)], [(pt + "r2", i)])
                TT(("pool" if "usepool" in SKIP else "dve"), dst, r1[i][:, :W_], r2[i][:, :W_], ALU.add, [(pt + "r1", i), (pt + "r2", i)], [dres])

            if pid == 0:
                for c in range(4):
                    pp, pres = proj_fm(h_, hi, cmap[O_GK] + c * 128, W_)
                    CP("act", stg["gk"][:, c, :W_], pp[:, :W_], [pres], [("stg_gk", c)])
                DMA("sp", GKT_v[:, :, c0:c0 + W_], stg["gk"][:, :, :W_], [("stg_gk", c) for c in range(4)], [("GKT", c0)])
                for c in ([] if 'dk' in SKIP else range(8)):
                    pp, pres = proj_fm(h_, hi, cmap[O_DK] + c * 128, W_)
                    rope_evac(pp, pres, stg["dk"][:, c, :W_], ("stg_dk", c))
                if "dk" not in SKIP:
                    DMA("sp", DKT_v[:, :, c0:c0 + W_], stg["dk"][:, :, :W_], [("stg_dk", c) for c in range(8)], [("DKT", c0)])
                for (ocol, dst, dname) in ([] if "v" in SKIP else ((cmap[O_GV], GV, "GV"), (cmap[O_DV], DV, "DV"))):
                    for cb in range(2):
                        k = pcount[0] % 2
                        sv = stv[k]
                        for tt_ in range(W_ // 128):
                            kk = pcount[0] % 4
                            pcount[0] += 1
                            pp = ps_p[kk]
                            for kc in range(8):
                                MM(pp[:, :512], h_[:, kc, tt_ * 128:(tt_ + 1) * 128], winb[:, kc, ocol + cb * 512:ocol + (cb + 1) * 512],
                                   kc == 0, kc == 7, [(wtag, kc), (pt + "h2t", hi)], [(pt + "ps_p", kk)])
                            CP(("act" if tt_ % 2 == 0 else "dve"), sv[:, tt_, :], pp[:, :512], [(pt + "ps_p", kk)], [("stv", k, tt_)])
                        dv_ = dst[c0:c0 + W_, cb * 512:(cb + 1) * 512].rearrange("(n p) e -> p n e", p=128)
                        DMA("sp", dv_, sv[:, :W_ // 128, :], [("stv", k, tt_) for tt_ in range(W_ // 128)], [(dname, c0, cb)])
                for d_ in ([] if 'lr' in SKIP else range(2)):
                    i = d_
                    pl = ps_l[i]
                    lc = cmap[O_LR] + 16 * d_
                    for kc in range(8):
                        MM(pl[0:16, :W_], winb[:, kc, lc:lc + 16], h_[:, kc, :W_], kc == 0, kc == 7,
                           [(wtag, kc), (pt + "h2t", hi)], [("ps_l", i)])
                    CP("dve", lrs[i][:, :W_], pl[0:16, :W_], [("ps_l", i)], [("lrs", i)])
                    for c in range(4):
                        kk = pcount[0] % 4
                        pcount[0] += 1
                        pp = ps_p[kk]
                        MM(pp[:, :W_], gw2s[:, d_, c * 128:(c + 1) * 128], lrs[i][:, :W_], True, True,
                           [("gw2s", d_), ("lrs", i)], [(pt + "ps_p", kk)])
                        e_ = ex[c % 2]
                        ACT(e_[:, :W_], pp[:, :W_], AF.Exp, [(pt + "ps_p", kk), "ngb"], [("ex", c % 2)], bias=ngb[:, d_, c:c + 1], scale=-1.0)
                        ACT(spb[i][:, c, :W_], e_[:, :W_], AF.Ln, [("ex", c % 2)], [("spb", i, c)], bias=1.0)
                    DMA("sp", SPT_v[d_][:, :, c0:c0 + W_], spb[i][:, :, :W_], [("spb", i, c) for c in range(4)], [("SPT", d_, c0)])
            else:
                for c in range(4):
                    pp, pres = proj_fm(h_, hi, cmap[O_GQ] + c * 128, W_)
                    ACT(stg["gq"][:, c, :W_], pp[:, :W_], AF.Identity, [pres], [("stg_gq", c)], scale=128.0 ** -0.5)
                DMA("sp", GQT_v[:, :, l0:l0 + W_], stg["gq"][:, :, :W_], [("stg_gq", c) for c in range(4)], [("GQT", l0)])
                for c in range(8):
                    pp, pres = proj_fm(h_, hi, cmap[O_GG] + c * 128, W_)
                    ACT(stg["gg"][:, c, :W_], pp[:, :W_], AF.Silu, [pres], [("stg_gg", c)])
                DMA("sp", GGT_v[:, :, l0:l0 + W_], stg["gg"][:, :, :W_], [("stg_gg", c) for c in range(8)], [("GGT", l0)])
                for c in range(8):
                    pp, pres = proj_fm(h_, hi, cmap[O_DQ] + c * 128, W_)
                    rope_evac(pp, pres, stg["dq"][:, c, :W_], ("stg_dq", c))
                DMA("sp", DQT_v[:, :, l0:l0 + W_], stg["dq"][:, :, :W_], [("stg_dq", c) for c in range(8)], [("DQT", l0)])
                for c in range(16):
                    pp, pres = proj_fm(h_, hi, cmap[O_MG] + c * 128, W_)
                    ACT(stg["mg"][:, c, :W_], pp[:, :W_], AF.Sigmoid, [pres], [("stg_mg", c)])
                DMA("sp", MGT_v[:, :, l0:l0 + W_], stg["mg"][:, :, :W_], [("stg_mg", c) for c in range(16)], [("MGT", l0)])
        es.close()

    proj_pass(0)
    P.barrier()
    if 'pass1' not in SKIP:
        proj_pass(1)

    def dram_deps(name, a, b, step):
        return [(name, cc) for cc in range(a, b, step)]

    if stop_after < 3:
        return finish()
    P.barrier()
    es = ExitStack()
    NCK = TOK // 128
    NOWN = HALF // 128
    qT = sb("g_qT", [128, HALF], BF16)
    kT = sb("g_kT", [128, TOK], BF16)
    vt = sb("g_v", [128, NCK, 256], BF16)
    Pst = [sb("g_P0", [128, CTX + HALF + 1], F32), sb("g_P1", [128, TOK + 1], F32)]
    spl = [sb("g_sp%d" % i, [128, 512], F32) for i in range(2)]
    oT = sb("g_oT", [128, 2, HALF], F32)
    S = sb("g_S", [128, 256], F32)
    Stmp = sb("g_Stmp", [128, 256], F32)
    Sb = sb("g_Sb", [128, 256], BF16)
    E1 = [sb("g_E1_%d" % i, [128, 128], F32) for i in range(2)]
    E2 = [sb("g_E2_%d" % i, [128, 128], F32) for i in range(2)]
    qe = [sb("g_qe%d" % i, [128, 128], BF16) for i in range(2)]
    keT = [sb("g_keT%d" % i, [128, 128], BF16) for i in range(2)]
    ket = [sb("g_ket%d" % i, [128, 128], BF16) for i in range(2)]
    atb = [sb("g_atb%d" % i, [128, 128], BF16) for i in range(2)]
    bcol = [sb("g_bcol%d" % i, [128, 4], F32) for i in range(4)]
    ggt = [sb("g_ggt%d" % i, [128, 2, 512], BF16) for i in range(1)] * 2
    ygs = [sb("g_ygs%d" % i, [128, 2, 512], BF16) for i in range(1)] * 2
    gsq = [sb("g_sq%d" % i, [128, 512], F32) for i in range(2)]
    grs = sb("g_rs", [128, 512], F32)
    gtm = [sb("g_tm%d" % i, [128, 512], F32) for i in range(2)]
    ps_at = [pst("ps_at%d" % i, [128, 512])[:, 0:128] for i in range(2)]
    ps_tr = [pst("ps_tr%d" % i, [128, 1024], BF16)[:, 0:128] for i in range(2)]
    ps_o = [pst("ps_o%d" % i, [128, 4, 128])[:, 0:2, :] for i in range(2)]
    ps_ds = pst("ps_ds", [128, 512])[:, 0:256]
    ps_gs = pst("ps_gs", [128, 512])
    GV_v = GV.rearrange("(n p) e -> p n e", p=128)
    YGT_v = YGT.rearrange("(c p) n -> p c n", p=128)
    cc_ = [0]
    for h in range(4):
        DMA("sp", qT[:], GQT[h * 128:(h + 1) * 128, :], dram_deps("GQT", 0, HALF, TW2), ["g_qT"])
        DMA("sp", kT[:], GKT[h * 128:(h + 1) * 128, :], dram_deps("GKT", 0, CTX, CTX) + dram_deps("GKT", CTX, TOK, TW2), ["g_kT"])
        vdeps = [("GV", 0, cb) for cb in range(2)] + [("GV", c0, cb) for c0 in range(CTX, TOK, TW2) for cb in range(2)]
        for n0 in range(0, NCK, 16):
            n1 = min(NCK, n0 + 16)
            DMA("sp", vt[:, n0:n1, :], GV_v[:, n0:n1, h * 256:(h + 1) * 256], vdeps, [("g_v", n0)])
        vres = [("g_v", n0) for n0 in range(0, NCK, 16)]
        for d_ in range(2):
            MEMSET("dve", Pst[d_][:, 0:1], 0.0, [("g_P", d_)])
            NT_ = (CTX + HALF) if d_ == 0 else TOK
            for n0 in range(0, NT_, 512):
                n1 = min(NT_, n0 + 512)
                s_ = spl[(n0 // 512) % 2]
                spdeps = [("SPT", d_, c0) for c0 in ([0] + list(range(CTX, TOK, TW2)))]
                DMA("sp", s_[:, :n1 - n0], SPT[d_, h * 128:(h + 1) * 128, n0:n1], spdeps, [("g_sp", (n0 // 512) % 2)])
                P.op("dve", (lambda d_=d_, n0=n0, n1=n1, s_=s_: (lambda e: e.tensor_tensor_scan(
                    out=Pst[d_][:, n0 + 1:n1 + 1], data0=ones_f[:, :n1 - n0], data1=s_[:, :n1 - n0],
                    initial=Pst[d_][:, n0:n0 + 1], op0=ALU.mult, op1=ALU.add)))(),
                    [("g_sp", (n0 // 512) % 2), "ones_f", ("g_P", d_)], [("g_P", d_)])

        def chunk(d_, ck, full, first_out):
            i = cc_[0] % 2
            cc_[0] += 1
            a = ck * 128
            Pd = Pst[d_]
            bc = bcol[cc_[0] % 4]
            bres = ("g_bcol", cc_[0] % 4)
            if d_ == 0:
                sgn = -1.0 / 16.0
                pcols = Pd[:, a + 1:a + 129]
                bsrc = Pd[:, a:a + 1]
                elcol = 127
            else:
                sgn = 1.0 / 16.0
                pcols = Pd[:, a:a + 128]
                bsrc = Pd[:, a + 128:a + 129]
                elcol = 0
            TS("dve", bc[:, 0:1], bsrc, -sgn, None, ALU.mult, None, [("g_P", d_)], [bres])
            TS("dve", bc[:, 1:2], bsrc, sgn, None, ALU.mult, None, [("g_P", d_)], [bres])
            ACT(E1[i][:], pcols, AF.Exp, [("g_P", d_), bres], [("g_E1", i)], bias=bc[:, 0:1], scale=sgn)
            ACT(E2[i][:], pcols, AF.Exp, [("g_P", d_), bres], [("g_E2", i)], bias=bc[:, 1:2], scale=-sgn)
            TT("dve", keT[i][:], kT[:, a:a + 128], E2[i][:], ALU.mult, ["g_kT", ("g_E2", i)], [("g_keT", i)])
            TR(ps_tr[i][:], keT[i][:], id_b[:], [("g_keT", i), "id_b"], [("ps_tr", i)])
            CP("act", ket[i][:], ps_tr[i][:], [("ps_tr", i)], [("g_ket", i)])
            if full:
                lo = a - CTX
                TT("dve", qe[i][:], qT[:, lo:lo + 128], E1[i][:], ALU.mult, ["g_qT", ("g_E1", i)], [("g_qe", i)])
                MM(ps_at[i][:], keT[i][:], qe[i][:], True, True, [("g_keT", i), ("g_qe", i)], [("ps_at", i)])
                TT("dve", atb[i][:], ps_at[i][:], mask_f[:, d_, :], ALU.mult, [("ps_at", i), "mask_f"], [("g_atb", i)])
                for ec in range(2):
                    MM(ps_o[i][:, ec, :], Sb[:, ec * 128:(ec + 1) * 128], qe[i][:], True, False,
                       ["g_Sb", ("g_qe", i)], [("ps_o", i)])
                    MM(ps_o[i][:, ec, :], vt[:, ck, ec * 128:(ec + 1) * 128], atb[i][:], False, True,
                       vres + [("g_atb", i)], [("ps_o", i)])
                if first_out:
                    CP("act", oT[:, :, lo:lo + 128], ps_o[i][:], [("ps_o", i)], [("g_oT", lo)])
                else:
                    TT("dve", oT[:, :, lo:lo + 128], oT[:, :, lo:lo + 128], ps_o[i][:], ALU.add,
                       [("ps_o", i), ("g_oT", lo)], [("g_oT", lo)])
            MM(ps_ds[:], ket[i][:], vt[:, ck, :], True, True, [("g_ket", i)] + vres, ["ps_ds"])
            TT("dve", Stmp[:], S[:], ps_ds[:], ALU.add, ["g_S", "ps_ds"], ["g_Stmp"])
            TS("dve", S[:], Stmp[:], E1[i][:, elcol:elcol + 1], None, ALU.mult, None, ["g_Stmp", ("g_E1", i)], ["g_S"])
            P.op("act", (lambda i=i, elcol=elcol: (lambda e: e.activation(out=Sb[:], in_=Stmp[:], func=AF.Copy,
                                                                         scale=E1[i][:, elcol:elcol + 1])))(),
                 ["g_Stmp", ("g_E1", i)], ["g_Sb"])

        MEMSET("dve", S[:], 0.0, ["g_S"])
        MEMSET("dve", Sb[:], 0.0, ["g_Sb"])
        for ck in range(CTX // 128):
            chunk(0, ck, False, False)
        for ck in range(CTX // 128, CTX // 128 + NOWN):
            chunk(0, ck, True, True)
        MEMSET("dve", S[:], 0.0, ["g_S"])
        MEMSET("dve", Sb[:], 0.0, ["g_Sb"])
        for ck in range(CTX // 128 - 1, -1, -1):
            chunk(1, ck, False, False)
        for ck in range(NCK - 1, CTX // 128 + NOWN - 1, -1):
            chunk(1, ck, False, False)
        for ck in range(CTX // 128 + NOWN - 1, CTX // 128 - 1, -1):
            chunk(1, ck, True, False)
        for q0 in range(0, HALF, 512):
            k = 0
            DMA("sp", ggt[k][:], GGT_v[:, 2 * h:2 * h + 2, q0:q0 + 512], dram_deps("GGT", q0, q0 + 512, TW2), [("g_ggt", k)])
            ores = [("g_oT", lo) for lo in range(q0, q0 + 512, 128)]
            for ec in range(2):
                ACT(gsq[ec][:], oT[:, ec, q0:q0 + 512], AF.Square, ores, [("g_sq", ec)])
                MM(ps_gs[:], ones_f[:, 0:128], gsq[ec][:], ec == 0, ec == 1, [("g_sq", ec), "ones_f"], ["ps_gs"])
            ACT(grs[:], ps_gs[:], AF.Sqrt, ["ps_gs", "eps_t"], ["g_rs"], bias=eps_t[:, 0:1], scale=1.0 / 256)
            RECIP(grs[:], grs[:], ["g_rs"], ["g_rs"])
            for ec in range(2):
                STT(gtm[ec][:], oT[:, ec, q0:q0 + 512], gng[:, 2 * h + ec:2 * h + ec + 1], grs[:], ALU.mult, ALU.mult,
                    ores + ["gng", "g_rs"], [("g_tm", ec)])
                TT("pool", ygs[k][:, ec, :], gtm[ec][:], ggt[k][:, ec, :], ALU.mult, [("g_tm", ec), ("g_ggt", k)], [("g_ygs", k, ec)])
            DMA("sp", YGT_v[:, 2 * h:2 * h + 2, q0:q0 + 512], ygs[k][:], [("g_ygs", k, 0), ("g_ygs", k, 1)], [("YGT", h, q0)])
    es.close()

    if stop_after < 4:
        return finish()
    P.barrier()
    es = ExitStack()
    QT_ = 512 if HALF >= 512 else HALF
    dq = [sb("d_q%d" % i, [128, HALF], BF16) for i in range(2)]
    dk = [sb("d_k%d" % i, [128, TOK], BF16) for i in range(2)]
    dv = [sb("d_v%d" % i, [128, NCK, 128], BF16) for i in range(2)]
    NPB = 4
    pb = [sb("d_p%d" % i, [128, 2, QT_], BF16) for i in range(NPB)]
    acc = sb("d_acc", [128, 2, QT_], F32)
    rz = [sb("d_rz%d" % i, [128, QT_], F32) for i in range(2)]
    t12 = [sb("d_t%d" % i, [128, QT_], F32) for i in range(2)]
    od = sb("d_o", [128, QT_], F32)
    dsq = sb("d_sq", [128, QT_], F32)
    drs = sb("d_rs", [128, QT_], F32)
    yds = [sb("d_yds%d" % i, [128, QT_], BF16) for i in range(2)]
    dgn = sb("d_gn", [128, 8], F32)
    TS("dve", dgn[:], dng[:], 0.8, None, ALU.mult, None, ["dng"], ["d_gn"])
    ps_s = [pst("ps_s%d" % i, [128, 2, QT_]) for i in range(2)]
    ps_av = [pst("ps_av%d" % c, [128, QT_]) for c in range(2)]
    ps_z = [pst("ps_z%d" % c, [128, QT_]) for c in range(2)]
    DV_v = DV.rearrange("(n p) e -> p n e", p=128)
    kdeps = dram_deps("DKT", 0, CTX, CTX) + dram_deps("DKT", CTX, TOK, TW2)
    vdeps_d = [("DV", 0, cb) for cb in range(2)] + [("DV", c0, cb) for c0 in range(CTX, TOK, TW2) for cb in range(2)]
    it = [0]
    for h in range(8):
        hb_ = h % 2
        DMA("sp", dq[hb_][:], DQT[h * 128:(h + 1) * 128, :], dram_deps("DQT", 0, HALF, TW2), [("d_q", hb_)])
        DMA("sp", dk[hb_][:], DKT[h * 128:(h + 1) * 128, :], kdeps, [("d_k", hb_)])
        for n0 in range(0, NCK, 16):
            n1 = min(NCK, n0 + 16)
            DMA("sp", dv[hb_][:, n0:n1, :], DV_v[:, n0:n1, h * 128:(h + 1) * 128], vdeps_d, [("d_v", hb_, n0)])
        vres = [("d_v", hb_, n0) for n0 in range(0, NCK, 16)]
        for q0 in range(0, HALF, QT_):
            def scores(kc):
                sbuf_i = kc % 2
                for c in range(2):
                    MM(ps_s[sbuf_i][:, c, :], dk[hb_][64 * c:64 * c + 64, kc * 128:(kc + 1) * 128],
                       dq[hb_][64 * c:64 * c + 64, q0:q0 + QT_], True, True,
                       [("d_k", hb_), ("d_q", hb_)], [("ps_s", sbuf_i)])

            MEMSET("dve", acc[:, 0, :], 0.0, [("d_acc", 0)])
            MEMSET("pool", acc[:, 1, :], 0.0, [("d_acc", 1)])
            scores(0)
            for kc in range(NCK):
                sbuf_i = kc % 2
                pi = kc % NPB
                P.op("act", (lambda sbuf_i=sbuf_i, pi=pi: (lambda e: e.activation(
                    out=pb[pi][:], in_=ps_s[sbuf_i][:], func=AF.Exp, scale=0.125)))(),
                    [("ps_s", sbuf_i)], [("d_p", pi)])
                if kc + 1 < NCK:
                    scores(kc + 1)
                for c in range(2):
                    MM(ps_av[c][:], dv[hb_][:, kc, :], pb[pi][:, c, :], kc == 0, kc == NCK - 1,
                       vres + [("d_p", pi)], [("ps_av", c)])
                TT("dve", acc[:, 0, :], acc[:, 0, :], pb[pi][:, 0, :], ALU.add, [("d_acc", 0), ("d_p", pi)], [("d_acc", 0)])
                TT("pool", acc[:, 1, :], acc[:, 1, :], pb[pi][:, 1, :], ALU.add, [("d_acc", 1), ("d_p", pi)], [("d_acc", 1)])
            for c in range(2):
                MM(ps_z[c][:], ones_f[:, 0:128], acc[:, c, :], True, True, ["ones_f", ("d_acc", c)], [("ps_z", c)])
            for c in range(2):
                RECIP(rz[c][:], ps_z[c][:], [("ps_z", c)], [("d_rz", c)])
                TT("dve", t12[c][:], ps_av[c][:], rz[c][:], ALU.mult, [("ps_av", c), ("d_rz", c)], [("d_t", c)])
            STT(od[:], t12[1][:], neglam, t12[0][:], ALU.mult, ALU.add, [("d_t", 0), ("d_t", 1), "lam_w"], ["d_o"])
            ACT(dsq[:], od[:], AF.Square, ["d_o"], ["d_sq"])
            MM(ps_z[0][:], ones_f[:, 0:128], dsq[:], True, True, ["d_sq", "ones_f"], [("ps_z", 0)])
            ACT(drs[:], ps_z[0][:], AF.Sqrt, [("ps_z", 0), "eps_t"], ["d_rs"], bias=eps_t[:, 0:1], scale=1.0 / 128)
            RECIP(drs[:], drs[:], ["d_rs"], ["d_rs"])
            k = it[0] % 2
            it[0] += 1
            STT(yds[k][:], od[:], dgn[:, h:h + 1], drs[:], ALU.mult, ALU.mult, ["d_o", "d_gn", "d_rs"], [("d_yds", k)])
            DMA("sp", YDT[h * 128:(h + 1) * 128, q0:q0 + QT_], yds[k][:], [("d_yds", k)], [("YDT", h, q0)])
    es.close()

    if stop_after < 5:
        return finish()
    P.barrier()
    es = ExitStack()
    TW5 = 512 if HALF >= 512 else HALF
    wbg = sb("m_wbg", [128, 8, D], BF16)
    wbd = sb("m_wbd", [128, 8, D], BF16)
    wo = sb("m_wo", [128, 8, D], BF16)
    load_w_bf16(wbg, wbg_d.rearrange("(kc p) n -> p kc n", p=128), 8, D, "m_wbg")
    load_w_bf16(wbd, wbd_d.rearrange("(kc p) n -> p kc n", p=128), 8, D, "m_wbd")
    load_w_bf16(wo, wo_d.rearrange("(kc p) n -> p kc n", p=128), 8, D, "m_wo")
    ygt = [sb("m_yg%d" % i, [128, 8, TW5], BF16) for i in range(2)]
    ydt = [sb("m_yd%d" % i, [128, 8, TW5], BF16) for i in range(2)]
    mgt = [sb("m_mg%d" % i, [128, 16, TW5], BF16) for i in range(2)]
    x1t = [sb("m_x1%d" % i, [128, 8, TW5], F32) for i in range(2)]
    zb = sb("m_z", [128, 8, TW5], BF16)
    z1 = [sb("m_z1_%d" % i, [128, TW5], F32) for i in range(2)]
    z2 = [sb("m_z2_%d" % i, [128, TW5], F32) for i in range(2)]
    ps_a = [pst("ps_a%d" % i, [128, TW5]) for i in range(2)]
    ps_b = [pst("ps_b%d" % i, [128, TW5]) for i in range(2)]
    ps_w = [pst("ps_w%d" % i, [128, TW5]) for i in range(2)]
    YDT_v = YDT.rearrange("(c p) n -> p c n", p=128)
    X2T_v = X2T.rearrange("(c p) n -> p c n", p=128)
    for t, q0 in enumerate(range(0, HALF, TW5)):
        k = t % 2
        ygd = [("YGT", hh, qq) for hh in range(4) for qq in range(q0 - q0 % 512, q0 + TW5, 512)]
        ydd = [("YDT", hh, qq) for hh in range(8) for qq in range(q0 - q0 % QT_, q0 + TW5, QT_)]
        DMA("sp", ygt[k][:], YGT_v[:, :, q0:q0 + TW5], ygd, [("m_yg", k)])
        DMA("sp", ydt[k][:], YDT_v[:, :, q0:q0 + TW5], ydd, [("m_yd", k)])
        DMA("sp", mgt[k][:], MGT_v[:, :, q0:q0 + TW5], dram_deps("MGT", q0, q0 + TW5, TW2), [("m_mg", k)])
        DMA("sp", x1t[k][:], X1T_v[:, :, q0:q0 + TW5], dram_deps("X1T", q0, q0 + TW5, TW1), [("m_x1", k, c) for c in range(8)])
        for c in range(8):
            i = c % 2
            for kc in range(8):
                MM(ps_a[i][:], wbg[:, kc, c * 128:(c + 1) * 128], ygt[k][:, kc, :], kc == 0, kc == 7,
                   [("m_wbg", kc), ("m_yg", k)], [("ps_a", i)])
            for kc in range(8):
                MM(ps_b[i][:], wbd[:, kc, c * 128:(c + 1) * 128], ydt[k][:, kc, :], kc == 0, kc == 7,
                   [("m_wbd", kc), ("m_yd", k)], [("ps_b", i)])
            TT("dve", z1[i][:], ps_a[i][:], mgt[k][:, c, :], ALU.mult, [("ps_a", i), ("m_mg", k)], [("m_z1", i)])
            TT("dve", z2[i][:], ps_b[i][:], mgt[k][:, 8 + c, :], ALU.mult, [("ps_b", i), ("m_mg", k)], [("m_z2", i)])
            TT("pool", zb[:, c, :], z1[i][:], z2[i][:], ALU.add, [("m_z1", i), ("m_z2", i)], [("m_z", c)])
        for c in range(8):
            i = c % 2
            for kc in range(8):
                MM(ps_w[i][:], wo[:, kc, c * 128:(c + 1) * 128], zb[:, kc, :], kc == 0, kc == 7,
                   [("m_wo", kc), ("m_z", kc)], [("ps_w", i)])
            STT(x1t[k][:, c, :], ps_w[i][:], gte[:, 1, c, 0:1], x1t[k][:, c, :], ALU.mult, ALU.add,
                [("ps_w", i), "gte", ("m_x1", k, c)], [("m_x1", k, c)])
        DMA("sp", X2T_v[:, :, q0:q0 + TW5], x1t[k][:], [("m_x1", k, c) for c in range(8)], [("X2T", q0)])
    es.close()

    if stop_after < 6:
        return finish()
    P.barrier()
    tiles6 = [(i * TW1, TW1, 0, i * TW1) for i in range(HALF // TW1)]
    outT_v = outT.rearrange("(c p) n -> p c n", p=128)

    def post6(t, x_, W_, sq, ps_stat, rstd2, tmp, h2, tag):
        c0 = tiles6[t][0]
        rms_rstd(tag, tag + "brstd", lambda c: x_[:, c, :W_], 8, W_, sq, ps_stat, rstd2, 1.0 / D,
                 lambda c: [(tag + "xt", t % 2, c)])
        for hh in range(2):
            for c4 in range(4):
                c = hh * 4 + c4
                STT(h2[:, c4, :W_], x_[:, c, :W_], fg[:, c:c + 1], rstd2[:, :W_], ALU.mult, ALU.mult,
                    [(tag + "xt", t % 2, c), "fg", tag + "brstd"], [(tag + "h2", c4)])
            DMA("sp", outT_v[:, hh * 4:hh * 4 + 4, c0:c0 + W_], h2[:, :, :W_], [(tag + "h2", c4) for c4 in range(4)], [("outT", c0, hh)])

    ffn_phase("f2", f2w1, f2w3, f2w2, tiles6, TW1, X2T, 2, post6, lambda t: [("X2T", (tiles6[t][0] // TW5) * TW5)])

    P.emit()
    cons.close()
    return nc


def _const_tables(SEQ):
    half, quarter = 32, 16
    inv = (10000.0 ** (-np.arange(quarter, dtype=np.float32) / quarter)).astype(np.float32)
    t = np.arange(SEQ, dtype=np.int32)
    row = (t // GRID_W).astype(np.float32)
    col = (t % GRID_W).astype(np.float32)
    cosT = np.zeros((128, SEQ), np.float32)
    sinT = np.zeros((128, SEQ), np.float32)
    perm = np.zeros((128, 128), np.float32)
    for p in range(128):
        d = p % 64
        pos = row if d < half else col
        j = d % quarter
        ang = (pos * inv[j]).astype(np.float32)
        cosT[p] = np.cos(ang).astype(np.float32)
        sn = np.sin(ang).astype(np.float32)
        if (d % half) < quarter:
            sinT[p] = -sn
            perm[p + 16, p] = 1.0
        else:
            sinT[p] = sn
            perm[p - 16, p] = 1.0
    s = np.arange(128)
    masks = np.zeros((128, 2, 128), np.float32)
    masks[:, 0, :] = (s[:, None] <= s[None, :])
    masks[:, 1, :] = (s[:, None] >= s[None, :])
    return cosT, sinT, perm, masks


def _prep(inputs, SEQ):
    f = lambda a: np.ascontiguousarray(np.asarray(a, dtype=np.float32))
    x = f(inputs["x"]); c = f(inputs["c"]); ctx = f(inputs["ctx"]); c_ctx = f(inputs["c_ctx"])
    B = x.shape[0]
    cosT, sinT, perm, masks = _const_tables(SEQ)
    w_in0 = f(inputs["w_in"][0])
    w_in1 = w_in0.copy()
    w_in1[:, O_LR:O_LR + 16] = w_in0[:, O_LR + 16:O_LR + 32]
    w_in1[:, O_LR + 16:O_LR + 32] = w_in0[:, O_LR:O_LR + 16]
    gw2 = f(inputs["gla_gate_w2"][0]); gb = f(inputs["gla_gate_b"][0])
    shared = {
        "w_ada": f(inputs["w_ada"][0]),
        "bT_ada": f(inputs["b_ada"][0].reshape(72, 128).T),
        "ngT": f(np.stack([inputs["ffn1_norm"][0], inputs["mix_norm"][0], inputs["ffn2_norm"][0]]).reshape(3, 8, 128).transpose(2, 0, 1)),
        "fgT": f(np.asarray(inputs["final_norm"]).reshape(8, 128).T),
        "gngT": f(np.asarray(inputs["gla_out_norm"][0]).reshape(8, 128).T),
        "dngT": f(np.asarray(inputs["diff_out_norm"][0]).reshape(8, 128).T),
        "f1w1": f(inputs["ffn1_w1"][0]), "f1w3": f(inputs["ffn1_w3"][0]), "f1w2": f(inputs["ffn1_w2"][0]),
        "f2w1": f(inputs["ffn2_w1"][0]), "f2w3": f(inputs["ffn2_w3"][0]), "f2w2": f(inputs["ffn2_w2"][0]),
        "lamT": f(np.tile(np.asarray(inputs["diff_lambda"][0]).reshape(1, 256), (128, 1))),
        "permM": perm, "masks": masks, "ident": np.eye(128, dtype=np.float32),
        "wbg": f(inputs["w_branch_gla"][0]), "wbd": f(inputs["w_branch_diff"][0]), "wo": f(inputs["w_out"][0]),
    }
    in_maps = []
    for core in range(2 * B):
        b, hf = core // 2, core % 2
        xs = x[b]; cs_ = ctx[b]
        if hf == 1:
            xs = xs[::-1]; cs_ = cs_[::-1]
        xT = np.ascontiguousarray(np.concatenate([cs_, xs], axis=0).T)
        order = (0, 1) if hf == 0 else (1, 0)
        m = dict(shared)
        m["xT"] = xT
        m["cT"] = f(np.stack([c[b].reshape(8, 128).T, c_ctx.reshape(8, 128).T], axis=-1))
        m["w_in"] = w_in0 if hf == 0 else w_in1
        m["gw2"] = f(np.stack([gw2[order[0]], gw2[order[1]]]))
        m["gbT"] = f(np.stack([gb[order[0]].reshape(4, 128).T, gb[order[1]].reshape(4, 128).T], axis=1))
        m["cosT"] = cosT if hf == 0 else np.ascontiguousarray(cosT[:, ::-1])
        m["sinT"] = sinT if hf == 0 else np.ascontiguousarray(sinT[:, ::-1])
        in_maps.append(m)
    return in_maps


def _run(inputs, SEQ, dbg=False, stop_after=99):
    B = np.asarray(inputs["x"]).shape[0]
    HALF = SEQ // 2
    in_maps = _prep(inputs, SEQ)
    nc = build_nc(SEQ, dbg=dbg, stop_after=stop_after)
    res = run_bass_kernel_spmd(nc, in_maps, core_ids=list(range(2 * B)))
    out = np.empty((B, SEQ, D), np.float32)
    for core in range(2 * B):
        b, hf = core // 2, core % 2
        o = res.results[core]["outT"].T
        if hf == 0:
            out[b, :HALF] = o
        else:
            out[b, HALF:] = o[::-1]
    return out, res


def kernel(**inputs):
    SEQ = np.asarray(inputs["x"]).shape[1]
    out, _ = _run(inputs, SEQ)
    return out
```

```python
import numpy as np
import concourse.bass as bass
import concourse.mybir as mybir
from concourse.bass_utils import run_bass_kernel_spmd

F32 = mybir.dt.float32
BF16 = mybir.dt.bfloat16
AF = mybir.ActivationFunctionType
ALU = mybir.AluOpType

D = 1024
NC = 8
FF = 2816
NF = 22
CTX = 256
EPS = 1e-6
GRID_W = 64
IN_COLS = 8224
O_GQ, O_GK, O_GV, O_GG, O_LR, O_DQ, O_DK, O_DV, O_MG = 0, 512, 1024, 2048, 3072, 3104, 4128, 5152, 6176
N_DMA_SEMS = 24
SKIP = set()


class Prog:
    def __init__(self, nc):
        self.nc = nc
        self.ops = []
        self.lastw = {}
        self.readers = {}
        self.ndma = 0
        self.nsw = 0
        self.dma_prev = {}
        self._guards = []
        self.barriers = []
        self.pending_bar = {}
        self.pool_dmas = []

    def barrier(self):
        ops = self.ops
        prior = set()
        seen_e = set()
        seen_s = set()
        for i in range(len(ops) - 1, -1, -1):
            o = ops[i]
            if o[3]:
                if o[4] not in seen_s:
                    seen_s.add(o[4]); prior.add(i)
            elif o[0] not in seen_e:
                seen_e.add(o[0]); prior.add(i)
        prior = sorted(prior)
        last = None
        for j in range(0, max(1, len(prior)), 3):
            idx = len(ops)
            ops.append(["sp", (lambda e: e.nop()), set(prior[j:j + 3]), False, None])
            last = idx
        self.pending_bar = {eng: last for eng in ("pe", "act", "dve", "pool")}

    def _deps(self, reads, writes):
        d = set()
        for r in reads:
            w = self.lastw.get(r)
            if w is not None:
                d.add(w)
        for w_ in writes:
            w = self.lastw.get(w_)
            if w is not None:
                d.add(w)
            for r in self.readers.get(w_, ()):
                d.add(r)
        return d

    def _commit(self, idx, reads, writes):
        for w_ in writes:
            self.lastw[w_] = idx
            self.readers[w_] = []
        for r in reads:
            if r not in writes:
                self.readers.setdefault(r, []).append(idx)

    def op(self, eng, fn, reads=(), writes=()):
        idx = len(self.ops)
        d = self._deps(reads, writes)
        if eng in self.pending_bar:
            d.add(self.pending_bar.pop(eng))
        self.ops.append([eng, fn, d, False, None])
        self._commit(idx, reads, writes)
        return idx

    def dma(self, eng, fn, reads=(), writes=()):
        idx = len(self.ops)
        d = self._deps(reads, writes)
        if eng in self.pending_bar:
            d.add(self.pending_bar.pop(eng))
        if eng == "pool":
            s = "sw%d" % (self.nsw % 8)
            self.nsw += 1
        else:
            s = "hw%d" % (self.ndma % N_DMA_SEMS)
            self.ndma += 1
        if s in self.dma_prev:
            d.add(self.dma_prev[s])
        self.dma_prev[s] = idx
        if eng == "pool":
            if len(self.pool_dmas) >= 12:
                d.add(self.pool_dmas[-12])
            self.pool_dmas.append(idx)
        self.ops.append([eng, fn, d, True, s])
        self._commit(idx, reads, writes)
        return idx

    def emit(self, final_wait_all_dma=True):
        nc = self.nc
        ops = self.ops
        for p in self.barriers:
            prior = set()
            seen_e = set()
            seen_s = set()
            for i in range(p - 1, -1, -1):
                o = ops[i]
                if o[3]:
                    if o[4] not in seen_s:
                        seen_s.add(o[4]); prior.add(i)
                elif o[0] not in seen_e:
                    seen_e.add(o[0]); prior.add(i)
                if len(seen_e) >= 4 and len(seen_s) >= N_DMA_SEMS + 8:
                    break
            done_e = set()
            for i in range(p, len(ops)):
                if ops[i][0] not in done_e:
                    done_e.add(ops[i][0])
                    ops[i][2] |= prior
                if len(done_e) >= 5:
                    break
        need = [False] * len(ops)
        for i, (eng, fn, deps, is_dma, s) in enumerate(ops):
            for dp in deps:
                pe = ops[dp]
                if pe[0] == "pe" and eng == "pe" and not pe[3] and not is_dma:
                    continue
                need[dp] = True
        for i, o in enumerate(ops):
            if o[3]:
                need[i] = True
        sems = {}

        def getsem(name):
            if name not in sems:
                g = nc.semaphore(name)
                sems[name] = g.__enter__()
                self._guards.append(g)
            return sems[name]

        cnt = {}
        tok = [None] * len(ops)
        for i, (eng, fn, deps, is_dma, s) in enumerate(ops):
            if not need[i]:
                continue
            name = ("dma" + s) if is_dma else ("e_" + eng)
            getsem(name)
            cnt[name] = cnt.get(name, 0) + (16 if is_dma else 1)
            tok[i] = (name, cnt[name])
        per = {k: [] for k in ("pe", "act", "dve", "pool", "sp")}
        for i, (eng, fn, deps, is_dma, s) in enumerate(ops):
            waits = []
            for dp in deps:
                pe = ops[dp]
                if pe[0] == "pe" and eng == "pe" and not pe[3] and not is_dma:
                    continue
                waits.append(tok[dp])
            per[eng].append((fn, waits, tok[i], 16 if is_dma else 1))
        final = [(n, c) for n, c in cnt.items() if n.startswith("dma")]
        with nc.Block() as block:
            def run(e, lst, fin=False):
                seen = {}
                for fn, waits, t, amt in lst:
                    for (sn, v) in waits:
                        if seen.get(sn, -1) >= v:
                            continue
                        seen[sn] = v
                        e.wait_ge(sems[sn], v)
                    ins = fn(e)
                    if t is not None:
                        ins.then_inc(sems[t[0]], amt)
                if fin:
                    for j, (sn, v) in enumerate(final):
                        e.wait_ge(sems[sn], v)
                        if j % 3 == 2:
                            e.nop()

            @block.tensor
            def _(e): run(e, per["pe"])

            @block.scalar
            def _(e): run(e, per["act"])

            @block.vector
            def _(e): run(e, per["dve"])

            @block.gpsimd
            def _(e): run(e, per["pool"])

            @block.sync
            def _(e): run(e, per["sp"], fin=True)


def build_nc(SEQ, dbg=False, stop_after=99):
    HALF = SEQ // 2
    TOK = CTX + SEQ
    OWN_END = CTX + HALF
    nc = bass.Bass("TRN2", target_bir_lowering=False)
    P = Prog(nc)

    def din(name, shape, dt=F32):
        return nc.dram_tensor(name, list(shape), dt, kind="ExternalInput").ap()

    def dscr(name, shape, dt):
        return nc.dram_tensor(name, list(shape), dt, kind=("ExternalOutput" if dbg else "Internal")).ap()

    xT = din("xT", [D, TOK])
    cT = din("cT", [128, 8, 2])
    w_ada = din("w_ada", [D, 9 * D])
    bT_ada = din("bT_ada", [128, 72])
    ngT = din("ngT", [128, 3, 8])
    fgT = din("fgT", [128, 8])
    gngT = din("gngT", [128, 8])
    dngT = din("dngT", [128, 8])
    f1w1 = din("f1w1", [D, FF]); f1w3 = din("f1w3", [D, FF]); f1w2 = din("f1w2", [FF, D])
    f2w1 = din("f2w1", [D, FF]); f2w3 = din("f2w3", [D, FF]); f2w2 = din("f2w2", [FF, D])
    w_in = din("w_in", [D, IN_COLS])
    gw2 = din("gw2", [2, 16, 512])
    gbT = din("gbT", [128, 2, 4])
    lamT = din("lamT", [128, 256])
    cosT = din("cosT", [128, SEQ])
    sinT = din("sinT", [128, SEQ])
    permM = din("permM", [128, 128])
    masks = din("masks", [128, 2, 128])
    ident = din("ident", [128, 128])
    wbg_d = din("wbg", [D, D]); wbd_d = din("wbd", [D, D]); wo_d = din("wo", [D, D])
    outT = nc.dram_tensor("outT", [D, HALF], F32, kind="ExternalOutput").ap()

    X1T = dscr("X1T", [D, HALF], F32)
    H2T = dscr("H2T", [D, TOK], BF16)
    GQT = dscr("GQT", [512, HALF], BF16)
    GKT = dscr("GKT", [512, TOK], BF16)
    GV = dscr("GV", [TOK, D], BF16)
    GGT = dscr("GGT", [D, HALF], BF16)
    SPT = dscr("SPT", [2, 512, TOK], F32)
    DQT = dscr("DQT", [D, HALF], BF16)
    DKT = dscr("DKT", [D, TOK], BF16)
    DV = dscr("DV", [TOK, D], BF16)
    MGT = dscr("MGT", [2 * D, HALF], BF16)
    YGT = dscr("YGT", [D, HALF], BF16)
    YDT = dscr("YDT", [D, HALF], BF16)
    X2T = dscr("X2T", [D, HALF], F32)

    def MM(out, lhsT, rhs, start, stop, R, W):
        P.op("pe", lambda e: e.matmul(out, lhsT, rhs, start=start, stop=stop), R, W)

    def TR(out, in_, idn, R, W):
        P.op("pe", lambda e: e.transpose(out, in_, idn), R, W)

    def ACT(out, in_, func, R, W, bias=None, scale=None):
        kw = {}
        if bias is not None:
            kw["bias"] = bias
        if scale is not None:
            kw["scale"] = scale
        P.op("act", lambda e: e.activation(out=out, in_=in_, func=func, **kw), R, W)

    def TS(eng, out, in0, s1, s2, op0, op1, R, W):
        if op1 is None:
            P.op(eng, lambda e: e.tensor_scalar(out=out, in0=in0, scalar1=s1, scalar2=None, op0=op0), R, W)
        else:
            P.op(eng, lambda e: e.tensor_scalar(out=out, in0=in0, scalar1=s1, scalar2=s2, op0=op0, op1=op1), R, W)

    def STT(out, in0, scalar, in1, op0, op1, R, W):
        P.op("dve", lambda e: e.scalar_tensor_tensor(out=out, in0=in0, scalar=scalar, in1=in1, op0=op0, op1=op1), R, W)

    def TT(eng, out, in0, in1, op, R, W):
        P.op(eng, lambda e: e.tensor_tensor(out=out, in0=in0, in1=in1, op=op), R, W)

    def RECIP(out, in_, R, W):
        P.op("dve", lambda e: e.reciprocal(out=out, in_=in_), R, W)

    def CP(eng, out, in_, R, W):
        if eng == "act":
            P.op("act", lambda e: e.activation(out=out, in_=in_, func=AF.Identity), R, W)
        else:
            P.op(eng, lambda e: e.tensor_copy(out=out, in_=in_), R, W)

    def MEMSET(eng, ap, v, W):
        P.op(eng, lambda e: e.memset(ap, v), (), W)

    def DMA(q, out, in_, R, W):
        P.dma(q, lambda e: e.dma_start(out=out, in_=in_), R, W)

    from contextlib import ExitStack
    es = ExitStack()

    def sb(name, shape, dt):
        return es.enter_context(nc.sbuf_tensor(name, list(shape), dt))

    def pst(name, shape, dt=F32):
        return es.enter_context(nc.psum_tensor(name, list(shape), dt))

    cons = ExitStack()

    def csb(name, shape, dt):
        return cons.enter_context(nc.sbuf_tensor(name, list(shape), dt))

    modt = csb("modt", [128, 72, 2], F32)
    gm = csb("gm", [128, 3, 8, 2], F32)
    shf = csb("shf", [128, 3, 8, 2], F32)
    gte = csb("gte", [128, 3, 8, 2], F32)
    ng = csb("ng", [128, 3, 8], F32)
    fg = csb("fg", [128, 8], F32)
    gng = csb("gng", [128, 8], F32)
    dng = csb("dng", [128, 8], F32)
    ones_f = csb("ones_f", [128, 512], F32)
    ones_b = csb("ones_b", [128, 128], BF16)
    id_b = csb("id_b", [128, 128], BF16)
    perm_b = csb("perm_b", [128, 128], BF16)
    mask_f = csb("mask_f", [128, 2, 128], F32)
    gw2s = csb("gw2s", [16, 2, 512], F32)
    gb = csb("gb", [128, 2, 4], F32)
    ngb = csb("ngb", [128, 2, 4], F32)
    lam_s = csb("lam_s", [128, 256], F32)
    lam_w = csb("lam_w", [128, 8], F32)
    eps_t = csb("eps_t", [128, 1], F32)

    DMA("sp", ng[:], ngT, (), ["ng"])
    DMA("sp", fg[:], fgT, (), ["fg"])
    DMA("sp", gng[:], gngT, (), ["gng"])
    DMA("sp", dng[:], dngT, (), ["dng"])
    DMA("sp", mask_f[:], masks, (), ["mask_f"])
    DMA("sp", gb[:], gbT, (), ["gb"])
    DMA("sp", lam_s[:], lamT, (), ["lam_s"])
    for d_ in range(2):
        DMA("sp", gw2s[:, d_, :], gw2[d_], (), [("gw2s", d_)])
    DMA("pool", id_b[:], ident, (), ["id_b"])
    DMA("pool", perm_b[:], permM, (), ["perm_b"])
    MEMSET("dve", ones_f[:], 1.0, ["ones_f"])
    MEMSET("dve", ones_b[:], 1.0, ["ones_b"])
    MEMSET("dve", eps_t[:], EPS, ["eps_t"])
    TS("dve", ngb[:], gb[:], -1.0, None, ALU.mult, None, ["gb"], ["ngb"])
    TT("dve", lam_s[:, 0:64], lam_s[:, 0:64], lam_s[:, 64:128], ALU.mult, ["lam_s"], ["lam_s"])
    TT("dve", lam_s[:, 128:192], lam_s[:, 128:192], lam_s[:, 192:256], ALU.mult, ["lam_s"], ["lam_s"])
    P.op("dve", lambda e: e.reduce_sum(out=lam_w[:, 0:1], in_=lam_s[:, 0:64], axis=mybir.AxisListType.X), ["lam_s"], ["lam_w"])
    P.op("dve", lambda e: e.reduce_sum(out=lam_w[:, 1:2], in_=lam_s[:, 128:192], axis=mybir.AxisListType.X), ["lam_s"], ["lam_w"])
    ACT(lam_w[:, 2:4], lam_w[:, 0:2], AF.Exp, ["lam_w"], ["lam_w"])
    TT("dve", lam_w[:, 4:5], lam_w[:, 3:4], lam_w[:, 2:3], ALU.subtract, ["lam_w"], ["lam_w"])
    TS("dve", lam_w[:, 4:5], lam_w[:, 4:5], -0.2, None, ALU.add, None, ["lam_w"], ["lam_w"])
    neglam = lam_w[:, 4:5]

    es = ExitStack()
    sc = sb("sc", [128, 8, 2], F32)
    bTa = sb("bTa", [128, 72], F32)
    wa = [sb("wa%d" % i, [128, 8, 1024], F32) for i in range(2)]
    ps_m = [pst("ps_m%d" % i, [128, 16]) for i in range(2)]
    DMA("sp", sc[:], cT, (), ["sc"])
    DMA("sp", bTa[:], bT_ada, (), ["bTa"])
    ACT(sc[:], sc[:], AF.Silu, ["sc"], ["sc"])
    w_ada_v = w_ada.rearrange("(kc p) n -> p kc n", p=128)
    for m in range(9):
        wb_ = wa[m % 2]
        for kc in range(8):
            DMA("sp", wb_[:, kc, :], w_ada_v[:, kc, m * 1024:(m + 1) * 1024], (), [("wa", m % 2, kc)])
        pm = ps_m[m % 2]
        for c in range(8):
            for kc in range(8):
                MM(pm[:, 2 * c:2 * c + 2], wb_[:, kc, c * 128:(c + 1) * 128], sc[:, kc, :], kc == 0, kc == 7,
                   [("wa", m % 2, kc), "sc"], [("ps_m", m % 2)])
        for j in range(2):
            TT("dve", modt[:, m * 8:(m + 1) * 8, j], pm[:, j:16:2], bTa[:, m * 8:(m + 1) * 8], ALU.add,
               [("ps_m", m % 2), "bTa"] + [("wa", m % 2, kc) for kc in range(8)], ["modt"])
    for i in range(3):
        for j in range(2):
            STT(gm[:, i, :, j], modt[:, (3 * i + 1) * 8:(3 * i + 2) * 8, j], 1.0, ng[:, i, :], ALU.add, ALU.mult,
                ["modt", "ng"], ["gm"])
            CP("dve", shf[:, i, :, j], modt[:, (3 * i) * 8:(3 * i + 1) * 8, j], ["modt"], ["shf"])
            TS("dve", gte[:, i, :, j], modt[:, (3 * i + 2) * 8:(3 * i + 3) * 8, j], (1.0 if i == 1 else 0.5), None,
               ALU.mult, None, ["modt"], ["gte"])
    es.close()

    def load_w_bf16(dst, src_v, nk, ncols, resname):
        step = 1024
        for kc in range(nk):
            for c0 in range(0, ncols, step):
                c1 = min(ncols, c0 + step)
                DMA("pool", dst[:, kc, c0:c1], src_v[:, kc, c0:c1], (), [(resname, kc)])

    def rms_rstd(key, rres, src_chunks, nch, W_, sq, ps_stat, rstd, inv_n, Rsrc):
        for c in range(nch):
            s_ = sq[c % 2]
            ACT(s_[:, :W_], src_chunks(c), AF.Square, Rsrc(c), [(key + "sq", c % 2)])
            MM(ps_stat[:, :W_], ones_f[:, 0:128], s_[:, :W_], c == 0, c == nch - 1,
               [(key + "sq", c % 2), "ones_f"], [key + "ps_stat"])
        ACT(rstd[:, :W_], ps_stat[:, :W_], AF.Sqrt, [key + "ps_stat", "eps_t"], [rres], bias=eps_t[:, 0:1], scale=inv_n)
        RECIP(rstd[:, :W_], rstd[:, :W_], [rres], [rres])

    def ffn_phase(tag, w1d, w3d, w2d, tiles, TW, src_T, li, post, src_deps):
        nonlocal es
        es = ExitStack()
        w1b = sb(tag + "w1b", [128, 8, FF], BF16)
        w3b = sb(tag + "w3b", [128, 8, FF], BF16)
        w2b = sb(tag + "w2b", [128, NF, D], BF16)
        load_w_bf16(w1b, w1d.rearrange("(kc p) n -> p kc n", p=128), 8, FF, tag + "w1b")
        load_w_bf16(w3b, w3d.rearrange("(kc p) n -> p kc n", p=128), 8, FF, tag + "w3b")
        load_w_bf16(w2b, w2d.rearrange("(kc p) n -> p kc n", p=128), NF, D, tag + "w2b")
        xt = [sb(tag + "xt%d" % i, [128, 8, TW], F32) for i in range(2)]
        hb = sb(tag + "hb", [128, 8, TW], BF16)
        ab = sb(tag + "ab", [128, NF, TW], BF16)
        sq = [sb(tag + "sq%d" % i, [128, TW], F32) for i in range(2)]
        rstd = sb(tag + "rstd", [128, TW], F32)
        rstd2 = sb(tag + "rstd2", [128, TW], F32)
        tmp = [sb(tag + "tmp%d" % i, [128, TW], F32) for i in range(2)]
        su = [sb(tag + "su%d" % i, [128, TW], F32) for i in range(2)]
        h2 = sb(tag + "h2", [128, 8, TW], BF16) if li == 0 else sb(tag + "h2", [128, 4, TW], F32)
        ps_u = [pst(tag + "ps_u%d" % i, [128, TW]) for i in range(2)]
        ps_v = [pst(tag + "ps_v%d" % i, [128, TW]) for i in range(2)]
        ps_y = [pst(tag + "ps_y%d" % i, [128, TW]) for i in range(2)]
        ps_stat = pst(tag + "ps_stat", [128, TW])
        src_v = src_T.rearrange("(c p) n -> p c n", p=128)

        def prologue(t):
            c0, W_, j, own = tiles[t]
            x_ = xt[t % 2]
            DMA("sp", x_[:, :, :W_], src_v[:, :, c0:c0 + W_], src_deps(t), [(tag + "xt", t % 2, c) for c in range(8)])
            rms_rstd(tag, tag + "arstd", lambda c: x_[:, c, :W_], 8, W_, sq, ps_stat, rstd, 1.0 / D,
                     lambda c: [(tag + "xt", t % 2, c)])
            for c in range(8):
                tm = tmp[c % 2]
                STT(tm[:, :W_], x_[:, c, :W_], gm[:, li, c, j:j + 1], rstd[:, :W_], ALU.mult, ALU.mult,
                    [(tag + "xt", t % 2, c), "gm", tag + "arstd"], [(tag + "tmp", c % 2)])
                ACT(hb[:, c, :W_], tm[:, :W_], AF.Identity, [(tag + "tmp", c % 2), "shf"], [(tag + "hb", c)],
                    bias=shf[:, li, c, j:j + 1])

        prologue(0)
        for t in range(len(tiles)):
            c0, W_, j, own = tiles[t]
            x_ = xt[t % 2]
            for f in range(NF):
                pu = ps_u[f % 2]; pv = ps_v[f % 2]
                for kc in range(8):
                    MM(pu[:, :W_], w1b[:, kc, f * 128:(f + 1) * 128], hb[:, kc, :W_], kc == 0, kc == 7,
                       [(tag + "w1b", kc), (tag + "hb", kc)], [(tag + "ps_u", f % 2)])
                for kc in range(8):
                    MM(pv[:, :W_], w3b[:, kc, f * 128:(f + 1) * 128], hb[:, kc, :W_], kc == 0, kc == 7,
                       [(tag + "w3b", kc), (tag + "hb", kc)], [(tag + "ps_v", f % 2)])
                s_ = su[f % 2]
                ACT(s_[:, :W_], pu[:, :W_], AF.Silu, [(tag + "ps_u", f % 2)], [(tag + "su", f % 2)])
                TT("dve", ab[:, f, :W_], s_[:, :W_], pv[:, :W_], ALU.mult,
                   [(tag + "su", f % 2), (tag + "ps_v", f % 2)], [(tag + "ab", f)])
            if t + 1 < len(tiles):
                prologue(t + 1)
            for c in range(8):
                py = ps_y[c % 2]
                for f in range(NF):
                    MM(py[:, :W_], w2b[:, f, c * 128:(c + 1) * 128], ab[:, f, :W_], f == 0, f == NF - 1,
                       [(tag + "w2b", f), (tag + "ab", f)], [(tag + "ps_y", c % 2)])
                STT(x_[:, c, :W_], py[:, :W_], gte[:, li, c, j:j + 1], x_[:, c, :W_], ALU.mult, ALU.add,
                    [(tag + "ps_y", c % 2), "gte", (tag + "xt", t % 2, c)], [(tag + "xt", t % 2, c)])
            post(t, x_, W_, sq, ps_stat, rstd2, tmp, h2, tag)
        es.close()

    def finish():
        P.emit()
        cons.close()
        return nc
    if stop_after < 1:
        return finish()
    P.barrier()
    TW1 = 256
    tiles1 = [(0, CTX, 1, None)]
    for i in range(SEQ // TW1):
        tiles1.append((CTX + i * TW1, TW1, 0, (i * TW1 if i * TW1 < HALF else None)))
    H2T_v = H2T.rearrange("(c p) n -> p c n", p=128)
    X1T_v = X1T.rearrange("(c p) n -> p c n", p=128)

    def post1(t, x_, W_, sq, ps_stat, rstd2, tmp, h2, tag):
        c0, _, j, own = tiles1[t]
        if own is not None:
            DMA("sp", X1T_v[:, :, own:own + W_], x_[:, :, :W_], [(tag + "xt", t % 2, c) for c in range(8)], [("X1T", own)])
        rms_rstd(tag, tag + "brstd", lambda c: x_[:, c, :W_], 8, W_, sq, ps_stat, rstd2, 1.0 / D,
                 lambda c: [(tag + "xt", t % 2, c)])
        for c in range(8):
            tm = tmp[c % 2]
            STT(tm[:, :W_], x_[:, c, :W_], gm[:, 1, c, j:j + 1], rstd2[:, :W_], ALU.mult, ALU.mult,
                [(tag + "xt", t % 2, c), "gm", tag + "brstd"], [(tag + "tmp", c % 2)])
            ACT(h2[:, c, :W_], tm[:, :W_], AF.Identity, [(tag + "tmp", c % 2), "shf"], [(tag + "h2", c)],
                bias=shf[:, 1, c, j:j + 1])
        DMA("sp", H2T_v[:, :, c0:c0 + W_], h2[:, :, :W_], [(tag + "h2", c) for c in range(8)], [("H2T", c0)])

    if "p1" not in SKIP:
        ffn_phase("f1", f1w1, f1w3, f1w2, tiles1, TW1, xT, 0, post1, lambda t: ())

    if stop_after < 2:
        return finish()
    P.barrier()
    TW2 = 256
    tiles2 = [(0, CTX, "ctx", None)]
    for i in range(SEQ // TW2):
        tiles2.append((CTX + i * TW2, TW2, ("own" if i * TW2 < HALF else "oth"), i * TW2))
    GKT_v = GKT.rearrange("(c p) n -> p c n", p=128)
    GQT_v = GQT.rearrange("(c p) n -> p c n", p=128)
    GGT_v = GGT.rearrange("(c p) n -> p c n", p=128)
    DQT_v = DQT.rearrange("(c p) n -> p c n", p=128)
    DKT_v = DKT.rearrange("(c p) n -> p c n", p=128)
    MGT_v = MGT.rearrange("(c p) n -> p c n", p=128)
    SPT_v = SPT.rearrange("d (c p) n -> d p c n", p=128)
    w_in_v = w_in.rearrange("(kc p) n -> p kc n", p=128)

    def proj_pass(pid):
        nonlocal es
        es = ExitStack()
        if pid == 0:
            segs = [(O_GK, 512), (O_GV, 1024), (O_LR, 32), (O_DK, 1024), (O_DV, 1024)]
        else:
            segs = [(O_GQ, 512), (O_GG, 1024), (O_DQ, 1024), (O_MG, 2048)]
        cmap = {}
        tot = 0
        for (s0, n) in segs:
            cmap[s0] = tot
            tot += n
        wtag = "winb%d" % pid
        winb = sb(wtag, [128, 8, tot], BF16)
        for (s0, n) in segs:
            if n == 32 and 'lr' in SKIP:
                continue
            for kc in range(8):
                for c0 in range(0, n, 1024):
                    c1 = min(n, c0 + 1024)
                    DMA("pool", winb[:, kc, cmap[s0] + c0:cmap[s0] + c1], w_in_v[:, kc, s0 + c0:s0 + c1], (), [(wtag, kc)])
        pt = "p2%d" % pid
        h2t = [sb(pt + "h2t%d" % i, [128, 8, TW2], BF16) for i in range(2)]
        ps_p = [pst(pt + "ps_p%d" % i, [128, 512]) for i in range(4)]
        pcount = [0]

        def proj_fm(h_, hi, col0, W_):
            k = pcount[0] % 4
            pcount[0] += 1
            pp = ps_p[k]
            for kc in range(8):
                MM(pp[:, :W_], winb[:, kc, col0:col0 + 128], h_[:, kc, :W_], kc == 0, kc == 7,
                   [(wtag, kc), (pt + "h2t", hi)], [(pt + "ps_p", k)])
            return pp, (pt + "ps_p", k)

        stg = {}
        names = (("gk", 4), ("dk", 8)) if pid == 0 else (("gq", 4), ("gg", 8), ("dq", 8), ("mg", 16))
        for nm, nchk in names:
            stg[nm] = sb("stg_" + nm, [128, nchk, TW2], BF16)
        qb = [sb(pt + "qb%d" % i, [128, TW2], BF16) for i in range(2)]
        r1 = [sb(pt + "r1_%d" % i, [128, TW2], F32) for i in range(2)]
        r2 = [sb(pt + "r2_%d" % i, [128, TW2], F32) for i in range(2)]
        cs = [sb(pt + "cs%d" % i, [128, 2, TW2], F32) for i in range(2)]
        ps_r = [pst(pt + "ps_r%d" % i, [128, TW2]) for i in range(2)]
        if pid == 0:
            stv = [sb("stv%d" % i, [128, TW2 // 128, 512], BF16) for i in range(2)]
            lrs = [sb("lrs%d" % i, [16, TW2], F32) for i in range(2)]
            spb = [sb("spb%d" % i, [128, 4, TW2], F32) for i in range(2)]
            ex = [sb("ex%d" % i, [128, TW2], F32) for i in range(2)]
            ps_l = [pst("ps_l%d" % i, [128, TW2]) for i in range(2)]
        ropec = [0]
        tcount = 0
        for (c0, W_, kind, l0) in tiles2:
            if pid == 1 and kind != "own":
                continue
            hi = tcount % 2
            tcount += 1
            h_ = h2t[hi]
            DMA("sp", h_[:, :, :W_], H2T_v[:, :, c0:c0 + W_], [("H2T", c0)], [(pt + "h2t", hi)])
            latent = kind != "ctx"
            if latent:
                DMA("sp", cs[hi][:, 0, :W_], cosT[:, l0:l0 + W_], (), [(pt + "cs", hi, 0)])
                DMA("sp", cs[hi][:, 1, :W_], sinT[:, l0:l0 + W_], (), [(pt + "cs", hi, 1)])

            def rope_evac(pp, pres, dst, dres):
                i = ropec[0] % 2
                ropec[0] += 1
                if not latent or 'norope' in SKIP:
                    CP("act", dst, pp[:, :W_], [pres], [dres])
                    return
                if 'rope_noperm' in SKIP:
                    if 'nocs' in SKIP:
                        TT("dve", r1[i][:, :W_], pp[:, :W_], ones_f[:, :W_], ALU.mult, [pres, "ones_f"], [(pt + "r1", i)])
                    elif 'cs_nodep' in SKIP:
                        TT("dve", r1[i][:, :W_], pp[:, :W_], cs[hi][:, 0, :W_], ALU.mult, [pres], [(pt + "r1", i)])
                    else:
                        TT("dve", r1[i][:, :W_], pp[:, :W_], cs[hi][:, 0, :W_], ALU.mult, [pres, (pt + "cs", hi, 0)], [(pt + "r1", i)])
                    CP("dve", dst, r1[i][:, :W_], [(pt + "r1", i)], [dres])
                    return
                CP("act", qb[i][:, :W_], pp[:, :W_], [pres], [(pt + "qb", i)])
                MM(ps_r[i][:, :W_], perm_b[:], qb[i][:, :W_], True, True, ["perm_b", (pt + "qb", i)], [(pt + "ps_r", i)])
                if 'rope_nodve' in SKIP:
                    CP("act", dst, ps_r[i][:, :W_], [(pt + "ps_r", i)], [dres])
                    return
                if 'r1pp' in SKIP:
                    TT("dve", r1[i][:, :W_], pp[:, :W_], cs[hi][:, 0, :W_], ALU.mult, [pres, (pt + "cs", hi, 0)], [(pt + "r1", i)])
                else:
                    TT("dve", r1[i][:, :W_], qb[i][:, :W_], cs[hi][:, 0, :W_], ALU.mult, [(pt + "qb", i), (pt + "cs", hi, 0)], [(pt + "r1", i)])
                TT("dve", r2[i][:, :W_], ps_r[i][:, :W_], cs[hi][:, 1, :W_], ALU.mult, [(pt + "ps_r", i), (pt + "cs", hi, 1)], [(pt + "r2", i)])
                TT(("pool" if "usepool" in SKIP else "dve"), dst, r1[i][:, :W_], r2[i][:, :W_], ALU.add, [(pt + "r1", i), (pt + "r2", i)], [dres])

            if pid == 0:
                for c in range(4):
                    pp, pres = proj_fm(h_, hi, cmap[O_GK] + c * 128, W_)
                    CP("act", stg["gk"][:, c, :W_], pp[:, :W_], [pres], [("stg_gk", c)])
                DMA("sp", GKT_v[:, :, c0:c0 + W_], stg["gk"][:, :, :W_], [("stg_gk", c) for c in range(4)], [("GKT", c0)])
                for c in ([] if 'dk' in SKIP else range(8)):
                    pp, pres = proj_fm(h_, hi, cmap[O_DK] + c * 128, W_)
                    rope_evac(pp, pres, stg["dk"][:, c, :W_], ("stg_dk", c))
                if "dk" not in SKIP:
                    DMA("sp", DKT_v[:, :, c0:c0 + W_], stg["dk"][:, :, :W_], [("stg_dk", c) for c in range(8)], [("DKT", c0)])
                for (ocol, dst, dname) in ([] if "v" in SKIP else ((cmap[O_GV], GV, "GV"), (cmap[O_DV], DV, "DV"))):
                    for cb in range(2):
                        k = pcount[0] % 2
                        sv = stv[k]
                        for tt_ in range(W_ // 128):
                            kk = pcount[0] % 4
                            pcount[0] += 1
                            pp = ps_p[kk]
                            for kc in range(8):
                                MM(pp[:, :512], h_[:, kc, tt_ * 128:(tt_ + 1) * 128], winb[:, kc, ocol + cb * 512:ocol + (cb + 1) * 512],
                                   kc == 0, kc == 7, [(wtag, kc), (pt + "h2t", hi)], [(pt + "ps_p", kk)])
                            CP(("act" if tt_ % 2 == 0 else "dve"), sv[:, tt_, :], pp[:, :512], [(pt + "ps_p", kk)], [("stv", k, tt_)])
                        dv_ = dst[c0:c0 + W_, cb * 512:(cb + 1) * 512].rearrange("(n p) e -> p n e", p=128)
                        DMA("sp", dv_, sv[:, :W_ // 128, :], [("stv", k, tt_) for tt_ in range(W_ // 128)], [(dname, c0, cb)])
                for d_ in ([] if 'lr' in SKIP else range(2)):
                    i = d_
                    pl = ps_l[i]
                    lc = cmap[O_LR] + 16 * d_
                    for kc in range(8):
                        MM(pl[0:16, :W_], winb[:, kc, lc:lc + 16], h_[:, kc, :W_], kc == 0, kc == 7,
                           [(wtag, kc), (pt + "h2t", hi)], [("ps_l", i)])
                    CP("dve", lrs[i][:, :W_], pl[0:16, :W_], [("ps_l", i)], [("lrs", i)])
                    for c in range(4):
                        kk = pcount[0] % 4
                        pcount[0] += 1
                        pp = ps_p[kk]
                        MM(pp[:, :W_], gw2s[:, d_, c * 128:(c + 1) * 128], lrs[i][:, :W_], True, True,
                           [("gw2s", d_), ("lrs", i)], [(pt + "ps_p", kk)])
                        e_ = ex[c % 2]
                        ACT(e_[:, :W_], pp[:, :W_], AF.Exp, [(pt + "ps_p", kk), "ngb"], [("ex", c % 2)], bias=ngb[:, d_, c:c + 1], scale=-1.0)
                        ACT(spb[i][:, c, :W_], e_[:, :W_], AF.Ln, [("ex", c % 2)], [("spb", i, c)], bias=1.0)
                    DMA("sp", SPT_v[d_][:, :, c0:c0 + W_], spb[i][:, :, :W_], [("spb", i, c) for c in range(4)], [("SPT", d_, c0)])
            else:
                for c in range(4):
                    pp, pres = proj_fm(h_, hi, cmap[O_GQ] + c * 128, W_)
                    ACT(stg["gq"][:, c, :W_], pp[:, :W_], AF.Identity, [pres], [("stg_gq", c)], scale=128.0 ** -0.5)
                DMA("sp", GQT_v[:, :, l0:l0 + W_], stg["gq"][:, :, :W_], [("stg_gq", c) for c in range(4)], [("GQT", l0)])
                for c in range(8):
                    pp, pres = proj_fm(h_, hi, cmap[O_GG] + c * 128, W_)
                    ACT(stg["gg"][:, c, :W_], pp[:, :W_], AF.Silu, [pres], [("stg_gg", c)])
                DMA("sp", GGT_v[:, :, l0:l0 + W_], stg["gg"][:, :, :W_], [("stg_gg", c) for c in range(8)], [("GGT", l0)])
                for c in range(8):
                    pp, pres = proj_fm(h_, hi, cmap[O_DQ] + c * 128, W_)
                    rope_evac(pp, pres, stg["dq"][:, c, :W_], ("stg_dq", c))
                DMA("sp", DQT_v[:, :, l0:l0 + W_], stg["dq"][:, :, :W_], [("stg_dq", c) for c in range(8)], [("DQT", l0)])
                for c in range(16):
                    pp, pres = proj_fm(h_, hi, cmap[O_MG] + c * 128, W_)
                    ACT(stg["mg"][:, c, :W_], pp[:, :W_], AF.Sigmoid, [pres], [("stg_mg", c)])
                DMA("sp", MGT_v[:, :, l0:l0 + W_], stg["mg"][:, :, :W_], [("stg_mg", c) for c in range(16)], [("MGT", l0)])
        es.close()

    proj_pass(0)
    P.barrier()
    if 'pass1' not in SKIP:
        proj_pass(1)

    def dram_deps(name, a, b, step):
        return [(name, cc) for cc in range(a, b, step)]

    if stop_after < 3:
        return finish()
    P.barrier()
    es = ExitStack()
    NCK = TOK // 128
    NOWN = HALF // 128
    qT = sb("g_qT", [128, HALF], BF16)
    kT = sb("g_kT", [128, TOK], BF16)
    vt = sb("g_v", [128, NCK, 256], BF16)
    Pst = [sb("g_P0", [128, CTX + HALF + 1], F32), sb("g_P1", [128, TOK + 1], F32)]
    spl = [sb("g_sp%d" % i, [128, 512], F32) for i in range(2)]
    oT = sb("g_oT", [128, 2, HALF], F32)
    S = sb("g_S", [128, 256], F32)
    Stmp = sb("g_Stmp", [128, 256], F32)
    Sb = sb("g_Sb", [128, 256], BF16)
    E1 = [sb("g_E1_%d" % i, [128, 128], F32) for i in range(2)]
    E2 = [sb("g_E2_%d" % i, [128, 128], F32) for i in range(2)]
    qe = [sb("g_qe%d" % i, [128, 128], BF16) for i in range(2)]
    keT = [sb("g_keT%d" % i, [128, 128], BF16) for i in range(2)]
    ket = [sb("g_ket%d" % i, [128, 128], BF16) for i in range(2)]
    atb = [sb("g_atb%d" % i, [128, 128], BF16) for i in range(2)]
    bcol = [sb("g_bcol%d" % i, [128, 4], F32) for i in range(4)]
    ggt = [sb("g_ggt%d" % i, [128, 2, 512], BF16) for i in range(1)] * 2
    ygs = [sb("g_ygs%d" % i, [128, 2, 512], BF16) for i in range(1)] * 2
    gsq = [sb("g_sq%d" % i, [128, 512], F32) for i in range(2)]
    grs = sb("g_rs", [128, 512], F32)
    gtm = [sb("g_tm%d" % i, [128, 512], F32) for i in range(2)]
    ps_at = [pst("ps_at%d" % i, [128, 512])[:, 0:128] for i in range(2)]
    ps_tr = [pst("ps_tr%d" % i, [128, 1024], BF16)[:, 0:128] for i in range(2)]
    ps_o = [pst("ps_o%d" % i, [128, 4, 128])[:, 0:2, :] for i in range(2)]
    ps_ds = pst("ps_ds", [128, 512])[:, 0:256]
    ps_gs = pst("ps_gs", [128, 512])
    GV_v = GV.rearrange("(n p) e -> p n e", p=128)
    YGT_v = YGT.rearrange("(c p) n -> p c n", p=128)
    cc_ = [0]
    for h in range(4):
        DMA("sp", qT[:], GQT[h * 128:(h + 1) * 128, :], dram_deps("GQT", 0, HALF, TW2), ["g_qT"])
        DMA("sp", kT[:], GKT[h * 128:(h + 1) * 128, :], dram_deps("GKT", 0, CTX, CTX) + dram_deps("GKT", CTX, TOK, TW2), ["g_kT"])
        vdeps = [("GV", 0, cb) for cb in range(2)] + [("GV", c0, cb) for c0 in range(CTX, TOK, TW2) for cb in range(2)]
        for n0 in range(0, NCK, 16):
            n1 = min(NCK, n0 + 16)
            DMA("sp", vt[:, n0:n1, :], GV_v[:, n0:n1, h * 256:(h + 1) * 256], vdeps, [("g_v", n0)])
        vres = [("g_v", n0) for n0 in range(0, NCK, 16)]
        for d_ in range(2):
            MEMSET("dve", Pst[d_][:, 0:1], 0.0, [("g_P", d_)])
            NT_ = (CTX + HALF) if d_ == 0 else TOK
            for n0 in range(0, NT_, 512):
                n1 = min(NT_, n0 + 512)
                s_ = spl[(n0 // 512) % 2]
                spdeps = [("SPT", d_, c0) for c0 in ([0] + list(range(CTX, TOK, TW2)))]
                DMA("sp", s_[:, :n1 - n0], SPT[d_, h * 128:(h + 1) * 128, n0:n1], spdeps, [("g_sp", (n0 // 512) % 2)])
                P.op("dve", (lambda d_=d_, n0=n0, n1=n1, s_=s_: (lambda e: e.tensor_tensor_scan(
                    out=Pst[d_][:, n0 + 1:n1 + 1], data0=ones_f[:, :n1 - n0], data1=s_[:, :n1 - n0],
                    initial=Pst[d_][:, n0:n0 + 1], op0=ALU.mult, op1=ALU.add)))(),
                    [("g_sp", (n0 // 512) % 2), "ones_f", ("g_P", d_)], [("g_P", d_)])

        def chunk(d_, ck, full, first_out):
            i = cc_[0] % 2
            cc_[0] += 1
            a = ck * 128
            Pd = Pst[d_]
            bc = bcol[cc_[0] % 4]
            bres = ("g_bcol", cc_[0] % 4)
            if d_ == 0:
                sgn = -1.0 / 16.0
                pcols = Pd[:, a + 1:a + 129]
                bsrc = Pd[:, a:a + 1]
                elcol = 127
            else:
                sgn = 1.0 / 16.0
                pcols = Pd[:, a:a + 128]
                bsrc = Pd[:, a + 128:a + 129]
                elcol = 0
            TS("dve", bc[:, 0:1], bsrc, -sgn, None, ALU.mult, None, [("g_P", d_)], [bres])
            TS("dve", bc[:, 1:2], bsrc, sgn, None, ALU.mult, None, [("g_P", d_)], [bres])
            ACT(E1[i][:], pcols, AF.Exp, [("g_P", d_), bres], [("g_E1", i)], bias=bc[:, 0:1], scale=sgn)
            ACT(E2[i][:], pcols, AF.Exp, [("g_P", d_), bres], [("g_E2", i)], bias=bc[:, 1:2], scale=-sgn)
            TT("dve", keT[i][:], kT[:, a:a + 128], E2[i][:], ALU.mult, ["g_kT", ("g_E2", i)], [("g_keT", i)])
            TR(ps_tr[i][:], keT[i][:], id_b[:], [("g_keT", i), "id_b"], [("ps_tr", i)])
            CP("act", ket[i][:], ps_tr[i][:], [("ps_tr", i)], [("g_ket", i)])
            if full:
                lo = a - CTX
                TT("dve", qe[i][:], qT[:, lo:lo + 128], E1[i][:], ALU.mult, ["g_qT", ("g_E1", i)], [("g_qe", i)])
                MM(ps_at[i][:], keT[i][:], qe[i][:], True, True, [("g_keT", i), ("g_qe", i)], [("ps_at", i)])
                TT("dve", atb[i][:], ps_at[i][:], mask_f[:, d_, :], ALU.mult, [("ps_at", i), "mask_f"], [("g_atb", i)])
                for ec in range(2):
                    MM(ps_o[i][:, ec, :], Sb[:, ec * 128:(ec + 1) * 128], qe[i][:], True, False,
                       ["g_Sb", ("g_qe", i)], [("ps_o", i)])
                    MM(ps_o[i][:, ec, :], vt[:, ck, ec * 128:(ec + 1) * 128], atb[i][:], False, True,
                       vres + [("g_atb", i)], [("ps_o", i)])
                if first_out:
                    CP("act", oT[:, :, lo:lo + 128], ps_o[i][:], [("ps_o", i)], [("g_oT", lo)])
                else:
                    TT("dve", oT[:, :, lo:lo + 128], oT[:, :, lo:lo + 128], ps_o[i][:], ALU.add,
                       [("ps_o", i), ("g_oT", lo)], [("g_oT", lo)])
            MM(ps_ds[:], ket[i][:], vt[:, ck, :], True, True, [("g_ket", i)] + vres, ["ps_ds"])
            TT("dve", Stmp[:], S[:], ps_ds[:], ALU.add, ["g_S", "ps_ds"], ["g_Stmp"])
            TS("dve", S[:], Stmp[:], E1[i][:, elcol:elcol + 1], None, ALU.mult, None, ["g_Stmp", ("g_E1", i)], ["g_S"])
            P.op("act", (lambda i=i, elcol=elcol: (lambda e: e.activation(out=Sb[:], in_=Stmp[:], func=AF.Copy,
                                                                         scale=E1[i][:, elcol:elcol + 1])))(),
                 ["g_Stmp", ("g_E1", i)], ["g_Sb"])

        MEMSET("dve", S[:], 0.0, ["g_S"])
        MEMSET("dve", Sb[:], 0.0, ["g_Sb"])
        for ck in range(CTX // 128):
            chunk(0, ck, False, False)
        for ck in range(CTX // 128, CTX // 128 + NOWN):
            chunk(0, ck, True, True)
        MEMSET("dve", S[:], 0.0, ["g_S"])
        MEMSET("dve", Sb[:], 0.0, ["g_Sb"])
        for ck in range(CTX // 128 - 1, -1, -1):
            chunk(1, ck, False, False)
        for ck in range(NCK - 1, CTX // 128 + NOWN - 1, -1):
            chunk(1, ck, False, False)
        for ck in range(CTX // 128 + NOWN - 1, CTX // 128 - 1, -1):
            chunk(1, ck, True, False)
        for q0 in range(0, HALF, 512):
            k = 0
            DMA("sp", ggt[k][:], GGT_v[:, 2 * h:2 * h + 2, q0:q0 + 512], dram_deps("GGT", q0, q0 + 512, TW2), [("g_ggt", k)])
            ores = [("g_oT", lo) for lo in range(q0, q0 + 512, 128)]
            for ec in range(2):
                ACT(gsq[ec][:], oT[:, ec, q0:q0 + 512], AF.Square, ores, [("g_sq", ec)])
                MM(ps_gs[:], ones_f[:, 0:128], gsq[ec][:], ec == 0, ec == 1, [("g_sq", ec), "ones_f"], ["ps_gs"])
            ACT(grs[:], ps_gs[:], AF.Sqrt, ["ps_gs", "eps_t"], ["g_rs"], bias=eps_t[:, 0:1], scale=1.0 / 256)
            RECIP(grs[:], grs[:], ["g_rs"], ["g_rs"])
            for ec in range(2):
                STT(gtm[ec][:], oT[:, ec, q0:q0 + 512], gng[:, 2 * h + ec:2 * h + ec + 1], grs[:], ALU.mult, ALU.mult,
                    ores + ["gng", "g_rs"], [("g_tm", ec)])
                TT("pool", ygs[k][:, ec, :], gtm[ec][:], ggt[k][:, ec, :], ALU.mult, [("g_tm", ec), ("g_ggt", k)], [("g_ygs", k, ec)])
            DMA("sp", YGT_v[:, 2 * h:2 * h + 2, q0:q0 + 512], ygs[k][:], [("g_ygs", k, 0), ("g_ygs", k, 1)], [("YGT", h, q0)])
    es.close()

    if stop_after < 4:
        return finish()
    P.barrier()
    es = ExitStack()
    QT_ = 512 if HALF >= 512 else HALF
    dq = [sb("d_q%d" % i, [128, HALF], BF16) for i in range(2)]
    dk = [sb("d_k%d" % i, [128, TOK], BF16) for i in range(2)]
    dv = [sb("d_v%d" % i, [128, NCK, 128], BF16) for i in range(2)]
    NPB = 4
    pb = [sb("d_p%d" % i, [128, 2, QT_], BF16) for i in range(NPB)]
    acc = sb("d_acc", [128, 2, QT_], F32)
    rz = [sb("d_rz%d" % i, [128, QT_], F32) for i in range(2)]
    t12 = [sb("d_t%d" % i, [128, QT_], F32) for i in range(2)]
    od = sb("d_o", [128, QT_], F32)
    dsq = sb("d_sq", [128, QT_], F32)
    drs = sb("d_rs", [128, QT_], F32)
    yds = [sb("d_yds%d" % i, [128, QT_], BF16) for i in range(2)]
    dgn = sb("d_gn", [128, 8], F32)
    TS("dve", dgn[:], dng[:], 0.8, None, ALU.mult, None, ["dng"], ["d_gn"])
    ps_s = [pst("ps_s%d" % i, [128, 2, QT_]) for i in range(2)]
    ps_av = [pst("ps_av%d" % c, [128, QT_]) for c in range(2)]
    ps_z = [pst("ps_z%d" % c, [128, QT_]) for c in range(2)]
    DV_v = DV.rearrange("(n p) e -> p n e", p=128)
    kdeps = dram_deps("DKT", 0, CTX, CTX) + dram_deps("DKT", CTX, TOK, TW2)
    vdeps_d = [("DV", 0, cb) for cb in range(2)] + [("DV", c0, cb) for c0 in range(CTX, TOK, TW2) for cb in range(2)]
    it = [0]
    for h in range(8):
        hb_ = h % 2
        DMA("sp", dq[hb_][:], DQT[h * 128:(h + 1) * 128, :], dram_deps("DQT", 0, HALF, TW2), [("d_q", hb_)])
        DMA("sp", dk[hb_][:], DKT[h * 128:(h + 1) * 128, :], kdeps, [("d_k", hb_)])
        for n0 in range(0, NCK, 16):
            n1 = min(NCK, n0 + 16)
            DMA("sp", dv[hb_][:, n0:n1, :], DV_v[:, n0:n1, h * 128:(h + 1) * 128], vdeps_d, [("d_v", hb_, n0)])
        vres = [("d_v", hb_, n0) for n0 in range(0, NCK, 16)]
        for q0 in range(0, HALF, QT_):
            def scores(kc):
                sbuf_i = kc % 2
                for c in range(2):
                    MM(ps_s[sbuf_i][:, c, :], dk[hb_][64 * c:64 * c + 64, kc * 128:(kc + 1) * 128],
                       dq[hb_][64 * c:64 * c + 64, q0:q0 + QT_], True, True,
                       [("d_k", hb_), ("d_q", hb_)], [("ps_s", sbuf_i)])

            MEMSET("dve", acc[:, 0, :], 0.0, [("d_acc", 0)])
            scores(0)
            for kc in range(NCK):
                sbuf_i = kc % 2
                pi = kc % NPB
                P.op("act", (lambda sbuf_i=sbuf_i, pi=pi: (lambda e: e.activation(
                    out=pb[pi][:], in_=ps_s[sbuf_i][:], func=AF.Exp, scale=0.125)))(),
                    [("ps_s", sbuf_i)], [("d_p", pi)])
                if kc + 1 < NCK:
                    scores(kc + 1)
                for c in range(2):
                    MM(ps_av[c][:], dv[hb_][:, kc, :], pb[pi][:, c, :], kc == 0, kc == NCK - 1,
                       vres + [("d_p", pi)], [("ps_av", c)])
                MM(ps_z[1][:], ones_b[:], pb[pi][:, 1, :], kc == 0, kc == NCK - 1,
                   ["ones_b", ("d_p", pi)], [("ps_z", 1)])
                TT("dve", acc[:, 0, :], acc[:, 0, :], pb[pi][:, 0, :], ALU.add, [("d_acc", 0), ("d_p", pi)], [("d_acc", 0)])
            MM(ps_z[0][:], ones_f[:, 0:128], acc[:, 0, :], True, True, ["ones_f", ("d_acc", 0)], [("ps_z", 0)])
            for c in range(2):
                RECIP(rz[c][:], ps_z[c][:], [("ps_z", c)], [("d_rz", c)])
                TT("dve", t12[c][:], ps_av[c][:], rz[c][:], ALU.mult, [("ps_av", c), ("d_rz", c)], [("d_t", c)])
            STT(od[:], t12[1][:], neglam, t12[0][:], ALU.mult, ALU.add, [("d_t", 0), ("d_t", 1), "lam_w"], ["d_o"])
            ACT(dsq[:], od[:], AF.Square, ["d_o"], ["d_sq"])
            MM(ps_z[0][:], ones_f[:, 0:128], dsq[:], True, True, ["d_sq", "ones_f"], [("ps_z", 0)])
            ACT(drs[:], ps_z[0][:], AF.Sqrt, [("ps_z", 0), "eps_t"], ["d_rs"], bias=eps_t[:, 0:1], scale=1.0 / 128)
            RECIP(drs[:], drs[:], ["d_rs"], ["d_rs"])
            k = it[0] % 2
            it[0] += 1
            STT(yds[k][:], od[:], dgn[:, h:h + 1], drs[:], ALU.mult, ALU.mult, ["d_o", "d_gn", "d_rs"], [("d_yds", k)])
            DMA("sp", YDT[h * 128:(h + 1) * 128, q0:q0 + QT_], yds[k][:], [("d_yds", k)], [("YDT", h, q0)])
    es.close()

    if stop_after < 5:
        return finish()
    P.barrier()
    es = ExitStack()
    TW5 = 512 if HALF >= 512 else HALF
    wbg = sb("m_wbg", [128, 8, D], BF16)
    wbd = sb("m_wbd", [128, 8, D], BF16)
    wo = sb("m_wo", [128, 8, D], BF16)
    load_w_bf16(wbg, wbg_d.rearrange("(kc p) n -> p kc n", p=128), 8, D, "m_wbg")
    load_w_bf16(wbd, wbd_d.rearrange("(kc p) n -> p kc n", p=128), 8, D, "m_wbd")
    load_w_bf16(wo, wo_d.rearrange("(kc p) n -> p kc n", p=128), 8, D, "m_wo")
    ygt = [sb("m_yg%d" % i, [128, 8, TW5], BF16) for i in range(2)]
    ydt = [sb("m_yd%d" % i, [128, 8, TW5], BF16) for i in range(2)]
    mgt = [sb("m_mg%d" % i, [128, 16, TW5], BF16) for i in range(2)]
    x1t = [sb("m_x1%d" % i, [128, 8, TW5], F32) for i in range(2)]
    zb = sb("m_z", [128, 8, TW5], BF16)
    z1 = [sb("m_z1_%d" % i, [128, TW5], F32) for i in range(2)]
    z2 = [sb("m_z2_%d" % i, [128, TW5], F32) for i in range(2)]
    ps_a = [pst("ps_a%d" % i, [128, TW5]) for i in range(2)]
    ps_b = [pst("ps_b%d" % i, [128, TW5]) for i in range(2)]
    ps_w = [pst("ps_w%d" % i, [128, TW5]) for i in range(2)]
    YDT_v = YDT.rearrange("(c p) n -> p c n", p=128)
    X2T_v = X2T.rearrange("(c p) n -> p c n", p=128)
    for t, q0 in enumerate(range(0, HALF, TW5)):
        k = t % 2
        ygd = [("YGT", hh, qq) for hh in range(4) for qq in range(q0 - q0 % 512, q0 + TW5, 512)]
        ydd = [("YDT", hh, qq) for hh in range(8) for qq in range(q0 - q0 % QT_, q0 + TW5, QT_)]
        DMA("sp", ygt[k][:], YGT_v[:, :, q0:q0 + TW5], ygd, [("m_yg", k)])
        DMA("sp", ydt[k][:], YDT_v[:, :, q0:q0 + TW5], ydd, [("m_yd", k)])
        DMA("sp", mgt[k][:], MGT_v[:, :, q0:q0 + TW5], dram_deps("MGT", q0, q0 + TW5, TW2), [("m_mg", k)])
        DMA("sp", x1t[k][:], X1T_v[:, :, q0:q0 + TW5], dram_deps("X1T", q0, q0 + TW5, TW1), [("m_x1", k, c) for c in range(8)])
        for c in range(8):
            i = c % 2
            for kc in range(8):
                MM(ps_a[i][:], wbg[:, kc, c * 128:(c + 1) * 128], ygt[k][:, kc, :], kc == 0, kc == 7,
                   [("m_wbg", kc), ("m_yg", k)], [("ps_a", i)])
            for kc in range(8):
                MM(ps_b[i][:], wbd[:, kc, c * 128:(c + 1) * 128], ydt[k][:, kc, :], kc == 0, kc == 7,
                   [("m_wbd", kc), ("m_yd", k)], [("ps_b", i)])
            TT("dve", z1[i][:], ps_a[i][:], mgt[k][:, c, :], ALU.mult, [("ps_a", i), ("m_mg", k)], [("m_z1", i)])
            TT("dve", z2[i][:], ps_b[i][:], mgt[k][:, 8 + c, :], ALU.mult, [("ps_b", i), ("m_mg", k)], [("m_z2", i)])
            TT("pool", zb[:, c, :], z1[i][:], z2[i][:], ALU.add, [("m_z1", i), ("m_z2", i)], [("m_z", c)])
        for c in range(8):
            i = c % 2
            for kc in range(8):
                MM(ps_w[i][:], wo[:, kc, c * 128:(c + 1) * 128], zb[:, kc, :], kc == 0, kc == 7,
                   [("m_wo", kc), ("m_z", kc)], [("ps_w", i)])
            STT(x1t[k][:, c, :], ps_w[i][:], gte[:, 1, c, 0:1], x1t[k][:, c, :], ALU.mult, ALU.add,
                [("ps_w", i), "gte", ("m_x1", k, c)], [("m_x1", k, c)])
        DMA("sp", X2T_v[:, :, q0:q0 + TW5], x1t[k][:], [("m_x1", k, c) for c in range(8)], [("X2T", q0)])
    es.close()

    if stop_after < 6:
        return finish()
    P.barrier()
    tiles6 = [(i * TW1, TW1, 0, i * TW1) for i in range(HALF // TW1)]
    outT_v = outT.rearrange("(c p) n -> p c n", p=128)

    def post6(t, x_, W_, sq, ps_stat, rstd2, tmp, h2, tag):
        c0 = tiles6[t][0]
        rms_rstd(tag, tag + "brstd", lambda c: x_[:, c, :W_], 8, W_, sq, ps_stat, rstd2, 1.0 / D,
                 lambda c: [(tag + "xt", t % 2, c)])
        for hh in range(2):
            for c4 in range(4):
                c = hh * 4 + c4
                STT(h2[:, c4, :W_], x_[:, c, :W_], fg[:, c:c + 1], rstd2[:, :W_], ALU.mult, ALU.mult,
                    [(tag + "xt", t % 2, c), "fg", tag + "brstd"], [(tag + "h2", c4)])
            DMA("sp", outT_v[:, hh * 4:hh * 4 + 4, c0:c0 + W_], h2[:, :, :W_], [(tag + "h2", c4) for c4 in range(4)], [("outT", c0, hh)])

    ffn_phase("f2", f2w1, f2w3, f2w2, tiles6, TW1, X2T, 2, post6, lambda t: [("X2T", (tiles6[t][0] // TW5) * TW5)])

    P.emit()
    cons.close()
    return nc


def _const_tables(SEQ):
    half, quarter = 32, 16
    inv = (10000.0 ** (-np.arange(quarter, dtype=np.float32) / quarter)).astype(np.float32)
    t = np.arange(SEQ, dtype=np.int32)
    row = (t // GRID_W).astype(np.float32)
    col = (t % GRID_W).astype(np.float32)
    cosT = np.zeros((128, SEQ), np.float32)
    sinT = np.zeros((128, SEQ), np.float32)
    perm = np.zeros((128, 128), np.float32)
    for p in range(128):
        d = p % 64
        pos = row if d < half else col
        j = d % quarter
        ang = (pos * inv[j]).astype(np.float32)
        cosT[p] = np.cos(ang).astype(np.float32)
        sn = np.sin(ang).astype(np.float32)
        if (d % half) < quarter:
            sinT[p] = -sn
            perm[p + 16, p] = 1.0
        else:
            sinT[p] = sn
            perm[p - 16, p] = 1.0
    s = np.arange(128)
    masks = np.zeros((128, 2, 128), np.float32)
    masks[:, 0, :] = (s[:, None] <= s[None, :])
    masks[:, 1, :] = (s[:, None] >= s[None, :])
    return cosT, sinT, perm, masks


def _prep(inputs, SEQ):
    f = lambda a: np.ascontiguousarray(np.asarray(a, dtype=np.float32))
    x = f(inputs["x"]); c = f(inputs["c"]); ctx = f(inputs["ctx"]); c_ctx = f(inputs["c_ctx"])
    B = x.shape[0]
    cosT, sinT, perm, masks = _const_tables(SEQ)
    w_in0 = f(inputs["w_in"][0])
    w_in1 = w_in0.copy()
    w_in1[:, O_LR:O_LR + 16] = w_in0[:, O_LR + 16:O_LR + 32]
    w_in1[:, O_LR + 16:O_LR + 32] = w_in0[:, O_LR:O_LR + 16]
    gw2 = f(inputs["gla_gate_w2"][0]); gb = f(inputs["gla_gate_b"][0])
    shared = {
        "w_ada": f(inputs["w_ada"][0]),
        "bT_ada": f(inputs["b_ada"][0].reshape(72, 128).T),
        "ngT": f(np.stack([inputs["ffn1_norm"][0], inputs["mix_norm"][0], inputs["ffn2_norm"][0]]).reshape(3, 8, 128).transpose(2, 0, 1)),
        "fgT": f(np.asarray(inputs["final_norm"]).reshape(8, 128).T),
        "gngT": f(np.asarray(inputs["gla_out_norm"][0]).reshape(8, 128).T),
        "dngT": f(np.asarray(inputs["diff_out_norm"][0]).reshape(8, 128).T),
        "f1w1": f(inputs["ffn1_w1"][0]), "f1w3": f(inputs["ffn1_w3"][0]), "f1w2": f(inputs["ffn1_w2"][0]),
        "f2w1": f(inputs["ffn2_w1"][0]), "f2w3": f(inputs["ffn2_w3"][0]), "f2w2": f(inputs["ffn2_w2"][0]),
        "lamT": f(np.tile(np.asarray(inputs["diff_lambda"][0]).reshape(1, 256), (128, 1))),
        "permM": perm, "masks": masks, "ident": np.eye(128, dtype=np.float32),
        "wbg": f(inputs["w_branch_gla"][0]), "wbd": f(inputs["w_branch_diff"][0]), "wo": f(inputs["w_out"][0]),
    }
    in_maps = []
    for core in range(2 * B):
        b, hf = core // 2, core % 2
        xs = x[b]; cs_ = ctx[b]
        if hf == 1:
            xs = xs[::-1]; cs_ = cs_[::-1]
        xT = np.ascontiguousarray(np.concatenate([cs_, xs], axis=0).T)
        order = (0, 1) if hf == 0 else (1, 0)
        m = dict(shared)
        m["xT"] = xT
        m["cT"] = f(np.stack([c[b].reshape(8, 128).T, c_ctx.reshape(8, 128).T], axis=-1))
        m["w_in"] = w_in0 if hf == 0 else w_in1
        m["gw2"] = f(np.stack([gw2[order[0]], gw2[order[1]]]))
        m["gbT"] = f(np.stack([gb[order[0]].reshape(4, 128).T, gb[order[1]].reshape(4, 128).T], axis=1))
        m["cosT"] = cosT if hf == 0 else np.ascontiguousarray(cosT[:, ::-1])
        m["sinT"] = sinT if hf == 0 else np.ascontiguousarray(sinT[:, ::-1])
        in_maps.append(m)
    return in_maps


def _run(inputs, SEQ, dbg=False, stop_after=99):
    B = np.asarray(inputs["x"]).shape[0]
    HALF = SEQ // 2
    in_maps = _prep(inputs, SEQ)
    nc = build_nc(SEQ, dbg=dbg, stop_after=stop_after)
    res = run_bass_kernel_spmd(nc, in_maps, core_ids=list(range(2 * B)))
    out = np.empty((B, SEQ, D), np.float32)
    for core in range(2 * B):
        b, hf = core // 2, core % 2
        o = res.results[core]["outT"].T
        if hf == 0:
            out[b, :HALF] = o
        else:
            out[b, HALF:] = o[::-1]
    return out, res


def kernel(**inputs):
    SEQ = np.asarray(inputs["x"]).shape[1]
    out, _ = _run(inputs, SEQ)
    return out
```

```python
import numpy as np
import concourse.bass as bass
import concourse.mybir as mybir
from concourse.bass_utils import run_bass_kernel_spmd

F32 = mybir.dt.float32
BF16 = mybir.dt.bfloat16
AF = mybir.ActivationFunctionType
ALU = mybir.AluOpType

D = 1024
NC = 8
FF = 2816
NF = 22
CTX = 256
EPS = 1e-6
GRID_W = 64
IN_COLS = 8224
O_GQ, O_GK, O_GV, O_GG, O_LR, O_DQ, O_DK, O_DV, O_MG = 0, 512, 1024, 2048, 3072, 3104, 4128, 5152, 6176
N_DMA_SEMS = 24
SKIP = set()


class Prog:
    def __init__(self, nc):
        self.nc = nc
        self.ops = []
        self.lastw = {}
        self.readers = {}
        self.ndma = 0
        self.nsw = 0
        self.dma_prev = {}
        self._guards = []
        self.barriers = []
        self.pending_bar = {}
        self.pool_dmas = []

    def barrier(self):
        ops = self.ops
        prior = set()
        seen_e = set()
        seen_s = set()
        for i in range(len(ops) - 1, -1, -1):
            o = ops[i]
            if o[3]:
                if o[4] not in seen_s:
                    seen_s.add(o[4]); prior.add(i)
            elif o[0] not in seen_e:
                seen_e.add(o[0]); prior.add(i)
        prior = sorted(prior)
        last = None
        for j in range(0, max(1, len(prior)), 3):
            idx = len(ops)
            ops.append(["sp", (lambda e: e.nop()), set(prior[j:j + 3]), False, None])
            last = idx
        self.pending_bar = {eng: last for eng in ("pe", "act", "dve", "pool")}

    def _deps(self, reads, writes):
        d = set()
        for r in reads:
            w = self.lastw.get(r)
            if w is not None:
                d.add(w)
        for w_ in writes:
            w = self.lastw.get(w_)
            if w is not None:
                d.add(w)
            for r in self.readers.get(w_, ()):
                d.add(r)
        return d

    def _commit(self, idx, reads, writes):
        for w_ in writes:
            self.lastw[w_] = idx
            self.readers[w_] = []
        for r in reads:
            if r not in writes:
                self.readers.setdefault(r, []).append(idx)

    def op(self, eng, fn, reads=(), writes=()):
        idx = len(self.ops)
        d = self._deps(reads, writes)
        if eng in self.pending_bar:
            d.add(self.pending_bar.pop(eng))
        self.ops.append([eng, fn, d, False, None])
        self._commit(idx, reads, writes)
        return idx

    def dma(self, eng, fn, reads=(), writes=()):
        idx = len(self.ops)
        d = self._deps(reads, writes)
        if eng in self.pending_bar:
            d.add(self.pending_bar.pop(eng))
        if eng == "pool":
            s = "sw%d" % (self.nsw % 8)
            self.nsw += 1
        else:
            s = "hw%d" % (self.ndma % N_DMA_SEMS)
            self.ndma += 1
        if s in self.dma_prev:
            d.add(self.dma_prev[s])
        self.dma_prev[s] = idx
        if eng == "pool":
            if len(self.pool_dmas) >= 12:
                d.add(self.pool_dmas[-12])
            self.pool_dmas.append(idx)
        self.ops.append([eng, fn, d, True, s])
        self._commit(idx, reads, writes)
        return idx

    def emit(self, final_wait_all_dma=True):
        nc = self.nc
        ops = self.ops
        for p in self.barriers:
            prior = set()
            seen_e = set()
            seen_s = set()
            for i in range(p - 1, -1, -1):
                o = ops[i]
                if o[3]:
                    if o[4] not in seen_s:
                        seen_s.add(o[4]); prior.add(i)
                elif o[0] not in seen_e:
                    seen_e.add(o[0]); prior.add(i)
                if len(seen_e) >= 4 and len(seen_s) >= N_DMA_SEMS + 8:
                    break
            done_e = set()
            for i in range(p, len(ops)):
                if ops[i][0] not in done_e:
                    done_e.add(ops[i][0])
                    ops[i][2] |= prior
                if len(done_e) >= 5:
                    break
        need = [False] * len(ops)
        for i, (eng, fn, deps, is_dma, s) in enumerate(ops):
            for dp in deps:
                pe = ops[dp]
                if pe[0] == "pe" and eng == "pe" and not pe[3] and not is_dma:
                    continue
                need[dp] = True
        for i, o in enumerate(ops):
            if o[3]:
                need[i] = True
        sems = {}

        def getsem(name):
            if name not in sems:
                g = nc.semaphore(name)
                sems[name] = g.__enter__()
                self._guards.append(g)
            return sems[name]

        cnt = {}
        tok = [None] * len(ops)
        for i, (eng, fn, deps, is_dma, s) in enumerate(ops):
            if not need[i]:
                continue
            name = ("dma" + s) if is_dma else ("e_" + eng)
            getsem(name)
            cnt[name] = cnt.get(name, 0) + (16 if is_dma else 1)
            tok[i] = (name, cnt[name])
        per = {k: [] for k in ("pe", "act", "dve", "pool", "sp")}
        for i, (eng, fn, deps, is_dma, s) in enumerate(ops):
            waits = []
            for dp in deps:
                pe = ops[dp]
                if pe[0] == "pe" and eng == "pe" and not pe[3] and not is_dma:
                    continue
                waits.append(tok[dp])
            per[eng].append((fn, waits, tok[i], 16 if is_dma else 1))
        final = [(n, c) for n, c in cnt.items() if n.startswith("dma")]
        with nc.Block() as block:
            def run(e, lst, fin=False):
                seen = {}
                for fn, waits, t, amt in lst:
                    for (sn, v) in waits:
                        if seen.get(sn, -1) >= v:
                            continue
                        seen[sn] = v
                        e.wait_ge(sems[sn], v)
                    ins = fn(e)
                    if t is not None:
                        ins.then_inc(sems[t[0]], amt)
                if fin:
                    for j, (sn, v) in enumerate(final):
                        e.wait_ge(sems[sn], v)
                        if j % 3 == 2:
                            e.nop()

            @block.tensor
            def _(e): run(e, per["pe"])

            @block.scalar
            def _(e): run(e, per["act"])

            @block.vector
            def _(e): run(e, per["dve"])

            @block.gpsimd
            def _(e): run(e, per["pool"])

            @block.sync
            def _(e): run(e, per["sp"], fin=True)


def build_nc(SEQ, dbg=False, stop_after=99):
    HALF = SEQ // 2
    TOK = CTX + SEQ
    OWN_END = CTX + HALF
    nc = bass.Bass("TRN2", target_bir_lowering=False)
    P = Prog(nc)

    def din(name, shape, dt=F32):
        return nc.dram_tensor(name, list(shape), dt, kind="ExternalInput").ap()

    def dscr(name, shape, dt):
        return nc.dram_tensor(name, list(shape), dt, kind=("ExternalOutput" if dbg else "Internal")).ap()

    xT = din("xT", [D, TOK])
    cT = din("cT", [128, 8, 2])
    w_ada = din("w_ada", [D, 9 * D])
    bT_ada = din("bT_ada", [128, 72])
    ngT = din("ngT", [128, 3, 8])
    fgT = din("fgT", [128, 8])
    gngT = din("gngT", [128, 8])
    dngT = din("dngT", [128, 8])
    f1w1 = din("f1w1", [D, FF]); f1w3 = din("f1w3", [D, FF]); f1w2 = din("f1w2", [FF, D])
    f2w1 = din("f2w1", [D, FF]); f2w3 = din("f2w3", [D, FF]); f2w2 = din("f2w2", [FF, D])
    w_in = din("w_in", [D, IN_COLS])
    gw2 = din("gw2", [2, 16, 512])
    gbT = din("gbT", [128, 2, 4])
    lamT = din("lamT", [128, 256])
    cosT = din("cosT", [128, SEQ])
    sinT = din("sinT", [128, SEQ])
    permM = din("permM", [128, 128])
    masks = din("masks", [128, 2, 128])
    ident = din("ident", [128, 128])
    wbg_d = din("wbg", [D, D]); wbd_d = din("wbd", [D, D]); wo_d = din("wo", [D, D])
    outT = nc.dram_tensor("outT", [D, HALF], F32, kind="ExternalOutput").ap()

    X1T = dscr("X1T", [D, HALF], F32)
    H2T = dscr("H2T", [D, TOK], BF16)
    GQT = dscr("GQT", [512, HALF], BF16)
    GKT = dscr("GKT", [512, TOK], BF16)
    GV = dscr("GV", [TOK, D], BF16)
    GGT = dscr("GGT", [D, HALF], BF16)
    SPT = dscr("SPT", [2, 512, TOK], F32)
    DQT = dscr("DQT", [D, HALF], BF16)
    DKT = dscr("DKT", [D, TOK], BF16)
    DV = dscr("DV", [TOK, D], BF16)
    MGT = dscr("MGT", [2 * D, HALF], BF16)
    YGT = dscr("YGT", [D, HALF], BF16)
    YDT = dscr("YDT", [D, HALF], BF16)
    X2T = dscr("X2T", [D, HALF], F32)

    def MM(out, lhsT, rhs, start, stop, R, W):
        P.op("pe", lambda e: e.matmul(out, lhsT, rhs, start=start, stop=stop), R, W)

    def TR(out, in_, idn, R, W):
        P.op("pe", lambda e: e.transpose(out, in_, idn), R, W)

    def ACT(out, in_, func, R, W, bias=None, scale=None):
        kw = {}
        if bias is not None:
            kw["bias"] = bias
        if scale is not None:
            kw["scale"] = scale
        P.op("act", lambda e: e.activation(out=out, in_=in_, func=func, **kw), R, W)

    def TS(eng, out, in0, s1, s2, op0, op1, R, W):
        if op1 is None:
            P.op(eng, lambda e: e.tensor_scalar(out=out, in0=in0, scalar1=s1, scalar2=None, op0=op0), R, W)
        else:
            P.op(eng, lambda e: e.tensor_scalar(out=out, in0=in0, scalar1=s1, scalar2=s2, op0=op0, op1=op1), R, W)

    def STT(out, in0, scalar, in1, op0, op1, R, W):
        P.op("dve", lambda e: e.scalar_tensor_tensor(out=out, in0=in0, scalar=scalar, in1=in1, op0=op0, op1=op1), R, W)

    def TT(eng, out, in0, in1, op, R, W):
        P.op(eng, lambda e: e.tensor_tensor(out=out, in0=in0, in1=in1, op=op), R, W)

    def RECIP(out, in_, R, W):
        P.op("dve", lambda e: e.reciprocal(out=out, in_=in_), R, W)

    def CP(eng, out, in_, R, W):
        if eng == "act":
            P.op("act", lambda e: e.activation(out=out, in_=in_, func=AF.Identity), R, W)
        else:
            P.op(eng, lambda e: e.tensor_copy(out=out, in_=in_), R, W)

    def MEMSET(eng, ap, v, W):
        P.op(eng, lambda e: e.memset(ap, v), (), W)

    def DMA(q, out, in_, R, W):
        P.dma(q, lambda e: e.dma_start(out=out, in_=in_), R, W)

    from contextlib import ExitStack
    es = ExitStack()

    def sb(name, shape, dt):
        return es.enter_context(nc.sbuf_tensor(name, list(shape), dt))

    def pst(name, shape, dt=F32):
        return es.enter_context(nc.psum_tensor(name, list(shape), dt))

    cons = ExitStack()

    def csb(name, shape, dt):
        return cons.enter_context(nc.sbuf_tensor(name, list(shape), dt))

    modt = csb("modt", [128, 72, 2], F32)
    gm = csb("gm", [128, 3, 8, 2], F32)
    shf = csb("shf", [128, 3, 8, 2], F32)
    gte = csb("gte", [128, 3, 8, 2], F32)
    ng = csb("ng", [128, 3, 8], F32)
    fg = csb("fg", [128, 8], F32)
    gng = csb("gng", [128, 8], F32)
    dng = csb("dng", [128, 8], F32)
    ones_f = csb("ones_f", [128, 512], F32)
    ones_b = csb("ones_b", [128, 128], BF16)
    id_b = csb("id_b", [128, 128], BF16)
    perm_b = csb("perm_b", [128, 128], BF16)
    mask_f = csb("mask_f", [128, 2, 128], F32)
    gw2s = csb("gw2s", [16, 2, 512], F32)
    gb = csb("gb", [128, 2, 4], F32)
    ngb = csb("ngb", [128, 2, 4], F32)
    lam_s = csb("lam_s", [128, 256], F32)
    lam_w = csb("lam_w", [128, 8], F32)
    eps_t = csb("eps_t", [128, 1], F32)

    DMA("sp", ng[:], ngT, (), ["ng"])
    DMA("sp", fg[:], fgT, (), ["fg"])
    DMA("sp", gng[:], gngT, (), ["gng"])
    DMA("sp", dng[:], dngT, (), ["dng"])
    DMA("sp", mask_f[:], masks, (), ["mask_f"])
    DMA("sp", gb[:], gbT, (), ["gb"])
    DMA("sp", lam_s[:], lamT, (), ["lam_s"])
    for d_ in range(2):
        DMA("sp", gw2s[:, d_, :], gw2[d_], (), [("gw2s", d_)])
    DMA("pool", id_b[:], ident, (), ["id_b"])
    DMA("pool", perm_b[:], permM, (), ["perm_b"])
    MEMSET("dve", ones_f[:], 1.0, ["ones_f"])
    MEMSET("dve", ones_b[:], 1.0, ["ones_b"])
    MEMSET("dve", eps_t[:], EPS, ["eps_t"])
    TS("dve", ngb[:], gb[:], -1.0, None, ALU.mult, None, ["gb"], ["ngb"])
    TT("dve", lam_s[:, 0:64], lam_s[:, 0:64], lam_s[:, 64:128], ALU.mult, ["lam_s"], ["lam_s"])
    TT("dve", lam_s[:, 128:192], lam_s[:, 128:192], lam_s[:, 192:256], ALU.mult, ["lam_s"], ["lam_s"])
    P.op("dve", lambda e: e.reduce_sum(out=lam_w[:, 0:1], in_=lam_s[:, 0:64], axis=mybir.AxisListType.X), ["lam_s"], ["lam_w"])
    P.op("dve", lambda e: e.reduce_sum(out=lam_w[:, 1:2], in_=lam_s[:, 128:192], axis=mybir.AxisListType.X), ["lam_s"], ["lam_w"])
    ACT(lam_w[:, 2:4], lam_w[:, 0:2], AF.Exp, ["lam_w"], ["lam_w"])
    TT("dve", lam_w[:, 4:5], lam_w[:, 3:4], lam_w[:, 2:3], ALU.subtract, ["lam_w"], ["lam_w"])
    TS("dve", lam_w[:, 4:5], lam_w[:, 4:5], -0.2, None, ALU.add, None, ["lam_w"], ["lam_w"])
    neglam = lam_w[:, 4:5]

    es = ExitStack()
    sc = sb("sc", [128, 8, 2], F32)
    bTa = sb("bTa", [128, 72], F32)
    wa = [sb("wa%d" % i, [128, 8, 1024], F32) for i in range(2)]
    ps_m = [pst("ps_m%d" % i, [128, 16]) for i in range(2)]
    DMA("sp", sc[:], cT, (), ["sc"])
    DMA("sp", bTa[:], bT_ada, (), ["bTa"])
    ACT(sc[:], sc[:], AF.Silu, ["sc"], ["sc"])
    w_ada_v = w_ada.rearrange("(kc p) n -> p kc n", p=128)
    for m in range(9):
        wb_ = wa[m % 2]
        for kc in range(8):
            DMA("sp", wb_[:, kc, :], w_ada_v[:, kc, m * 1024:(m + 1) * 1024], (), [("wa", m % 2, kc)])
        pm = ps_m[m % 2]
        for c in range(8):
            for kc in range(8):
                MM(pm[:, 2 * c:2 * c + 2], wb_[:, kc, c * 128:(c + 1) * 128], sc[:, kc, :], kc == 0, kc == 7,
                   [("wa", m % 2, kc), "sc"], [("ps_m", m % 2)])
        for j in range(2):
            TT("dve", modt[:, m * 8:(m + 1) * 8, j], pm[:, j:16:2], bTa[:, m * 8:(m + 1) * 8], ALU.add,
               [("ps_m", m % 2), "bTa"] + [("wa", m % 2, kc) for kc in range(8)], ["modt"])
    for i in range(3):
        for j in range(2):
            STT(gm[:, i, :, j], modt[:, (3 * i + 1) * 8:(3 * i + 2) * 8, j], 1.0, ng[:, i, :], ALU.add, ALU.mult,
                ["modt", "ng"], ["gm"])
            CP("dve", shf[:, i, :, j], modt[:, (3 * i) * 8:(3 * i + 1) * 8, j], ["modt"], ["shf"])
            TS("dve", gte[:, i, :, j], modt[:, (3 * i + 2) * 8:(3 * i + 3) * 8, j], (1.0 if i == 1 else 0.5), None,
               ALU.mult, None, ["modt"], ["gte"])
    es.close()

    def load_w_bf16(dst, src_v, nk, ncols, resname):
        step = 1024
        for kc in range(nk):
            for c0 in range(0, ncols, step):
                c1 = min(ncols, c0 + step)
                DMA("pool", dst[:, kc, c0:c1], src_v[:, kc, c0:c1], (), [(resname, kc)])

    def rms_rstd(key, rres, src_chunks, nch, W_, sq, ps_stat, rstd, inv_n, Rsrc):
        for c in range(nch):
            s_ = sq[c % 2]
            ACT(s_[:, :W_], src_chunks(c), AF.Square, Rsrc(c), [(key + "sq", c % 2)])
            MM(ps_stat[:, :W_], ones_f[:, 0:128], s_[:, :W_], c == 0, c == nch - 1,
               [(key + "sq", c % 2), "ones_f"], [key + "ps_stat"])
        ACT(rstd[:, :W_], ps_stat[:, :W_], AF.Sqrt, [key + "ps_stat", "eps_t"], [rres], bias=eps_t[:, 0:1], scale=inv_n)
        RECIP(rstd[:, :W_], rstd[:, :W_], [rres], [rres])

    def ffn_phase(tag, w1d, w3d, w2d, tiles, TW, src_T, li, post, src_deps):
        nonlocal es
        es = ExitStack()
        w1b = sb(tag + "w1b", [128, 8, FF], BF16)
        w3b = sb(tag + "w3b", [128, 8, FF], BF16)
        w2b = sb(tag + "w2b", [128, NF, D], BF16)
        load_w_bf16(w1b, w1d.rearrange("(kc p) n -> p kc n", p=128), 8, FF, tag + "w1b")
        load_w_bf16(w3b, w3d.rearrange("(kc p) n -> p kc n", p=128), 8, FF, tag + "w3b")
        load_w_bf16(w2b, w2d.rearrange("(kc p) n -> p kc n", p=128), NF, D, tag + "w2b")
        xt = [sb(tag + "xt%d" % i, [128, 8, TW], F32) for i in range(2)]
        hb = sb(tag + "hb", [128, 8, TW], BF16)
        ab = sb(tag + "ab", [128, NF, TW], BF16)
        sq = [sb(tag + "sq%d" % i, [128, TW], F32) for i in range(2)]
        rstd = sb(tag + "rstd", [128, TW], F32)
        rstd2 = sb(tag + "rstd2", [128, TW], F32)
        tmp = [sb(tag + "tmp%d" % i, [128, TW], F32) for i in range(2)]
        su = [sb(tag + "su%d" % i, [128, TW], F32) for i in range(2)]
        h2 = sb(tag + "h2", [128, 8, TW], BF16) if li == 0 else sb(tag + "h2", [128, 4, TW], F32)
        ps_u = [pst(tag + "ps_u%d" % i, [128, TW]) for i in range(2)]
        ps_v = [pst(tag + "ps_v%d" % i, [128, TW]) for i in range(2)]
        ps_y = [pst(tag + "ps_y%d" % i, [128, TW]) for i in range(2)]
        ps_stat = pst(tag + "ps_stat", [128, TW])
        src_v = src_T.rearrange("(c p) n -> p c n", p=128)

        def prologue(t):
            c0, W_, j, own = tiles[t]
            x_ = xt[t % 2]
            DMA("sp", x_[:, :, :W_], src_v[:, :, c0:c0 + W_], src_deps(t), [(tag + "xt", t % 2, c) for c in range(8)])
            rms_rstd(tag, tag + "arstd", lambda c: x_[:, c, :W_], 8, W_, sq, ps_stat, rstd, 1.0 / D,
                     lambda c: [(tag + "xt", t % 2, c)])
            for c in range(8):
                tm = tmp[c % 2]
                STT(tm[:, :W_], x_[:, c, :W_], gm[:, li, c, j:j + 1], rstd[:, :W_], ALU.mult, ALU.mult,
                    [(tag + "xt", t % 2, c), "gm", tag + "arstd"], [(tag + "tmp", c % 2)])
                ACT(hb[:, c, :W_], tm[:, :W_], AF.Identity, [(tag + "tmp", c % 2), "shf"], [(tag + "hb", c)],
                    bias=shf[:, li, c, j:j + 1])

        prologue(0)
        for t in range(len(tiles)):
            c0, W_, j, own = tiles[t]
            x_ = xt[t % 2]
            for f in range(NF):
                pu = ps_u[f % 2]; pv = ps_v[f % 2]
                for kc in range(8):
                    MM(pu[:, :W_], w1b[:, kc, f * 128:(f + 1) * 128], hb[:, kc, :W_], kc == 0, kc == 7,
                       [(tag + "w1b", kc), (tag + "hb", kc)], [(tag + "ps_u", f % 2)])
                for kc in range(8):
                    MM(pv[:, :W_], w3b[:, kc, f * 128:(f + 1) * 128], hb[:, kc, :W_], kc == 0, kc == 7,
                       [(tag + "w3b", kc), (tag + "hb", kc)], [(tag + "ps_v", f % 2)])
                s_ = su[f % 2]
                ACT(s_[:, :W_], pu[:, :W_], AF.Silu, [(tag + "ps_u", f % 2)], [(tag + "su", f % 2)])
                TT("dve", ab[:, f, :W_], s_[:, :W_], pv[:, :W_], ALU.mult,
                   [(tag + "su", f % 2), (tag + "ps_v", f % 2)], [(tag + "ab", f)])
            if t + 1 < len(tiles):
                prologue(t + 1)
            for c in range(8):
                py = ps_y[c % 2]
                for f in range(NF):
                    MM(py[:, :W_], w2b[:, f, c * 128:(c + 1) * 128], ab[:, f, :W_], f == 0, f == NF - 1,
                       [(tag + "w2b", f), (tag + "ab", f)], [(tag + "ps_y", c % 2)])
                STT(x_[:, c, :W_], py[:, :W_], gte[:, li, c, j:j + 1], x_[:, c, :W_], ALU.mult, ALU.add,
                    [(tag + "ps_y", c % 2), "gte", (tag + "xt", t % 2, c)], [(tag + "xt", t % 2, c)])
            post(t, x_, W_, sq, ps_stat, rstd2, tmp, h2, tag)
        es.close()

    def finish():
        P.emit()
        cons.close()
        return nc
    if stop_after < 1:
        return finish()
    P.barrier()
    TW1 = 256
    tiles1 = [(0, CTX, 1, None)]
    for i in range(SEQ // TW1):
        tiles1.append((CTX + i * TW1, TW1, 0, (i * TW1 if i * TW1 < HALF else None)))
    H2T_v = H2T.rearrange("(c p) n -> p c n", p=128)
    X1T_v = X1T.rearrange("(c p) n -> p c n", p=128)

    def post1(t, x_, W_, sq, ps_stat, rstd2, tmp, h2, tag):
        c0, _, j, own = tiles1[t]
        if own is not None:
            DMA("sp", X1T_v[:, :, own:own + W_], x_[:, :, :W_], [(tag + "xt", t % 2, c) for c in range(8)], [("X1T", own)])
        rms_rstd(tag, tag + "brstd", lambda c: x_[:, c, :W_], 8, W_, sq, ps_stat, rstd2, 1.0 / D,
                 lambda c: [(tag + "xt", t % 2, c)])
        for c in range(8):
            tm = tmp[c % 2]
            STT(tm[:, :W_], x_[:, c, :W_], gm[:, 1, c, j:j + 1], rstd2[:, :W_], ALU.mult, ALU.mult,
                [(tag + "xt", t % 2, c), "gm", tag + "brstd"], [(tag + "tmp", c % 2)])
            ACT(h2[:, c, :W_], tm[:, :W_], AF.Identity, [(tag + "tmp", c % 2), "shf"], [(tag + "h2", c)],
                bias=shf[:, 1, c, j:j + 1])
        DMA("sp", H2T_v[:, :, c0:c0 + W_], h2[:, :, :W_], [(tag + "h2", c) for c in range(8)], [("H2T", c0)])

    if "p1" not in SKIP:
        ffn_phase("f1", f1w1, f1w3, f1w2, tiles1, TW1, xT, 0, post1, lambda t: ())

    if stop_after < 2:
        return finish()
    P.barrier()
    TW2 = 256
    tiles2 = [(0, CTX, "ctx", None)]
    for i in range(SEQ // TW2):
        tiles2.append((CTX + i * TW2, TW2, ("own" if i * TW2 < HALF else "oth"), i * TW2))
    GKT_v = GKT.rearrange("(c p) n -> p c n", p=128)
    GQT_v = GQT.rearrange("(c p) n -> p c n", p=128)
    GGT_v = GGT.rearrange("(c p) n -> p c n", p=128)
    DQT_v = DQT.rearrange("(c p) n -> p c n", p=128)
    DKT_v = DKT.rearrange("(c p) n -> p c n", p=128)
    MGT_v = MGT.rearrange("(c p) n -> p c n", p=128)
    SPT_v = SPT.rearrange("d (c p) n -> d p c n", p=128)
    w_in_v = w_in.rearrange("(kc p) n -> p kc n", p=128)

    def proj_pass(pid):
        nonlocal es
        es = ExitStack()
        if pid == 0:
            segs = [(O_GK, 512), (O_GV, 1024), (O_LR, 32), (O_DK, 1024), (O_DV, 1024)]
        else:
            segs = [(O_GQ, 512), (O_GG, 1024), (O_DQ, 1024), (O_MG, 2048)]
        cmap = {}
        tot = 0
        for (s0, n) in segs:
            cmap[s0] = tot
            tot += n
        wtag = "winb%d" % pid
        winb = sb(wtag, [128, 8, tot], BF16)
        for (s0, n) in segs:
            if n == 32 and 'lr' in SKIP:
                continue
            for kc in range(8):
                for c0 in range(0, n, 1024):
                    c1 = min(n, c0 + 1024)
                    DMA("pool", winb[:, kc, cmap[s0] + c0:cmap[s0] + c1], w_in_v[:, kc, s0 + c0:s0 + c1], (), [(wtag, kc)])
        pt = "p2%d" % pid
        h2t = [sb(pt + "h2t%d" % i, [128, 8, TW2], BF16) for i in range(2)]
        ps_p = [pst(pt + "ps_p%d" % i, [128, 512]) for i in range(4)]
        pcount = [0]

        def proj_fm(h_, hi, col0, W_):
            k = pcount[0] % 4
            pcount[0] += 1
            pp = ps_p[k]
            for kc in range(8):
                MM(pp[:, :W_], winb[:, kc, col0:col0 + 128], h_[:, kc, :W_], kc == 0, kc == 7,
                   [(wtag, kc), (pt + "h2t", hi)], [(pt + "ps_p", k)])
            return pp, (pt + "ps_p", k)

        stg = {}
        names = (("gk", 4), ("dk", 8)) if pid == 0 else (("gq", 4), ("gg", 8), ("dq", 8), ("mg", 16))
        for nm, nchk in names:
            stg[nm] = sb("stg_" + nm, [128, nchk, TW2], BF16)
        qb = [sb(pt + "qb%d" % i, [128, TW2], BF16) for i in range(2)]
        r1 = [sb(pt + "r1_%d" % i, [128, TW2], F32) for i in range(2)]
        r2 = [sb(pt + "r2_%d" % i, [128, TW2], F32) for i in range(2)]
        cs = [sb(pt + "cs%d" % i, [128, 2, TW2], F32) for i in range(2)]
        ps_r = [pst(pt + "ps_r%d" % i, [128, TW2]) for i in range(2)]
        if pid == 0:
            stv = [sb("stv%d" % i, [128, TW2 // 128, 512], BF16) for i in range(2)]
            lrs = [sb("lrs%d" % i, [16, TW2], F32) for i in range(2)]
            spb = [sb("spb%d" % i, [128, 4, TW2], F32) for i in range(2)]
            ex = [sb("ex%d" % i, [128, TW2], F32) for i in range(2)]
            ps_l = [pst("ps_l%d" % i, [128, TW2]) for i in range(2)]
        ropec = [0]
        tcount = 0
        mytiles = [tl for tl in tiles2 if not (pid == 1 and tl[2] != "own")]

        def load_tile(ti):
            c0_, Wl, kind_, l0_ = mytiles[ti]
            hi_ = ti % 2
            DMA("sp", h2t[hi_][:, :, :Wl], H2T_v[:, :, c0_:c0_ + Wl], [("H2T", c0_)], [(pt + "h2t", hi_)])
            if kind_ != "ctx":
                DMA("sp", cs[hi_][:, 0, :Wl], cosT[:, l0_:l0_ + Wl], (), [(pt + "cs", hi_, 0)])
                DMA("sp", cs[hi_][:, 1, :Wl], sinT[:, l0_:l0_ + Wl], (), [(pt + "cs", hi_, 1)])

        load_tile(0)
        for (c0, W_, kind, l0) in mytiles:
            hi = tcount % 2
            if tcount + 1 < len(mytiles):
                load_tile(tcount + 1)
            tcount += 1
            h_ = h2t[hi]
            latent = kind != "ctx"

            def rope_evac(pp, pres, dst, dres):
                i = ropec[0] % 2
                ropec[0] += 1
                if not latent or 'norope' in SKIP:
                    CP("act", dst, pp[:, :W_], [pres], [dres])
                    return
                if 'rope_noperm' in SKIP:
                    if 'nocs' in SKIP:
                        TT("dve", r1[i][:, :W_], pp[:, :W_], ones_f[:, :W_], ALU.mult, [pres, "ones_f"], [(pt + "r1", i)])
                    elif 'cs_nodep' in SKIP:
                        TT("dve", r1[i][:, :W_], pp[:, :W_], cs[hi][:, 0, :W_], ALU.mult, [pres], [(pt + "r1", i)])
                    else:
                        TT("dve", r1[i][:, :W_], pp[:, :W_], cs[hi][:, 0, :W_], ALU.mult, [pres, (pt + "cs", hi, 0)], [(pt + "r1", i)])
                    CP("dve", dst, r1[i][:, :W_], [(pt + "r1", i)], [dres])
                    return
                CP("act", qb[i][:, :W_], pp[:, :W_], [pres], [(pt + "qb", i)])
                MM(ps_r[i][:, :W_], perm_b[:], qb[i][:, :W_], True, True, ["perm_b", (pt + "qb", i)], [(pt + "ps_r", i)])
                if 'rope_nodve' in SKIP:
                    CP("act", dst, ps_r[i][:, :W_], [(pt + "ps_r", i)], [dres])
                    return
                if 'r1pp' in SKIP:
                    TT("dve", r1[i][:, :W_], pp[:, :W_], cs[hi][:, 0, :W_], ALU.mult, [pres, (pt + "cs", hi, 0)], [(pt + "r1", i)])
                else:
                    TT("dve", r1[i][:, :W_], qb[i][:, :W_], cs[hi][:, 0, :W_], ALU.mult, [(pt + "qb", i), (pt + "cs", hi, 0)], [(pt + "r1", i)])
                TT("dve", r2[i][:, :W_], ps_r[i][:, :W_], cs[hi][:, 1, :W_], ALU.mult, [(pt + "ps_r", i), (pt + "cs", hi, 1)], [(pt + "r2", i)])
                TT(("pool" if "usepool" in SKIP else "dve"), dst, r1[i][:, :W_], r2[i][:, :W_], ALU.add, [(pt + "r1", i), (pt + "r2", i)], [dres])

            if pid == 0:
                for c in range(4):
                    pp, pres = proj_fm(h_, hi, cmap[O_GK] + c * 128, W_)
                    CP("act", stg["gk"][:, c, :W_], pp[:, :W_], [pres], [("stg_gk", c)])
                DMA("sp", GKT_v[:, :, c0:c0 + W_], stg["gk"][:, :, :W_], [("stg_gk", c) for c in range(4)], [("GKT", c0)])
                for c in ([] if 'dk' in SKIP else range(8)):
                    pp, pres = proj_fm(h_, hi, cmap[O_DK] + c * 128, W_)
                    rope_evac(pp, pres, stg["dk"][:, c, :W_], ("stg_dk", c))
                if "dk" not in SKIP:
                    DMA("sp", DKT_v[:, :, c0:c0 + W_], stg["dk"][:, :, :W_], [("stg_dk", c) for c in range(8)], [("DKT", c0)])
                for (ocol, dst, dname) in ([] if "v" in SKIP else ((cmap[O_GV], GV, "GV"), (cmap[O_DV], DV, "DV"))):
                    for cb in range(2):
                        k = pcount[0] % 2
                        sv = stv[k]
                        for tt_ in range(W_ // 128):
                            kk = pcount[0] % 4
                            pcount[0] += 1
                            pp = ps_p[kk]
                            for kc in range(8):
                                MM(pp[:, :512], h_[:, kc, tt_ * 128:(tt_ + 1) * 128], winb[:, kc, ocol + cb * 512:ocol + (cb + 1) * 512],
                                   kc == 0, kc == 7, [(wtag, kc), (pt + "h2t", hi)], [(pt + "ps_p", kk)])
                            CP(("act" if tt_ % 2 == 0 else "dve"), sv[:, tt_, :], pp[:, :512], [(pt + "ps_p", kk)], [("stv", k, tt_)])
                        dv_ = dst[c0:c0 + W_, cb * 512:(cb + 1) * 512].rearrange("(n p) e -> p n e", p=128)
                        DMA("sp", dv_, sv[:, :W_ // 128, :], [("stv", k, tt_) for tt_ in range(W_ // 128)], [(dname, c0, cb)])
                for d_ in ([] if 'lr' in SKIP else range(2)):
                    i = d_
                    pl = ps_l[i]
                    lc = cmap[O_LR] + 16 * d_
                    for kc in range(8):
                        MM(pl[0:16, :W_], winb[:, kc, lc:lc + 16], h_[:, kc, :W_], kc == 0, kc == 7,
                           [(wtag, kc), (pt + "h2t", hi)], [("ps_l", i)])
                    CP("dve", lrs[i][:, :W_], pl[0:16, :W_], [("ps_l", i)], [("lrs", i)])
                    for c in range(4):
                        kk = pcount[0] % 4
                        pcount[0] += 1
                        pp = ps_p[kk]
                        MM(pp[:, :W_], gw2s[:, d_, c * 128:(c + 1) * 128], lrs[i][:, :W_], True, True,
                           [("gw2s", d_), ("lrs", i)], [(pt + "ps_p", kk)])
                        e_ = ex[c % 2]
                        ACT(e_[:, :W_], pp[:, :W_], AF.Exp, [(pt + "ps_p", kk), "ngb"], [("ex", c % 2)], bias=ngb[:, d_, c:c + 1], scale=-1.0)
                        ACT(spb[i][:, c, :W_], e_[:, :W_], AF.Ln, [("ex", c % 2)], [("spb", i, c)], bias=1.0)
                    DMA("sp", SPT_v[d_][:, :, c0:c0 + W_], spb[i][:, :, :W_], [("spb", i, c) for c in range(4)], [("SPT", d_, c0)])
            else:
                for c in range(4):
                    pp, pres = proj_fm(h_, hi, cmap[O_GQ] + c * 128, W_)
                    ACT(stg["gq"][:, c, :W_], pp[:, :W_], AF.Identity, [pres], [("stg_gq", c)], scale=128.0 ** -0.5)
                DMA("sp", GQT_v[:, :, l0:l0 + W_], stg["gq"][:, :, :W_], [("stg_gq", c) for c in range(4)], [("GQT", l0)])
                for c in range(8):
                    pp, pres = proj_fm(h_, hi, cmap[O_GG] + c * 128, W_)
                    ACT(stg["gg"][:, c, :W_], pp[:, :W_], AF.Silu, [pres], [("stg_gg", c)])
                DMA("sp", GGT_v[:, :, l0:l0 + W_], stg["gg"][:, :, :W_], [("stg_gg", c) for c in range(8)], [("GGT", l0)])
                for c in range(8):
                    pp, pres = proj_fm(h_, hi, cmap[O_DQ] + c * 128, W_)
                    rope_evac(pp, pres, stg["dq"][:, c, :W_], ("stg_dq", c))
                DMA("sp", DQT_v[:, :, l0:l0 + W_], stg["dq"][:, :, :W_], [("stg_dq", c) for c in range(8)], [("DQT", l0)])
                for c in range(16):
                    pp, pres = proj_fm(h_, hi, cmap[O_MG] + c * 128, W_)
                    ACT(stg["mg"][:, c, :W_], pp[:, :W_], AF.Sigmoid, [pres], [("stg_mg", c)])
                DMA("sp", MGT_v[:, :, l0:l0 + W_], stg["mg"][:, :, :W_], [("stg_mg", c) for c in range(16)], [("MGT", l0)])
        es.close()

    proj_pass(0)
    P.barrier()
    if 'pass1' not in SKIP:
        proj_pass(1)

    def dram_deps(name, a, b, step):
        return [(name, cc) for cc in range(a, b, step)]

    if stop_after < 3:
        return finish()
    P.barrier()
    es = ExitStack()
    NCK = TOK // 128
    NOWN = HALF // 128
    qT = sb("g_qT", [128, HALF], BF16)
    kT = sb("g_kT", [128, TOK], BF16)
    vt = sb("g_v", [128, NCK, 256], BF16)
    Pst = [sb("g_P0", [128, CTX + HALF + 1], F32), sb("g_P1", [128, TOK + 1], F32)]
    spl = [sb("g_sp%d" % i, [128, 512], F32) for i in range(2)]
    oT = sb("g_oT", [128, 2, HALF], F32)
    S = sb("g_S", [128, 256], F32)
    Stmp = sb("g_Stmp", [128, 256], F32)
    Sb = sb("g_Sb", [128, 256], BF16)
    E1 = [sb("g_E1_%d" % i, [128, 128], F32) for i in range(2)]
    E2 = [sb("g_E2_%d" % i, [128, 128], F32) for i in range(2)]
    qe = [sb("g_qe%d" % i, [128, 128], BF16) for i in range(2)]
    keT = [sb("g_keT%d" % i, [128, 128], BF16) for i in range(2)]
    ket = [sb("g_ket%d" % i, [128, 128], BF16) for i in range(2)]
    atb = [sb("g_atb%d" % i, [128, 128], BF16) for i in range(2)]
    bcol = [sb("g_bcol%d" % i, [128, 4], F32) for i in range(4)]
    ggt = [sb("g_ggt%d" % i, [128, 2, 512], BF16) for i in range(1)] * 2
    ygs = [sb("g_ygs%d" % i, [128, 2, 512], BF16) for i in range(1)] * 2
    gsq = [sb("g_sq%d" % i, [128, 512], F32) for i in range(2)]
    grs = sb("g_rs", [128, 512], F32)
    gtm = [sb("g_tm%d" % i, [128, 512], F32) for i in range(2)]
    ps_at = [pst("ps_at%d" % i, [128, 512])[:, 0:128] for i in range(2)]
    ps_tr = [pst("ps_tr%d" % i, [128, 1024], BF16)[:, 0:128] for i in range(2)]
    ps_o = [pst("ps_o%d" % i, [128, 4, 128])[:, 0:2, :] for i in range(2)]
    ps_ds = pst("ps_ds", [128, 512])[:, 0:256]
    ps_gs = pst("ps_gs", [128, 512])
    GV_v = GV.rearrange("(n p) e -> p n e", p=128)
    YGT_v = YGT.rearrange("(c p) n -> p c n", p=128)
    cc_ = [0]
    for h in range(4):
        DMA("sp", qT[:], GQT[h * 128:(h + 1) * 128, :], dram_deps("GQT", 0, HALF, TW2), ["g_qT"])
        DMA("sp", kT[:], GKT[h * 128:(h + 1) * 128, :], dram_deps("GKT", 0, CTX, CTX) + dram_deps("GKT", CTX, TOK, TW2), ["g_kT"])
        vdeps = [("GV", 0, cb) for cb in range(2)] + [("GV", c0, cb) for c0 in range(CTX, TOK, TW2) for cb in range(2)]
        for n0 in range(0, NCK, 16):
            n1 = min(NCK, n0 + 16)
            DMA("sp", vt[:, n0:n1, :], GV_v[:, n0:n1, h * 256:(h + 1) * 256], vdeps, [("g_v", n0)])
        vres = [("g_v", n0) for n0 in range(0, NCK, 16)]
        for d_ in range(2):
            MEMSET("dve", Pst[d_][:, 0:1], 0.0, [("g_P", d_)])
            NT_ = (CTX + HALF) if d_ == 0 else TOK
            for n0 in range(0, NT_, 512):
                n1 = min(NT_, n0 + 512)
                s_ = spl[(n0 // 512) % 2]
                spdeps = [("SPT", d_, c0) for c0 in ([0] + list(range(CTX, TOK, TW2)))]
                DMA("sp", s_[:, :n1 - n0], SPT[d_, h * 128:(h + 1) * 128, n0:n1], spdeps, [("g_sp", (n0 // 512) % 2)])
                P.op("dve", (lambda d_=d_, n0=n0, n1=n1, s_=s_: (lambda e: e.tensor_tensor_scan(
                    out=Pst[d_][:, n0 + 1:n1 + 1], data0=ones_f[:, :n1 - n0], data1=s_[:, :n1 - n0],
                    initial=Pst[d_][:, n0:n0 + 1], op0=ALU.mult, op1=ALU.add)))(),
                    [("g_sp", (n0 // 512) % 2), "ones_f", ("g_P", d_)], [("g_P", d_)])

        def chunk(d_, ck, full, first_out):
            i = cc_[0] % 2
            cc_[0] += 1
            a = ck * 128
            Pd = Pst[d_]
            bc = bcol[cc_[0] % 4]
            bres = ("g_bcol", cc_[0] % 4)
            if d_ == 0:
                sgn = -1.0 / 16.0
                pcols = Pd[:, a + 1:a + 129]
                bsrc = Pd[:, a:a + 1]
                elcol = 127
            else:
                sgn = 1.0 / 16.0
                pcols = Pd[:, a:a + 128]
                bsrc = Pd[:, a + 128:a + 129]
                elcol = 0
            TS("dve", bc[:, 0:1], bsrc, -sgn, None, ALU.mult, None, [("g_P", d_)], [bres])
            TS("dve", bc[:, 1:2], bsrc, sgn, None, ALU.mult, None, [("g_P", d_)], [bres])
            ACT(E1[i][:], pcols, AF.Exp, [("g_P", d_), bres], [("g_E1", i)], bias=bc[:, 0:1], scale=sgn)
            ACT(E2[i][:], pcols, AF.Exp, [("g_P", d_), bres], [("g_E2", i)], bias=bc[:, 1:2], scale=-sgn)
            TT("dve", keT[i][:], kT[:, a:a + 128], E2[i][:], ALU.mult, ["g_kT", ("g_E2", i)], [("g_keT", i)])
            TR(ps_tr[i][:], keT[i][:], id_b[:], [("g_keT", i), "id_b"], [("ps_tr", i)])
            CP("act", ket[i][:], ps_tr[i][:], [("ps_tr", i)], [("g_ket", i)])
            if full:
                lo = a - CTX
                TT("dve", qe[i][:], qT[:, lo:lo + 128], E1[i][:], ALU.mult, ["g_qT", ("g_E1", i)], [("g_qe", i)])
                MM(ps_at[i][:], keT[i][:], qe[i][:], True, True, [("g_keT", i), ("g_qe", i)], [("ps_at", i)])
                TT("dve", atb[i][:], ps_at[i][:], mask_f[:, d_, :], ALU.mult, [("ps_at", i), "mask_f"], [("g_atb", i)])
                for ec in range(2):
                    MM(ps_o[i][:, ec, :], Sb[:, ec * 128:(ec + 1) * 128], qe[i][:], True, False,
                       ["g_Sb", ("g_qe", i)], [("ps_o", i)])
                    MM(ps_o[i][:, ec, :], vt[:, ck, ec * 128:(ec + 1) * 128], atb[i][:], False, True,
                       vres + [("g_atb", i)], [("ps_o", i)])
                if first_out:
                    CP("act", oT[:, :, lo:lo + 128], ps_o[i][:], [("ps_o", i)], [("g_oT", lo)])
                else:
                    TT("dve", oT[:, :, lo:lo + 128], oT[:, :, lo:lo + 128], ps_o[i][:], ALU.add,
                       [("ps_o", i), ("g_oT", lo)], [("g_oT", lo)])
            MM(ps_ds[:], ket[i][:], vt[:, ck, :], True, True, [("g_ket", i)] + vres, ["ps_ds"])
            TT("dve", Stmp[:], S[:], ps_ds[:], ALU.add, ["g_S", "ps_ds"], ["g_Stmp"])
            TS("dve", S[:], Stmp[:], E1[i][:, elcol:elcol + 1], None, ALU.mult, None, ["g_Stmp", ("g_E1", i)], ["g_S"])
            P.op("act", (lambda i=i, elcol=elcol: (lambda e: e.activation(out=Sb[:], in_=Stmp[:], func=AF.Copy,
                                                                         scale=E1[i][:, elcol:elcol + 1])))(),
                 ["g_Stmp", ("g_E1", i)], ["g_Sb"])

        MEMSET("dve", S[:], 0.0, ["g_S"])
        MEMSET("dve", Sb[:], 0.0, ["g_Sb"])
        for ck in range(CTX // 128):
            chunk(0, ck, False, False)
        for ck in range(CTX // 128, CTX // 128 + NOWN):
            chunk(0, ck, True, True)
        MEMSET("dve", S[:], 0.0, ["g_S"])
        MEMSET("dve", Sb[:], 0.0, ["g_Sb"])
        for ck in range(CTX // 128 - 1, -1, -1):
            chunk(1, ck, False, False)
        for ck in range(NCK - 1, CTX // 128 + NOWN - 1, -1):
            chunk(1, ck, False, False)
        for ck in range(CTX // 128 + NOWN - 1, CTX // 128 - 1, -1):
            chunk(1, ck, True, False)
        for q0 in range(0, HALF, 512):
            k = 0
            DMA("sp", ggt[k][:], GGT_v[:, 2 * h:2 * h + 2, q0:q0 + 512], dram_deps("GGT", q0, q0 + 512, TW2), [("g_ggt", k)])
            ores = [("g_oT", lo) for lo in range(q0, q0 + 512, 128)]
            for ec in range(2):
                ACT(gsq[ec][:], oT[:, ec, q0:q0 + 512], AF.Square, ores, [("g_sq", ec)])
                MM(ps_gs[:], ones_f[:, 0:128], gsq[ec][:], ec == 0, ec == 1, [("g_sq", ec), "ones_f"], ["ps_gs"])
            ACT(grs[:], ps_gs[:], AF.Sqrt, ["ps_gs", "eps_t"], ["g_rs"], bias=eps_t[:, 0:1], scale=1.0 / 256)
            RECIP(grs[:], grs[:], ["g_rs"], ["g_rs"])
            for ec in range(2):
                STT(gtm[ec][:], oT[:, ec, q0:q0 + 512], gng[:, 2 * h + ec:2 * h + ec + 1], grs[:], ALU.mult, ALU.mult,
                    ores + ["gng", "g_rs"], [("g_tm", ec)])
                TT("pool", ygs[k][:, ec, :], gtm[ec][:], ggt[k][:, ec, :], ALU.mult, [("g_tm", ec), ("g_ggt", k)], [("g_ygs", k, ec)])
            DMA("sp", YGT_v[:, 2 * h:2 * h + 2, q0:q0 + 512], ygs[k][:], [("g_ygs", k, 0), ("g_ygs", k, 1)], [("YGT", h, q0)])
    es.close()

    if stop_after < 4:
        return finish()
    P.barrier()
    es = ExitStack()
    QT_ = 512 if HALF >= 512 else HALF
    dq = [sb("d_q%d" % i, [128, HALF], BF16) for i in range(2)]
    dk = [sb("d_k%d" % i, [128, TOK], BF16) for i in range(2)]
    dv = [sb("d_v%d" % i, [128, NCK, 128], BF16) for i in range(2)]
    NPB = 4
    pb = [sb("d_p%d" % i, [128, 2, QT_], BF16) for i in range(NPB)]
    acc = sb("d_acc", [128, 2, QT_], F32)
    rz = [sb("d_rz%d" % i, [128, QT_], F32) for i in range(2)]
    t12 = [sb("d_t%d" % i, [128, QT_], F32) for i in range(2)]
    od = sb("d_o", [128, QT_], F32)
    dsq = sb("d_sq", [128, QT_], F32)
    drs = sb("d_rs", [128, QT_], F32)
    yds = [sb("d_yds%d" % i, [128, QT_], BF16) for i in range(2)]
    dgn = sb("d_gn", [128, 8], F32)
    TS("dve", dgn[:], dng[:], 0.8, None, ALU.mult, None, ["dng"], ["d_gn"])
    ps_s = [pst("ps_s%d" % i, [128, 2, QT_]) for i in range(2)]
    ps_av = [pst("ps_av%d" % c, [128, QT_]) for c in range(2)]
    ps_z = [pst("ps_z%d" % c, [128, QT_]) for c in range(2)]
    DV_v = DV.rearrange("(n p) e -> p n e", p=128)
    kdeps = dram_deps("DKT", 0, CTX, CTX) + dram_deps("DKT", CTX, TOK, TW2)
    vdeps_d = [("DV", 0, cb) for cb in range(2)] + [("DV", c0, cb) for c0 in range(CTX, TOK, TW2) for cb in range(2)]
    it = [0]
    for h in range(8):
        hb_ = h % 2
        DMA("sp", dq[hb_][:], DQT[h * 128:(h + 1) * 128, :], dram_deps("DQT", 0, HALF, TW2), [("d_q", hb_)])
        DMA("sp", dk[hb_][:], DKT[h * 128:(h + 1) * 128, :], kdeps, [("d_k", hb_)])
        for n0 in range(0, NCK, 16):
            n1 = min(NCK, n0 + 16)
            DMA("sp", dv[hb_][:, n0:n1, :], DV_v[:, n0:n1, h * 128:(h + 1) * 128], vdeps_d, [("d_v", hb_, n0)])
        vres = [("d_v", hb_, n0) for n0 in range(0, NCK, 16)]
        for q0 in range(0, HALF, QT_):
            def scores(kc):
                sbuf_i = kc % 2
                for c in range(2):
                    MM(ps_s[sbuf_i][:, c, :], dk[hb_][64 * c:64 * c + 64, kc * 128:(kc + 1) * 128],
                       dq[hb_][64 * c:64 * c + 64, q0:q0 + QT_], True, True,
                       [("d_k", hb_), ("d_q", hb_)], [("ps_s", sbuf_i)])

            MEMSET("dve", acc[:, 0, :], 0.0, [("d_acc", 0)])
            def expo(kc):
                sbuf_i = kc % 2
                pi = kc % NPB
                P.op("act", (lambda sbuf_i=sbuf_i, pi=pi: (lambda e: e.activation(
                    out=pb[pi][:], in_=ps_s[sbuf_i][:], func=AF.Exp, scale=0.125)))(),
                    [("ps_s", sbuf_i)], [("d_p", pi)])

            def avz(kc):
                pi = kc % NPB
                for c in range(2):
                    MM(ps_av[c][:], dv[hb_][:, kc, :], pb[pi][:, c, :], kc == 0, kc == NCK - 1,
                       vres + [("d_p", pi)], [("ps_av", c)])
                MM(ps_z[1][:], ones_b[:], pb[pi][:, 1, :], kc == 0, kc == NCK - 1,
                   ["ones_b", ("d_p", pi)], [("ps_z", 1)])
                TT("dve", acc[:, 0, :], acc[:, 0, :], pb[pi][:, 0, :], ALU.add, [("d_acc", 0), ("d_p", pi)], [("d_acc", 0)])

            scores(0)
            expo(0)
            if NCK > 1:
                scores(1)
            for kc in range(1, NCK):
                expo(kc)
                avz(kc - 1)
                if kc + 1 < NCK:
                    scores(kc + 1)
            avz(NCK - 1)
            MM(ps_z[0][:], ones_f[:, 0:128], acc[:, 0, :], True, True, ["ones_f", ("d_acc", 0)], [("ps_z", 0)])
            for c in range(2):
                RECIP(rz[c][:], ps_z[c][:], [("ps_z", c)], [("d_rz", c)])
                TT("dve", t12[c][:], ps_av[c][:], rz[c][:], ALU.mult, [("ps_av", c), ("d_rz", c)], [("d_t", c)])
            STT(od[:], t12[1][:], neglam, t12[0][:], ALU.mult, ALU.add, [("d_t", 0), ("d_t", 1), "lam_w"], ["d_o"])
            ACT(dsq[:], od[:], AF.Square, ["d_o"], ["d_sq"])
            MM(ps_z[0][:], ones_f[:, 0:128], dsq[:], True, True, ["d_sq", "ones_f"], [("ps_z", 0)])
            ACT(drs[:], ps_z[0][:], AF.Sqrt, [("ps_z", 0), "eps_t"], ["d_rs"], bias=eps_t[:, 0:1], scale=1.0 / 128)
            RECIP(drs[:], drs[:], ["d_rs"], ["d_rs"])
            k = it[0] % 2
            it[0] += 1
            STT(yds[k][:], od[:], dgn[:, h:h + 1], drs[:], ALU.mult, ALU.mult, ["d_o", "d_gn", "d_rs"], [("d_yds", k)])
            DMA("sp", YDT[h * 128:(h + 1) * 128, q0:q0 + QT_], yds[k][:], [("d_yds", k)], [("YDT", h, q0)])
    es.close()

    if stop_after < 5:
        return finish()
    P.barrier()
    es = ExitStack()
    TW5 = 512 if HALF >= 512 else HALF
    wbg = sb("m_wbg", [128, 8, D], BF16)
    wbd = sb("m_wbd", [128, 8, D], BF16)
    wo = sb("m_wo", [128, 8, D], BF16)
    load_w_bf16(wbg, wbg_d.rearrange("(kc p) n -> p kc n", p=128), 8, D, "m_wbg")
    load_w_bf16(wbd, wbd_d.rearrange("(kc p) n -> p kc n", p=128), 8, D, "m_wbd")
    load_w_bf16(wo, wo_d.rearrange("(kc p) n -> p kc n", p=128), 8, D, "m_wo")
    ygt = [sb("m_yg%d" % i, [128, 8, TW5], BF16) for i in range(2)]
    ydt = [sb("m_yd%d" % i, [128, 8, TW5], BF16) for i in range(2)]
    mgt = [sb("m_mg%d" % i, [128, 16, TW5], BF16) for i in range(2)]
    x1t = [sb("m_x1%d" % i, [128, 8, TW5], F32) for i in range(2)]
    zb = sb("m_z", [128, 8, TW5], BF16)
    z1 = [sb("m_z1_%d" % i, [128, TW5], F32) for i in range(2)]
    z2 = [sb("m_z2_%d" % i, [128, TW5], F32) for i in range(2)]
    ps_a = [pst("ps_a%d" % i, [128, TW5]) for i in range(2)]
    ps_b = [pst("ps_b%d" % i, [128, TW5]) for i in range(2)]
    ps_w = [pst("ps_w%d" % i, [128, TW5]) for i in range(2)]
    YDT_v = YDT.rearrange("(c p) n -> p c n", p=128)
    X2T_v = X2T.rearrange("(c p) n -> p c n", p=128)
    for t, q0 in enumerate(range(0, HALF, TW5)):
        k = t % 2
        ygd = [("YGT", hh, qq) for hh in range(4) for qq in range(q0 - q0 % 512, q0 + TW5, 512)]
        ydd = [("YDT", hh, qq) for hh in range(8) for qq in range(q0 - q0 % QT_, q0 + TW5, QT_)]
        DMA("sp", ygt[k][:], YGT_v[:, :, q0:q0 + TW5], ygd, [("m_yg", k)])
        DMA("sp", ydt[k][:], YDT_v[:, :, q0:q0 + TW5], ydd, [("m_yd", k)])
        DMA("sp", mgt[k][:], MGT_v[:, :, q0:q0 + TW5], dram_deps("MGT", q0, q0 + TW5, TW2), [("m_mg", k)])
        DMA("sp", x1t[k][:], X1T_v[:, :, q0:q0 + TW5], dram_deps("X1T", q0, q0 + TW5, TW1), [("m_x1", k, c) for c in range(8)])
        for c in range(8):
            i = c % 2
            for kc in range(8):
                MM(ps_a[i][:], wbg[:, kc, c * 128:(c + 1) * 128], ygt[k][:, kc, :], kc == 0, kc == 7,
                   [("m_wbg", kc), ("m_yg", k)], [("ps_a", i)])
            for kc in range(8):
                MM(ps_b[i][:], wbd[:, kc, c * 128:(c + 1) * 128], ydt[k][:, kc, :], kc == 0, kc == 7,
                   [("m_wbd", kc), ("m_yd", k)], [("ps_b", i)])
            TT("dve", z1[i][:], ps_a[i][:], mgt[k][:, c, :], ALU.mult, [("ps_a", i), ("m_mg", k)], [("m_z1", i)])
            TT("dve", z2[i][:], ps_b[i][:], mgt[k][:, 8 + c, :], ALU.mult, [("ps_b", i), ("m_mg", k)], [("m_z2", i)])
            TT("pool", zb[:, c, :], z1[i][:], z2[i][:], ALU.add, [("m_z1", i), ("m_z2", i)], [("m_z", c)])
        for c in range(8):
            i = c % 2
            for kc in range(8):
                MM(ps_w[i][:], wo[:, kc, c * 128:(c + 1) * 128], zb[:, kc, :], kc == 0, kc == 7,
                   [("m_wo", kc), ("m_z", kc)], [("ps_w", i)])
            STT(x1t[k][:, c, :], ps_w[i][:], gte[:, 1, c, 0:1], x1t[k][:, c, :], ALU.mult, ALU.add,
                [("ps_w", i), "gte", ("m_x1", k, c)], [("m_x1", k, c)])
        DMA("sp", X2T_v[:, :, q0:q0 + TW5], x1t[k][:], [("m_x1", k, c) for c in range(8)], [("X2T", q0)])
    es.close()

    if stop_after < 6:
        return finish()
    P.barrier()
    tiles6 = [(i * TW1, TW1, 0, i * TW1) for i in range(HALF // TW1)]
    outT_v = outT.rearrange("(c p) n -> p c n", p=128)

    def post6(t, x_, W_, sq, ps_stat, rstd2, tmp, h2, tag):
        c0 = tiles6[t][0]
        rms_rstd(tag, tag + "brstd", lambda c: x_[:, c, :W_], 8, W_, sq, ps_stat, rstd2, 1.0 / D,
                 lambda c: [(tag + "xt", t % 2, c)])
        for hh in range(2):
            for c4 in range(4):
                c = hh * 4 + c4
                STT(h2[:, c4, :W_], x_[:, c, :W_], fg[:, c:c + 1], rstd2[:, :W_], ALU.mult, ALU.mult,
                    [(tag + "xt", t % 2, c), "fg", tag + "brstd"], [(tag + "h2", c4)])
            DMA("sp", outT_v[:, hh * 4:hh * 4 + 4, c0:c0 + W_], h2[:, :, :W_], [(tag + "h2", c4) for c4 in range(4)], [("outT", c0, hh)])

    ffn_phase("f2", f2w1, f2w3, f2w2, tiles6, TW1, X2T, 2, post6, lambda t: [("X2T", (tiles6[t][0] // TW5) * TW5)])

    P.emit()
    cons.close()
    return nc


def _const_tables(SEQ):
    half, quarter = 32, 16
    inv = (10000.0 ** (-np.arange(quarter, dtype=np.float32) / quarter)).astype(np.float32)
    t = np.arange(SEQ, dtype=np.int32)
    row = (t // GRID_W).astype(np.float32)
    col = (t % GRID_W).astype(np.float32)
    cosT = np.zeros((128, SEQ), np.float32)
    sinT = np.zeros((128, SEQ), np.float32)
    perm = np.zeros((128, 128), np.float32)
    for p in range(128):
        d = p % 64
        pos = row if d < half else col
        j = d % quarter
        ang = (pos * inv[j]).astype(np.float32)
        cosT[p] = np.cos(ang).astype(np.float32)
        sn = np.sin(ang).astype(np.float32)
        if (d % half) < quarter:
            sinT[p] = -sn
            perm[p + 16, p] = 1.0
        else:
            sinT[p] = sn
            perm[p - 16, p] = 1.0
    s = np.arange(128)
    masks = np.zeros((128, 2, 128), np.float32)
    masks[:, 0, :] = (s[:, None] <= s[None, :])
    masks[:, 1, :] = (s[:, None] >= s[None, :])
    return cosT, sinT, perm, masks


def _prep(inputs, SEQ):
    f = lambda a: np.ascontiguousarray(np.asarray(a, dtype=np.float32))
    x = f(inputs["x"]); c = f(inputs["c"]); ctx = f(inputs["ctx"]); c_ctx = f(inputs["c_ctx"])
    B = x.shape[0]
    cosT, sinT, perm, masks = _const_tables(SEQ)
    w_in0 = f(inputs["w_in"][0])
    w_in1 = w_in0.copy()
    w_in1[:, O_LR:O_LR + 16] = w_in0[:, O_LR + 16:O_LR + 32]
    w_in1[:, O_LR + 16:O_LR + 32] = w_in0[:, O_LR:O_LR + 16]
    gw2 = f(inputs["gla_gate_w2"][0]); gb = f(inputs["gla_gate_b"][0])
    shared = {
        "w_ada": f(inputs["w_ada"][0]),
        "bT_ada": f(inputs["b_ada"][0].reshape(72, 128).T),
        "ngT": f(np.stack([inputs["ffn1_norm"][0], inputs["mix_norm"][0], inputs["ffn2_norm"][0]]).reshape(3, 8, 128).transpose(2, 0, 1)),
        "fgT": f(np.asarray(inputs["final_norm"]).reshape(8, 128).T),
        "gngT": f(np.asarray(inputs["gla_out_norm"][0]).reshape(8, 128).T),
        "dngT": f(np.asarray(inputs["diff_out_norm"][0]).reshape(8, 128).T),
        "f1w1": f(inputs["ffn1_w1"][0]), "f1w3": f(inputs["ffn1_w3"][0]), "f1w2": f(inputs["ffn1_w2"][0]),
        "f2w1": f(inputs["ffn2_w1"][0]), "f2w3": f(inputs["ffn2_w3"][0]), "f2w2": f(inputs["ffn2_w2"][0]),
        "lamT": f(np.tile(np.asarray(inputs["diff_lambda"][0]).reshape(1, 256), (128, 1))),
        "permM": perm, "masks": masks, "ident": np.eye(128, dtype=np.float32),
        "wbg": f(inputs["w_branch_gla"][0]), "wbd": f(inputs["w_branch_diff"][0]), "wo": f(inputs["w_out"][0]),
    }
    in_maps = []
    for core in range(2 * B):
        b, hf = core // 2, core % 2
        xs = x[b]; cs_ = ctx[b]
        if hf == 1:
            xs = xs[::-1]; cs_ = cs_[::-1]
        xT = np.ascontiguousarray(np.concatenate([cs_, xs], axis=0).T)
        order = (0, 1) if hf == 0 else (1, 0)
        m = dict(shared)
        m["xT"] = xT
        m["cT"] = f(np.stack([c[b].reshape(8, 128).T, c_ctx.reshape(8, 128).T], axis=-1))
        m["w_in"] = w_in0 if hf == 0 else w_in1
        m["gw2"] = f(np.stack([gw2[order[0]], gw2[order[1]]]))
        m["gbT"] = f(np.stack([gb[order[0]].reshape(4, 128).T, gb[order[1]].reshape(4, 128).T], axis=1))
        m["cosT"] = cosT if hf == 0 else np.ascontiguousarray(cosT[:, ::-1])
        m["sinT"] = sinT if hf == 0 else np.ascontiguousarray(sinT[:, ::-1])
        in_maps.append(m)
    return in_maps


def _run(inputs, SEQ, dbg=False, stop_after=99):
    B = np.asarray(inputs["x"]).shape[0]
    HALF = SEQ // 2
    in_maps = _prep(inputs, SEQ)
    nc = build_nc(SEQ, dbg=dbg, stop_after=stop_after)
    res = run_bass_kernel_spmd(nc, in_maps, core_ids=list(range(2 * B)))
    out = np.empty((B, SEQ, D), np.float32)
    for core in range(2 * B):
        b, hf = core // 2, core % 2
        o = res.results[core]["outT"].T
        if hf == 0:
            out[b, :HALF] = o
        else:
            out[b, HALF:] = o[::-1]
    return out, res


def kernel(**inputs):
    SEQ = np.asarray(inputs["x"]).shape[1]
    out, _ = _run(inputs, SEQ)
    return out
```

```python
import numpy as np
import concourse.bass as bass
import concourse.mybir as mybir
from concourse.bass_utils import run_bass_kernel_spmd

F32 = mybir.dt.float32
BF16 = mybir.dt.bfloat16
AF = mybir.ActivationFunctionType
ALU = mybir.AluOpType

D = 1024
NC = 8
FF = 2816
NF = 22
CTX = 256
EPS = 1e-6
GRID_W = 64
IN_COLS = 8224
O_GQ, O_GK, O_GV, O_GG, O_LR, O_DQ, O_DK, O_DV, O_MG = 0, 512, 1024, 2048, 3072, 3104, 4128, 5152, 6176
N_DMA_SEMS = 24
SKIP = set()


class Prog:
    def __init__(self, nc):
        self.nc = nc
        self.ops = []
        self.lastw = {}
        self.readers = {}
        self.ndma = 0
        self.nsw = 0
        self.dma_prev = {}
        self._guards = []
        self.barriers = []
        self.pending_bar = {}
        self.pool_dmas = []

    def barrier(self):
        ops = self.ops
        prior = set()
        seen_e = set()
        seen_s = set()
        for i in range(len(ops) - 1, -1, -1):
            o = ops[i]
            if o[3]:
                if o[4] not in seen_s:
                    seen_s.add(o[4]); prior.add(i)
            elif o[0] not in seen_e:
                seen_e.add(o[0]); prior.add(i)
        prior = sorted(prior)
        last = None
        for j in range(0, max(1, len(prior)), 3):
            idx = len(ops)
            ops.append(["sp", (lambda e: e.nop()), set(prior[j:j + 3]), False, None])
            last = idx
        self.pending_bar = {eng: last for eng in ("pe", "act", "dve", "pool")}

    def _deps(self, reads, writes):
        d = set()
        for r in reads:
            w = self.lastw.get(r)
            if w is not None:
                d.add(w)
        for w_ in writes:
            w = self.lastw.get(w_)
            if w is not None:
                d.add(w)
            for r in self.readers.get(w_, ()):
                d.add(r)
        return d

    def _commit(self, idx, reads, writes):
        for w_ in writes:
            self.lastw[w_] = idx
            self.readers[w_] = []
        for r in reads:
            if r not in writes:
                self.readers.setdefault(r, []).append(idx)

    def op(self, eng, fn, reads=(), writes=()):
        idx = len(self.ops)
        d = self._deps(reads, writes)
        if eng in self.pending_bar:
            d.add(self.pending_bar.pop(eng))
        self.ops.append([eng, fn, d, False, None])
        self._commit(idx, reads, writes)
        return idx

    def dma(self, eng, fn, reads=(), writes=()):
        idx = len(self.ops)
        d = self._deps(reads, writes)
        if eng in self.pending_bar:
            d.add(self.pending_bar.pop(eng))
        if eng == "pool":
            s = "sw%d" % (self.nsw % 8)
            self.nsw += 1
        else:
            s = "hw%d" % (self.ndma % N_DMA_SEMS)
            self.ndma += 1
        if s in self.dma_prev:
            d.add(self.dma_prev[s])
        self.dma_prev[s] = idx
        if eng == "pool":
            if len(self.pool_dmas) >= 12:
                d.add(self.pool_dmas[-12])
            self.pool_dmas.append(idx)
        self.ops.append([eng, fn, d, True, s])
        self._commit(idx, reads, writes)
        return idx

    def emit(self, final_wait_all_dma=True):
        nc = self.nc
        ops = self.ops
        for p in self.barriers:
            prior = set()
            seen_e = set()
            seen_s = set()
            for i in range(p - 1, -1, -1):
                o = ops[i]
                if o[3]:
                    if o[4] not in seen_s:
                        seen_s.add(o[4]); prior.add(i)
                elif o[0] not in seen_e:
                    seen_e.add(o[0]); prior.add(i)
                if len(seen_e) >= 4 and len(seen_s) >= N_DMA_SEMS + 8:
                    break
            done_e = set()
            for i in range(p, len(ops)):
                if ops[i][0] not in done_e:
                    done_e.add(ops[i][0])
                    ops[i][2] |= prior
                if len(done_e) >= 5:
                    break
        need = [False] * len(ops)
        for i, (eng, fn, deps, is_dma, s) in enumerate(ops):
            for dp in deps:
                pe = ops[dp]
                if pe[0] == "pe" and eng == "pe" and not pe[3] and not is_dma:
                    continue
                need[dp] = True
        for i, o in enumerate(ops):
            if o[3]:
                need[i] = True
        sems = {}

        def getsem(name):
            if name not in sems:
                g = nc.semaphore(name)
                sems[name] = g.__enter__()
                self._guards.append(g)
            return sems[name]

        cnt = {}
        tok = [None] * len(ops)
        for i, (eng, fn, deps, is_dma, s) in enumerate(ops):
            if not need[i]:
                continue
            name = ("dma" + s) if is_dma else ("e_" + eng)
            getsem(name)
            cnt[name] = cnt.get(name, 0) + (16 if is_dma else 1)
            tok[i] = (name, cnt[name])
        per = {k: [] for k in ("pe", "act", "dve", "pool", "sp")}
        for i, (eng, fn, deps, is_dma, s) in enumerate(ops):
            waits = []
            for dp in deps:
                pe = ops[dp]
                if pe[0] == "pe" and eng == "pe" and not pe[3] and not is_dma:
                    continue
                waits.append(tok[dp])
            per[eng].append((fn, waits, tok[i], 16 if is_dma else 1))
        final = [(n, c) for n, c in cnt.items() if n.startswith("dma")]
        with nc.Block() as block:
            def run(e, lst, fin=False):
                seen = {}
                for fn, waits, t, amt in lst:
                    for (sn, v) in waits:
                        if seen.get(sn, -1) >= v:
                            continue
                        seen[sn] = v
                        e.wait_ge(sems[sn], v)
                    ins = fn(e)
                    if t is not None:
                        ins.then_inc(sems[t[0]], amt)
                if fin:
                    for j, (sn, v) in enumerate(final):
                        e.wait_ge(sems[sn], v)
                        if j % 3 == 2:
                            e.nop()

            @block.tensor
            def _(e): run(e, per["pe"])

            @block.scalar
            def _(e): run(e, per["act"])

            @block.vector
            def _(e): run(e, per["dve"])

            @block.gpsimd
            def _(e): run(e, per["pool"])

            @block.sync
            def _(e): run(e, per["sp"], fin=True)


def build_nc(SEQ, dbg=False, stop_after=99):
    HALF = SEQ // 2
    TOK = CTX + SEQ
    OWN_END = CTX + HALF
    nc = bass.Bass("TRN2", target_bir_lowering=False)
    P = Prog(nc)

    def din(name, shape, dt=F32):
        return nc.dram_tensor(name, list(shape), dt, kind="ExternalInput").ap()

    def dscr(name, shape, dt):
        return nc.dram_tensor(name, list(shape), dt, kind=("ExternalOutput" if dbg else "Internal")).ap()

    xT = din("xT", [D, TOK])
    cT = din("cT", [128, 8, 2])
    w_ada = din("w_ada", [D, 9 * D])
    bT_ada = din("bT_ada", [128, 72])
    ngT = din("ngT", [128, 3, 8])
    fgT = din("fgT", [128, 8])
    gngT = din("gngT", [128, 8])
    dngT = din("dngT", [128, 8])
    f1w1 = din("f1w1", [D, FF]); f1w3 = din("f1w3", [D, FF]); f1w2 = din("f1w2", [FF, D])
    f2w1 = din("f2w1", [D, FF]); f2w3 = din("f2w3", [D, FF]); f2w2 = din("f2w2", [FF, D])
    w_in = din("w_in", [D, IN_COLS])
    gw2 = din("gw2", [2, 16, 512])
    gbT = din("gbT", [128, 2, 4])
    lamT = din("lamT", [128, 256])
    cosT = din("cosT", [128, SEQ])
    sinT = din("sinT", [128, SEQ])
    permM = din("permM", [128, 128])
    masks = din("masks", [128, 2, 128])
    ident = din("ident", [128, 128])
    wbg_d = din("wbg", [D, D]); wbd_d = din("wbd", [D, D]); wo_d = din("wo", [D, D])
    outT = nc.dram_tensor("outT", [D, HALF], F32, kind="ExternalOutput").ap()

    X1T = dscr("X1T", [D, HALF], F32)
    H2T = dscr("H2T", [D, TOK], BF16)
    GQT = dscr("GQT", [512, HALF], BF16)
    GKT = dscr("GKT", [512, TOK], BF16)
    GV = dscr("GV", [TOK, D], BF16)
    GGT = dscr("GGT", [D, HALF], BF16)
    SPT = dscr("SPT", [2, 512, TOK], F32)
    DQT = dscr("DQT", [D, HALF], BF16)
    DKT = dscr("DKT", [D, TOK], BF16)
    DV = dscr("DV", [TOK, D], BF16)
    MGT = dscr("MGT", [2 * D, HALF], BF16)
    YGT = dscr("YGT", [D, HALF], BF16)
    YDT = dscr("YDT", [D, HALF], BF16)
    X2T = dscr("X2T", [D, HALF], F32)

    def MM(out, lhsT, rhs, start, stop, R, W):
        P.op("pe", lambda e: e.matmul(out, lhsT, rhs, start=start, stop=stop), R, W)

    def TR(out, in_, idn, R, W):
        P.op("pe", lambda e: e.transpose(out, in_, idn), R, W)

    def ACT(out, in_, func, R, W, bias=None, scale=None):
        kw = {}
        if bias is not None:
            kw["bias"] = bias
        if scale is not None:
            kw["scale"] = scale
        P.op("act", lambda e: e.activation(out=out, in_=in_, func=func, **kw), R, W)

    def TS(eng, out, in0, s1, s2, op0, op1, R, W):
        if op1 is None:
            P.op(eng, lambda e: e.tensor_scalar(out=out, in0=in0, scalar1=s1, scalar2=None, op0=op0), R, W)
        else:
            P.op(eng, lambda e: e.tensor_scalar(out=out, in0=in0, scalar1=s1, scalar2=s2, op0=op0, op1=op1), R, W)

    def STT(out, in0, scalar, in1, op0, op1, R, W):
        P.op("dve", lambda e: e.scalar_tensor_tensor(out=out, in0=in0, scalar=scalar, in1=in1, op0=op0, op1=op1), R, W)

    def TT(eng, out, in0, in1, op, R, W):
        P.op(eng, lambda e: e.tensor_tensor(out=out, in0=in0, in1=in1, op=op), R, W)

    def RECIP(out, in_, R, W):
        P.op("dve", lambda e: e.reciprocal(out=out, in_=in_), R, W)

    def CP(eng, out, in_, R, W):
        if eng == "act":
            P.op("act", lambda e: e.activation(out=out, in_=in_, func=AF.Identity), R, W)
        else:
            P.op(eng, lambda e: e.tensor_copy(out=out, in_=in_), R, W)

    def MEMSET(eng, ap, v, W):
        P.op(eng, lambda e: e.memset(ap, v), (), W)

    def DMA(q, out, in_, R, W):
        P.dma(q, lambda e: e.dma_start(out=out, in_=in_), R, W)

    from contextlib import ExitStack
    es = ExitStack()

    def sb(name, shape, dt):
        return es.enter_context(nc.sbuf_tensor(name, list(shape), dt))

    def pst(name, shape, dt=F32):
        return es.enter_context(nc.psum_tensor(name, list(shape), dt))

    cons = ExitStack()

    def csb(name, shape, dt):
        return cons.enter_context(nc.sbuf_tensor(name, list(shape), dt))

    modt = csb("modt", [128, 72, 2], F32)
    gm = csb("gm", [128, 3, 8, 2], F32)
    shf = csb("shf", [128, 3, 8, 2], F32)
    gte = csb("gte", [128, 3, 8, 2], F32)
    ng = csb("ng", [128, 3, 8], F32)
    fg = csb("fg", [128, 8], F32)
    gng = csb("gng", [128, 8], F32)
    dng = csb("dng", [128, 8], F32)
    ones_f = csb("ones_f", [128, 512], F32)
    ones_b = csb("ones_b", [128, 128], BF16)
    id_b = csb("id_b", [128, 128], BF16)
    perm_b = csb("perm_b", [128, 128], BF16)
    mask_f = csb("mask_f", [128, 2, 128], F32)
    gw2s = csb("gw2s", [16, 2, 512], F32)
    gb = csb("gb", [128, 2, 4], F32)
    ngb = csb("ngb", [128, 2, 4], F32)
    lam_s = csb("lam_s", [128, 256], F32)
    lam_w = csb("lam_w", [128, 8], F32)
    eps_t = csb("eps_t", [128, 1], F32)

    DMA("sp", ng[:], ngT, (), ["ng"])
    DMA("sp", fg[:], fgT, (), ["fg"])
    DMA("sp", gng[:], gngT, (), ["gng"])
    DMA("sp", dng[:], dngT, (), ["dng"])
    DMA("sp", mask_f[:], masks, (), ["mask_f"])
    DMA("sp", gb[:], gbT, (), ["gb"])
    DMA("sp", lam_s[:], lamT, (), ["lam_s"])
    for d_ in range(2):
        DMA("sp", gw2s[:, d_, :], gw2[d_], (), [("gw2s", d_)])
    DMA("pool", id_b[:], ident, (), ["id_b"])
    DMA("pool", perm_b[:], permM, (), ["perm_b"])
    MEMSET("dve", ones_f[:], 1.0, ["ones_f"])
    MEMSET("dve", ones_b[:], 1.0, ["ones_b"])
    MEMSET("dve", eps_t[:], EPS, ["eps_t"])
    TS("dve", ngb[:], gb[:], -1.0, None, ALU.mult, None, ["gb"], ["ngb"])
    TT("dve", lam_s[:, 0:64], lam_s[:, 0:64], lam_s[:, 64:128], ALU.mult, ["lam_s"], ["lam_s"])
    TT("dve", lam_s[:, 128:192], lam_s[:, 128:192], lam_s[:, 192:256], ALU.mult, ["lam_s"], ["lam_s"])
    P.op("dve", lambda e: e.reduce_sum(out=lam_w[:, 0:1], in_=lam_s[:, 0:64], axis=mybir.AxisListType.X), ["lam_s"], ["lam_w"])
    P.op("dve", lambda e: e.reduce_sum(out=lam_w[:, 1:2], in_=lam_s[:, 128:192], axis=mybir.AxisListType.X), ["lam_s"], ["lam_w"])
    ACT(lam_w[:, 2:4], lam_w[:, 0:2], AF.Exp, ["lam_w"], ["lam_w"])
    TT("dve", lam_w[:, 4:5], lam_w[:, 3:4], lam_w[:, 2:3], ALU.subtract, ["lam_w"], ["lam_w"])
    TS("dve", lam_w[:, 4:5], lam_w[:, 4:5], -0.2, None, ALU.add, None, ["lam_w"], ["lam_w"])
    neglam = lam_w[:, 4:5]

    es = ExitStack()
    sc = sb("sc", [128, 8, 2], F32)
    bTa = sb("bTa", [128, 72], F32)
    wa = [sb("wa%d" % i, [128, 8, 1024], F32) for i in range(2)]
    ps_m = [pst("ps_m%d" % i, [128, 16]) for i in range(2)]
    DMA("sp", sc[:], cT, (), ["sc"])
    DMA("sp", bTa[:], bT_ada, (), ["bTa"])
    ACT(sc[:], sc[:], AF.Silu, ["sc"], ["sc"])
    w_ada_v = w_ada.rearrange("(kc p) n -> p kc n", p=128)
    for m in range(9):
        wb_ = wa[m % 2]
        for kc in range(8):
            DMA("sp", wb_[:, kc, :], w_ada_v[:, kc, m * 1024:(m + 1) * 1024], (), [("wa", m % 2, kc)])
        pm = ps_m[m % 2]
        for c in range(8):
            for kc in range(8):
                MM(pm[:, 2 * c:2 * c + 2], wb_[:, kc, c * 128:(c + 1) * 128], sc[:, kc, :], kc == 0, kc == 7,
                   [("wa", m % 2, kc), "sc"], [("ps_m", m % 2)])
        for j in range(2):
            TT("dve", modt[:, m * 8:(m + 1) * 8, j], pm[:, j:16:2], bTa[:, m * 8:(m + 1) * 8], ALU.add,
               [("ps_m", m % 2), "bTa"] + [("wa", m % 2, kc) for kc in range(8)], ["modt"])
    for i in range(3):
        for j in range(2):
            STT(gm[:, i, :, j], modt[:, (3 * i + 1) * 8:(3 * i + 2) * 8, j], 1.0, ng[:, i, :], ALU.add, ALU.mult,
                ["modt", "ng"], ["gm"])
            CP("dve", shf[:, i, :, j], modt[:, (3 * i) * 8:(3 * i + 1) * 8, j], ["modt"], ["shf"])
            TS("dve", gte[:, i, :, j], modt[:, (3 * i + 2) * 8:(3 * i + 3) * 8, j], (1.0 if i == 1 else 0.5), None,
               ALU.mult, None, ["modt"], ["gte"])
    es.close()

    def load_w_bf16(dst, src_v, nk, ncols, resname):
        step = 1024
        for kc in range(nk):
            for c0 in range(0, ncols, step):
                c1 = min(ncols, c0 + step)
                DMA("pool", dst[:, kc, c0:c1], src_v[:, kc, c0:c1], (), [(resname, kc)])

    def rms_rstd(key, rres, src_chunks, nch, W_, sq, ps_stat, rstd, inv_n, Rsrc):
        for c in range(nch):
            s_ = sq[c % 2]
            ACT(s_[:, :W_], src_chunks(c), AF.Square, Rsrc(c), [(key + "sq", c % 2)])
            MM(ps_stat[:, :W_], ones_f[:, 0:128], s_[:, :W_], c == 0, c == nch - 1,
               [(key + "sq", c % 2), "ones_f"], [key + "ps_stat"])
        ACT(rstd[:, :W_], ps_stat[:, :W_], AF.Sqrt, [key + "ps_stat", "eps_t"], [rres], bias=eps_t[:, 0:1], scale=inv_n)
        RECIP(rstd[:, :W_], rstd[:, :W_], [rres], [rres])

    def ffn_phase(tag, w1d, w3d, w2d, tiles, TW, src_T, li, post, src_deps):
        nonlocal es
        es = ExitStack()
        w1b = sb(tag + "w1b", [128, 8, FF], BF16)
        w3b = sb(tag + "w3b", [128, 8, FF], BF16)
        w2b = sb(tag + "w2b", [128, NF, D], BF16)
        load_w_bf16(w1b, w1d.rearrange("(kc p) n -> p kc n", p=128), 8, FF, tag + "w1b")
        load_w_bf16(w3b, w3d.rearrange("(kc p) n -> p kc n", p=128), 8, FF, tag + "w3b")
        load_w_bf16(w2b, w2d.rearrange("(kc p) n -> p kc n", p=128), NF, D, tag + "w2b")
        xt = [sb(tag + "xt%d" % i, [128, 8, TW], F32) for i in range(2)]
        hb = sb(tag + "hb", [128, 8, TW], BF16)
        ab = sb(tag + "ab", [128, NF, TW], BF16)
        sq = [sb(tag + "sq%d" % i, [128, TW], F32) for i in range(2)]
        rstd = sb(tag + "rstd", [128, TW], F32)
        rstd2 = sb(tag + "rstd2", [128, TW], F32)
        tmp = [sb(tag + "tmp%d" % i, [128, TW], F32) for i in range(2)]
        su = [sb(tag + "su%d" % i, [128, TW], F32) for i in range(2)]
        h2 = sb(tag + "h2", [128, 8, TW], BF16) if li == 0 else sb(tag + "h2", [128, 4, TW], F32)
        ps_u = [pst(tag + "ps_u%d" % i, [128, TW]) for i in range(2)]
        ps_v = [pst(tag + "ps_v%d" % i, [128, TW]) for i in range(2)]
        ps_y = [pst(tag + "ps_y%d" % i, [128, TW]) for i in range(2)]
        ps_stat = pst(tag + "ps_stat", [128, TW])
        src_v = src_T.rearrange("(c p) n -> p c n", p=128)

        def prologue(t):
            c0, W_, j, own = tiles[t]
            x_ = xt[t % 2]
            DMA("sp", x_[:, :, :W_], src_v[:, :, c0:c0 + W_], src_deps(t), [(tag + "xt", t % 2, c) for c in range(8)])
            rms_rstd(tag, tag + "arstd", lambda c: x_[:, c, :W_], 8, W_, sq, ps_stat, rstd, 1.0 / D,
                     lambda c: [(tag + "xt", t % 2, c)])
            for c in range(8):
                tm = tmp[c % 2]
                STT(tm[:, :W_], x_[:, c, :W_], gm[:, li, c, j:j + 1], rstd[:, :W_], ALU.mult, ALU.mult,
                    [(tag + "xt", t % 2, c), "gm", tag + "arstd"], [(tag + "tmp", c % 2)])
                ACT(hb[:, c, :W_], tm[:, :W_], AF.Identity, [(tag + "tmp", c % 2), "shf"], [(tag + "hb", c)],
                    bias=shf[:, li, c, j:j + 1])

        prologue(0)
        for t in range(len(tiles)):
            c0, W_, j, own = tiles[t]
            x_ = xt[t % 2]
            for f in range(NF):
                pu = ps_u[f % 2]; pv = ps_v[f % 2]
                for kc in range(8):
                    MM(pu[:, :W_], w1b[:, kc, f * 128:(f + 1) * 128], hb[:, kc, :W_], kc == 0, kc == 7,
                       [(tag + "w1b", kc), (tag + "hb", kc)], [(tag + "ps_u", f % 2)])
                for kc in range(8):
                    MM(pv[:, :W_], w3b[:, kc, f * 128:(f + 1) * 128], hb[:, kc, :W_], kc == 0, kc == 7,
                       [(tag + "w3b", kc), (tag + "hb", kc)], [(tag + "ps_v", f % 2)])
                s_ = su[f % 2]
                ACT(s_[:, :W_], pu[:, :W_], AF.Silu, [(tag + "ps_u", f % 2)], [(tag + "su", f % 2)])
                TT("dve", ab[:, f, :W_], s_[:, :W_], pv[:, :W_], ALU.mult,
                   [(tag + "su", f % 2), (tag + "ps_v", f % 2)], [(tag + "ab", f)])
            if t + 1 < len(tiles):
                prologue(t + 1)
            for c in range(8):
                py = ps_y[c % 2]
                for f in range(NF):
                    MM(py[:, :W_], w2b[:, f, c * 128:(c + 1) * 128], ab[:, f, :W_], f == 0, f == NF - 1,
                       [(tag + "w2b", f), (tag + "ab", f)], [(tag + "ps_y", c % 2)])
                STT(x_[:, c, :W_], py[:, :W_], gte[:, li, c, j:j + 1], x_[:, c, :W_], ALU.mult, ALU.add,
                    [(tag + "ps_y", c % 2), "gte", (tag + "xt", t % 2, c)], [(tag + "xt", t % 2, c)])
            post(t, x_, W_, sq, ps_stat, rstd2, tmp, h2, tag)
        es.close()

    def finish():
        P.emit()
        cons.close()
        return nc
    if stop_after < 1:
        return finish()
    P.barrier()
    TW1 = 256
    tiles1 = [(0, CTX, 1, None)]
    for i in range(SEQ // TW1):
        tiles1.append((CTX + i * TW1, TW1, 0, (i * TW1 if i * TW1 < HALF else None)))
    H2T_v = H2T.rearrange("(c p) n -> p c n", p=128)
    X1T_v = X1T.rearrange("(c p) n -> p c n", p=128)

    def post1(t, x_, W_, sq, ps_stat, rstd2, tmp, h2, tag):
        c0, _, j, own = tiles1[t]
        if own is not None:
            DMA("sp", X1T_v[:, :, own:own + W_], x_[:, :, :W_], [(tag + "xt", t % 2, c) for c in range(8)], [("X1T", own)])
        rms_rstd(tag, tag + "brstd", lambda c: x_[:, c, :W_], 8, W_, sq, ps_stat, rstd2, 1.0 / D,
                 lambda c: [(tag + "xt", t % 2, c)])
        for c in range(8):
            tm = tmp[c % 2]
            STT(tm[:, :W_], x_[:, c, :W_], gm[:, 1, c, j:j + 1], rstd2[:, :W_], ALU.mult, ALU.mult,
                [(tag + "xt", t % 2, c), "gm", tag + "brstd"], [(tag + "tmp", c % 2)])
            ACT(h2[:, c, :W_], tm[:, :W_], AF.Identity, [(tag + "tmp", c % 2), "shf"], [(tag + "h2", c)],
                bias=shf[:, 1, c, j:j + 1])
        DMA("sp", H2T_v[:, :, c0:c0 + W_], h2[:, :, :W_], [(tag + "h2", c) for c in range(8)], [("H2T", c0)])

    if "p1" not in SKIP:
        ffn_phase("f1", f1w1, f1w3, f1w2, tiles1, TW1, xT, 0, post1, lambda t: ())

    if stop_after < 2:
        return finish()
    P.barrier()
    TW2 = 256
    tiles2 = [(0, CTX, "ctx", None)]
    for i in range(SEQ // TW2):
        tiles2.append((CTX + i * TW2, TW2, ("own" if i * TW2 < HALF else "oth"), i * TW2))
    GKT_v = GKT.rearrange("(c p) n -> p c n", p=128)
    GQT_v = GQT.rearrange("(c p) n -> p c n", p=128)
    GGT_v = GGT.rearrange("(c p) n -> p c n", p=128)
    DQT_v = DQT.rearrange("(c p) n -> p c n", p=128)
    DKT_v = DKT.rearrange("(c p) n -> p c n", p=128)
    MGT_v = MGT.rearrange("(c p) n -> p c n", p=128)
    SPT_v = SPT.rearrange("d (c p) n -> d p c n", p=128)
    w_in_v = w_in.rearrange("(kc p) n -> p kc n", p=128)

    def proj_pass(pid):
        nonlocal es
        es = ExitStack()
        if pid == 0:
            segs = [(O_GK, 512), (O_GV, 1024), (O_LR, 32), (O_DK, 1024), (O_DV, 1024)]
        else:
            segs = [(O_GQ, 512), (O_GG, 1024), (O_DQ, 1024), (O_MG, 2048)]
        cmap = {}
        tot = 0
        for (s0, n) in segs:
            cmap[s0] = tot
            tot += n
        wtag = "winb%d" % pid
        winb = sb(wtag, [128, 8, tot], BF16)
        for (s0, n) in segs:
            if n == 32 and 'lr' in SKIP:
                continue
            for kc in range(8):
                for c0 in range(0, n, 1024):
                    c1 = min(n, c0 + 1024)
                    DMA("pool", winb[:, kc, cmap[s0] + c0:cmap[s0] + c1], w_in_v[:, kc, s0 + c0:s0 + c1], (), [(wtag, kc)])
        pt = "p2%d" % pid
        h2t = [sb(pt + "h2t%d" % i, [128, 8, TW2], BF16) for i in range(2)]
        ps_p = [pst(pt + "ps_p%d" % i, [128, 512]) for i in range(4)]
        pcount = [0]

        def proj_fm(h_, hi, col0, W_):
            k = pcount[0] % 4
            pcount[0] += 1
            pp = ps_p[k]
            for kc in range(8):
                MM(pp[:, :W_], winb[:, kc, col0:col0 + 128], h_[:, kc, :W_], kc == 0, kc == 7,
                   [(wtag, kc), (pt + "h2t", hi)], [(pt + "ps_p", k)])
            return pp, (pt + "ps_p", k)

        stg = {}
        names = (("gk", 4), ("dk", 8)) if pid == 0 else (("gq", 4), ("gg", 8), ("dq", 8), ("mg", 16))
        for nm, nchk in names:
            stg[nm] = sb("stg_" + nm, [128, nchk, TW2], BF16)
        qb = [sb(pt + "qb%d" % i, [128, TW2], BF16) for i in range(2)]
        r1 = [sb(pt + "r1_%d" % i, [128, TW2], F32) for i in range(2)]
        r2 = [sb(pt + "r2_%d" % i, [128, TW2], F32) for i in range(2)]
        cs = [sb(pt + "cs%d" % i, [128, 2, TW2], F32) for i in range(2)]
        ps_r = [pst(pt + "ps_r%d" % i, [128, TW2]) for i in range(2)]
        if pid == 0:
            stv = [sb("stv%d" % i, [128, TW2 // 128, 512], BF16) for i in range(2)]
            lrs = [sb("lrs%d" % i, [16, TW2], F32) for i in range(2)]
            spb = [sb("spb%d" % i, [128, 4, TW2], F32) for i in range(2)]
            ex = [sb("ex%d" % i, [128, TW2], F32) for i in range(2)]
            ps_l = [pst("ps_l%d" % i, [128, TW2]) for i in range(2)]
        ropec = [0]
        tcount = 0
        mytiles = [tl for tl in tiles2 if not (pid == 1 and tl[2] != "own")]

        def load_tile(ti):
            c0_, Wl, kind_, l0_ = mytiles[ti]
            hi_ = ti % 2
            DMA("sp", h2t[hi_][:, :, :Wl], H2T_v[:, :, c0_:c0_ + Wl], [("H2T", c0_)], [(pt + "h2t", hi_)])
            if kind_ != "ctx":
                DMA("sp", cs[hi_][:, 0, :Wl], cosT[:, l0_:l0_ + Wl], (), [(pt + "cs", hi_, 0)])
                DMA("sp", cs[hi_][:, 1, :Wl], sinT[:, l0_:l0_ + Wl], (), [(pt + "cs", hi_, 1)])

        load_tile(0)
        for (c0, W_, kind, l0) in mytiles:
            hi = tcount % 2
            if tcount + 1 < len(mytiles):
                load_tile(tcount + 1)
            tcount += 1
            h_ = h2t[hi]
            latent = kind != "ctx"

            def rope_evac(pp, pres, dst, dres):
                i = ropec[0] % 2
                ropec[0] += 1
                if not latent or 'norope' in SKIP:
                    CP("act", dst, pp[:, :W_], [pres], [dres])
                    return
                if 'rope_noperm' in SKIP:
                    if 'nocs' in SKIP:
                        TT("dve", r1[i][:, :W_], pp[:, :W_], ones_f[:, :W_], ALU.mult, [pres, "ones_f"], [(pt + "r1", i)])
                    elif 'cs_nodep' in SKIP:
                        TT("dve", r1[i][:, :W_], pp[:, :W_], cs[hi][:, 0, :W_], ALU.mult, [pres], [(pt + "r1", i)])
                    else:
                        TT("dve", r1[i][:, :W_], pp[:, :W_], cs[hi][:, 0, :W_], ALU.mult, [pres, (pt + "cs", hi, 0)], [(pt + "r1", i)])
                    CP("dve", dst, r1[i][:, :W_], [(pt + "r1", i)], [dres])
                    return
                CP("act", qb[i][:, :W_], pp[:, :W_], [pres], [(pt + "qb", i)])
                MM(ps_r[i][:, :W_], perm_b[:], qb[i][:, :W_], True, True, ["perm_b", (pt + "qb", i)], [(pt + "ps_r", i)])
                if 'rope_nodve' in SKIP:
                    CP("act", dst, ps_r[i][:, :W_], [(pt + "ps_r", i)], [dres])
                    return
                if 'r1pp' in SKIP:
                    TT("dve", r1[i][:, :W_], pp[:, :W_], cs[hi][:, 0, :W_], ALU.mult, [pres, (pt + "cs", hi, 0)], [(pt + "r1", i)])
                else:
                    TT("dve", r1[i][:, :W_], qb[i][:, :W_], cs[hi][:, 0, :W_], ALU.mult, [(pt + "qb", i), (pt + "cs", hi, 0)], [(pt + "r1", i)])
                TT("dve", r2[i][:, :W_], ps_r[i][:, :W_], cs[hi][:, 1, :W_], ALU.mult, [(pt + "ps_r", i), (pt + "cs", hi, 1)], [(pt + "r2", i)])
                TT(("pool" if "usepool" in SKIP else "dve"), dst, r1[i][:, :W_], r2[i][:, :W_], ALU.add, [(pt + "r1", i), (pt + "r2", i)], [dres])

            if pid == 0:
                for c in range(4):
                    pp, pres = proj_fm(h_, hi, cmap[O_GK] + c * 128, W_)
                    CP("act", stg["gk"][:, c, :W_], pp[:, :W_], [pres], [("stg_gk", c)])
                DMA("sp", GKT_v[:, :, c0:c0 + W_], stg["gk"][:, :, :W_], [("stg_gk", c) for c in range(4)], [("GKT", c0)])
                for c in ([] if 'dk' in SKIP else range(8)):
                    pp, pres = proj_fm(h_, hi, cmap[O_DK] + c * 128, W_)
                    rope_evac(pp, pres, stg["dk"][:, c, :W_], ("stg_dk", c))
                if "dk" not in SKIP:
                    DMA("sp", DKT_v[:, :, c0:c0 + W_], stg["dk"][:, :, :W_], [("stg_dk", c) for c in range(8)], [("DKT", c0)])
                for (ocol, dst, dname) in ([] if "v" in SKIP else ((cmap[O_GV], GV, "GV"), (cmap[O_DV], DV, "DV"))):
                    for cb in range(2):
                        k = pcount[0] % 2
                        sv = stv[k]
                        for tt_ in range(W_ // 128):
                            kk = pcount[0] % 4
                            pcount[0] += 1
                            pp = ps_p[kk]
                            for kc in range(8):
                                MM(pp[:, :512], h_[:, kc, tt_ * 128:(tt_ + 1) * 128], winb[:, kc, ocol + cb * 512:ocol + (cb + 1) * 512],
                                   kc == 0, kc == 7, [(wtag, kc), (pt + "h2t", hi)], [(pt + "ps_p", kk)])
                            CP(("act" if tt_ % 2 == 0 else "dve"), sv[:, tt_, :], pp[:, :512], [(pt + "ps_p", kk)], [("stv", k, tt_)])
                        dv_ = dst[c0:c0 + W_, cb * 512:(cb + 1) * 512].rearrange("(n p) e -> p n e", p=128)
                        DMA("sp", dv_, sv[:, :W_ // 128, :], [("stv", k, tt_) for tt_ in range(W_ // 128)], [(dname, c0, cb)])
                for d_ in ([] if 'lr' in SKIP else range(2)):
                    i = d_
                    pl = ps_l[i]
                    lc = cmap[O_LR] + 16 * d_
                    for kc in range(8):
                        MM(pl[0:16, :W_], winb[:, kc, lc:lc + 16], h_[:, kc, :W_], kc == 0, kc == 7,
                           [(wtag, kc), (pt + "h2t", hi)], [("ps_l", i)])
                    CP("dve", lrs[i][:, :W_], pl[0:16, :W_], [("ps_l", i)], [("lrs", i)])
                    for c in range(4):
                        kk = pcount[0] % 4
                        pcount[0] += 1
                        pp = ps_p[kk]
                        MM(pp[:, :W_], gw2s[:, d_, c * 128:(c + 1) * 128], lrs[i][:, :W_], True, True,
                           [("gw2s", d_), ("lrs", i)], [(pt + "ps_p", kk)])
                        e_ = ex[c % 2]
                        ACT(e_[:, :W_], pp[:, :W_], AF.Exp, [(pt + "ps_p", kk), "ngb"], [("ex", c % 2)], bias=ngb[:, d_, c:c + 1], scale=-1.0)
                        ACT(spb[i][:, c, :W_], e_[:, :W_], AF.Ln, [("ex", c % 2)], [("spb", i, c)], bias=1.0)
                    DMA("sp", SPT_v[d_][:, :, c0:c0 + W_], spb[i][:, :, :W_], [("spb", i, c) for c in range(4)], [("SPT", d_, c0)])
            else:
                for c in range(4):
                    pp, pres = proj_fm(h_, hi, cmap[O_GQ] + c * 128, W_)
                    ACT(stg["gq"][:, c, :W_], pp[:, :W_], AF.Identity, [pres], [("stg_gq", c)], scale=128.0 ** -0.5)
                DMA("sp", GQT_v[:, :, l0:l0 + W_], stg["gq"][:, :, :W_], [("stg_gq", c) for c in range(4)], [("GQT", l0)])
                for c in range(8):
                    pp, pres = proj_fm(h_, hi, cmap[O_GG] + c * 128, W_)
                    ACT(stg["gg"][:, c, :W_], pp[:, :W_], AF.Silu, [pres], [("stg_gg", c)])
                DMA("sp", GGT_v[:, :, l0:l0 + W_], stg["gg"][:, :, :W_], [("stg_gg", c) for c in range(8)], [("GGT", l0)])
                for c in range(8):
                    pp, pres = proj_fm(h_, hi, cmap[O_DQ] + c * 128, W_)
                    rope_evac(pp, pres, stg["dq"][:, c, :W_], ("stg_dq", c))
                DMA("sp", DQT_v[:, :, l0:l0 + W_], stg["dq"][:, :, :W_], [("stg_dq", c) for c in range(8)], [("DQT", l0)])
                for c in range(16):
                    pp, pres = proj_fm(h_, hi, cmap[O_MG] + c * 128, W_)
                    ACT(stg["mg"][:, c, :W_], pp[:, :W_], AF.Sigmoid, [pres], [("stg_mg", c)])
                DMA("sp", MGT_v[:, :, l0:l0 + W_], stg["mg"][:, :, :W_], [("stg_mg", c) for c in range(16)], [("MGT", l0)])
        es.close()

    proj_pass(0)
    P.barrier()
    if 'pass1' not in SKIP:
        proj_pass(1)

    def dram_deps(name, a, b, step):
        return [(name, cc) for cc in range(a, b, step)]

    if stop_after < 3:
        return finish()
    P.barrier()
    es = ExitStack()
    NCK = TOK // 128
    NOWN = HALF // 128
    qT = sb("g_qT", [128, HALF], BF16)
    kT = sb("g_kT", [128, TOK], BF16)
    vt = sb("g_v", [128, NCK, 256], BF16)
    Pst = [sb("g_P0", [128, CTX + HALF + 1], F32), sb("g_P1", [128, TOK + 1], F32)]
    spl = [sb("g_sp%d" % i, [128, 512], F32) for i in range(2)]
    oT = sb("g_oT", [128, 2, HALF], F32)
    S = sb("g_S", [128, 256], F32)
    Stmp = sb("g_Stmp", [128, 256], F32)
    Sb = sb("g_Sb", [128, 256], BF16)
    E1 = [sb("g_E1_%d" % i, [128, 128], F32) for i in range(2)]
    E2 = [sb("g_E2_%d" % i, [128, 128], F32) for i in range(2)]
    qe = [sb("g_qe%d" % i, [128, 128], BF16) for i in range(2)]
    keT = [sb("g_keT%d" % i, [128, 128], BF16) for i in range(2)]
    ket = [sb("g_ket%d" % i, [128, 128], BF16) for i in range(2)]
    atb = [sb("g_atb%d" % i, [128, 128], BF16) for i in range(2)]
    bcol = [sb("g_bcol%d" % i, [128, 4], F32) for i in range(4)]
    ggt = [sb("g_ggt%d" % i, [128, 2, 512], BF16) for i in range(1)] * 2
    ygs = [sb("g_ygs%d" % i, [128, 2, 512], BF16) for i in range(1)] * 2
    gsq = [sb("g_sq%d" % i, [128, 512], F32) for i in range(2)]
    grs = sb("g_rs", [128, 512], F32)
    gtm = [sb("g_tm%d" % i, [128, 512], F32) for i in range(2)]
    ps_at = [pst("ps_at%d" % i, [128, 512])[:, 0:128] for i in range(2)]
    ps_tr = [pst("ps_tr%d" % i, [128, 1024], BF16)[:, 0:128] for i in range(2)]
    ps_o = [pst("ps_o%d" % i, [128, 4, 128])[:, 0:2, :] for i in range(2)]
    ps_ds = pst("ps_ds", [128, 512])[:, 0:256]
    ps_gs = pst("ps_gs", [128, 512])
    GV_v = GV.rearrange("(n p) e -> p n e", p=128)
    YGT_v = YGT.rearrange("(c p) n -> p c n", p=128)
    cc_ = [0]
    for h in range(4):
        DMA("sp", qT[:], GQT[h * 128:(h + 1) * 128, :], dram_deps("GQT", 0, HALF, TW2), ["g_qT"])
        DMA("sp", kT[:], GKT[h * 128:(h + 1) * 128, :], dram_deps("GKT", 0, CTX, CTX) + dram_deps("GKT", CTX, TOK, TW2), ["g_kT"])
        vdeps = [("GV", 0, cb) for cb in range(2)] + [("GV", c0, cb) for c0 in range(CTX, TOK, TW2) for cb in range(2)]
        for n0 in range(0, NCK, 16):
            n1 = min(NCK, n0 + 16)
            DMA("sp", vt[:, n0:n1, :], GV_v[:, n0:n1, h * 256:(h + 1) * 256], vdeps, [("g_v", n0)])
        vres = [("g_v", n0) for n0 in range(0, NCK, 16)]
        for d_ in range(2):
            MEMSET("dve", Pst[d_][:, 0:1], 0.0, [("g_P", d_)])
            NT_ = (CTX + HALF) if d_ == 0 else TOK
            for n0 in range(0, NT_, 512):
                n1 = min(NT_, n0 + 512)
                s_ = spl[(n0 // 512) % 2]
                spdeps = [("SPT", d_, c0) for c0 in ([0] + list(range(CTX, TOK, TW2)))]
                DMA("sp", s_[:, :n1 - n0], SPT[d_, h * 128:(h + 1) * 128, n0:n1], spdeps, [("g_sp", (n0 // 512) % 2)])
                P.op("dve", (lambda d_=d_, n0=n0, n1=n1, s_=s_: (lambda e: e.tensor_tensor_scan(
                    out=Pst[d_][:, n0 + 1:n1 + 1], data0=ones_f[:, :n1 - n0], data1=s_[:, :n1 - n0],
                    initial=Pst[d_][:, n0:n0 + 1], op0=ALU.mult, op1=ALU.add)))(),
                    [("g_sp", (n0 // 512) % 2), "ones_f", ("g_P", d_)], [("g_P", d_)])

        def chunk(d_, ck, full, first_out):
            i = cc_[0] % 2
            cc_[0] += 1
            a = ck * 128
            Pd = Pst[d_]
            bc = bcol[cc_[0] % 4]
            bres = ("g_bcol", cc_[0] % 4)
            if d_ == 0:
                sgn = -1.0 / 16.0
                pcols = Pd[:, a + 1:a + 129]
                bsrc = Pd[:, a:a + 1]
                elcol = 127
            else:
                sgn = 1.0 / 16.0
                pcols = Pd[:, a:a + 128]
                bsrc = Pd[:, a + 128:a + 129]
                elcol = 0
            TS("dve", bc[:, 0:1], bsrc, -sgn, None, ALU.mult, None, [("g_P", d_)], [bres])
            TS("dve", bc[:, 1:2], bsrc, sgn, None, ALU.mult, None, [("g_P", d_)], [bres])
            ACT(E1[i][:], pcols, AF.Exp, [("g_P", d_), bres], [("g_E1", i)], bias=bc[:, 0:1], scale=sgn)
            ACT(E2[i][:], pcols, AF.Exp, [("g_P", d_), bres], [("g_E2", i)], bias=bc[:, 1:2], scale=-sgn)
            TT("dve", keT[i][:], kT[:, a:a + 128], E2[i][:], ALU.mult, ["g_kT", ("g_E2", i)], [("g_keT", i)])
            TR(ps_tr[i][:], keT[i][:], id_b[:], [("g_keT", i), "id_b"], [("ps_tr", i)])
            CP("act", ket[i][:], ps_tr[i][:], [("ps_tr", i)], [("g_ket", i)])
            if full:
                lo = a - CTX
                TT("dve", qe[i][:], qT[:, lo:lo + 128], E1[i][:], ALU.mult, ["g_qT", ("g_E1", i)], [("g_qe", i)])
                MM(ps_at[i][:], keT[i][:], qe[i][:], True, True, [("g_keT", i), ("g_qe", i)], [("ps_at", i)])
                TT("dve", atb[i][:], ps_at[i][:], mask_f[:, d_, :], ALU.mult, [("ps_at", i), "mask_f"], [("g_atb", i)])
                for ec in range(2):
                    MM(ps_o[i][:, ec, :], Sb[:, ec * 128:(ec + 1) * 128], qe[i][:], True, False,
                       ["g_Sb", ("g_qe", i)], [("ps_o", i)])
                    MM(ps_o[i][:, ec, :], vt[:, ck, ec * 128:(ec + 1) * 128], atb[i][:], False, True,
                       vres + [("g_atb", i)], [("ps_o", i)])
                if first_out:
                    CP("act", oT[:, :, lo:lo + 128], ps_o[i][:], [("ps_o", i)], [("g_oT", lo)])
                else:
                    TT("dve", oT[:, :, lo:lo + 128], oT[:, :, lo:lo + 128], ps_o[i][:], ALU.add,
                       [("ps_o", i), ("g_oT", lo)], [("g_oT", lo)])
            MM(ps_ds[:], ket[i][:], vt[:, ck, :], True, True, [("g_ket", i)] + vres, ["ps_ds"])
            TT("dve", Stmp[:], S[:], ps_ds[:], ALU.add, ["g_S", "ps_ds"], ["g_Stmp"])
            TS("dve", S[:], Stmp[:], E1[i][:, elcol:elcol + 1], None, ALU.mult, None, ["g_Stmp", ("g_E1", i)], ["g_S"])
            P.op("act", (lambda i=i, elcol=elcol: (lambda e: e.activation(out=Sb[:], in_=Stmp[:], func=AF.Copy,
                                                                         scale=E1[i][:, elcol:elcol + 1])))(),
                 ["g_Stmp", ("g_E1", i)], ["g_Sb"])

        MEMSET("dve", S[:], 0.0, ["g_S"])
        MEMSET("dve", Sb[:], 0.0, ["g_Sb"])
        for ck in range(CTX // 128):
            chunk(0, ck, False, False)
        for ck in range(CTX // 128, CTX // 128 + NOWN):
            chunk(0, ck, True, True)
        MEMSET("dve", S[:], 0.0, ["g_S"])
        MEMSET("dve", Sb[:], 0.0, ["g_Sb"])
        for ck in range(CTX // 128 - 1, -1, -1):
            chunk(1, ck, False, False)
        for ck in range(NCK - 1, CTX // 128 + NOWN - 1, -1):
            chunk(1, ck, False, False)
        for ck in range(CTX // 128 + NOWN - 1, CTX // 128 - 1, -1):
            chunk(1, ck, True, False)
        for q0 in range(0, HALF, 512):
            k = 0
            DMA("sp", ggt[k][:], GGT_v[:, 2 * h:2 * h + 2, q0:q0 + 512], dram_deps("GGT", q0, q0 + 512, TW2), [("g_ggt", k)])
            ores = [("g_oT", lo) for lo in range(q0, q0 + 512, 128)]
            for ec in range(2):
                ACT(gsq[ec][:], oT[:, ec, q0:q0 + 512], AF.Square, ores, [("g_sq", ec)])
                MM(ps_gs[:], ones_f[:, 0:128], gsq[ec][:], ec == 0, ec == 1, [("g_sq", ec), "ones_f"], ["ps_gs"])
            ACT(grs[:], ps_gs[:], AF.Sqrt, ["ps_gs", "eps_t"], ["g_rs"], bias=eps_t[:, 0:1], scale=1.0 / 256)
            RECIP(grs[:], grs[:], ["g_rs"], ["g_rs"])
            for ec in range(2):
                STT(gtm[ec][:], oT[:, ec, q0:q0 + 512], gng[:, 2 * h + ec:2 * h + ec + 1], grs[:], ALU.mult, ALU.mult,
                    ores + ["gng", "g_rs"], [("g_tm", ec)])
                TT("pool", ygs[k][:, ec, :], gtm[ec][:], ggt[k][:, ec, :], ALU.mult, [("g_tm", ec), ("g_ggt", k)], [("g_ygs", k, ec)])
            DMA("sp", YGT_v[:, 2 * h:2 * h + 2, q0:q0 + 512], ygs[k][:], [("g_ygs", k, 0), ("g_ygs", k, 1)], [("YGT", h, q0)])
    es.close()

    if stop_after < 4:
        return finish()
    P.barrier()
    es = ExitStack()
    QT_ = 512 if HALF >= 512 else HALF
    dq = [sb("d_q%d" % i, [128, HALF], BF16) for i in range(2)]
    dk = [sb("d_k%d" % i, [128, TOK], BF16) for i in range(2)]
    dv = [sb("d_v%d" % i, [128, NCK, 128], BF16) for i in range(2)]
    NPB = 4
    pb = [sb("d_p%d" % i, [128, 2, QT_], BF16) for i in range(NPB)]
    acc = sb("d_acc", [128, 2, QT_], F32)
    rz = [sb("d_rz%d" % i, [128, QT_], F32) for i in range(2)]
    t12 = [sb("d_t%d" % i, [128, QT_], F32) for i in range(2)]
    od = sb("d_o", [128, QT_], F32)
    dsq = sb("d_sq", [128, QT_], F32)
    drs = sb("d_rs", [128, QT_], F32)
    yds = [sb("d_yds%d" % i, [128, QT_], BF16) for i in range(2)]
    dgn = sb("d_gn", [128, 8], F32)
    TS("dve", dgn[:], dng[:], 0.8, None, ALU.mult, None, ["dng"], ["d_gn"])
    ps_s = [pst("ps_s%d" % i, [128, 2, QT_]) for i in range(2)]
    ps_av = [pst("ps_av%d" % c, [128, QT_]) for c in range(2)]
    ps_z = [pst("ps_z%d" % c, [128, QT_]) for c in range(2)]
    DV_v = DV.rearrange("(n p) e -> p n e", p=128)
    kdeps = dram_deps("DKT", 0, CTX, CTX) + dram_deps("DKT", CTX, TOK, TW2)
    vdeps_d = [("DV", 0, cb) for cb in range(2)] + [("DV", c0, cb) for c0 in range(CTX, TOK, TW2) for cb in range(2)]
    it = [0]
    for h in range(8):
        hb_ = h % 2
        DMA("sp", dq[hb_][:], DQT[h * 128:(h + 1) * 128, :], dram_deps("DQT", 0, HALF, TW2), [("d_q", hb_)])
        DMA("sp", dk[hb_][:], DKT[h * 128:(h + 1) * 128, :], kdeps, [("d_k", hb_)])
        for n0 in range(0, NCK, 16):
            n1 = min(NCK, n0 + 16)
            DMA("sp", dv[hb_][:, n0:n1, :], DV_v[:, n0:n1, h * 128:(h + 1) * 128], vdeps_d, [("d_v", hb_, n0)])
        vres = [("d_v", hb_, n0) for n0 in range(0, NCK, 16)]
        for q0 in range(0, HALF, QT_):
            def scores(kc):
                sbuf_i = kc % 2
                for c in range(2):
                    MM(ps_s[sbuf_i][:, c, :], dk[hb_][64 * c:64 * c + 64, kc * 128:(kc + 1) * 128],
                       dq[hb_][64 * c:64 * c + 64, q0:q0 + QT_], True, True,
                       [("d_k", hb_), ("d_q", hb_)], [("ps_s", sbuf_i)])

            MEMSET("dve", acc[:, 0, :], 0.0, [("d_acc", 0)])
            MEMSET("pool", acc[:, 1, :], 0.0, [("d_acc", 1)])
            def expo(kc):
                sbuf_i = kc % 2
                pi = kc % NPB
                P.op("act", (lambda sbuf_i=sbuf_i, pi=pi: (lambda e: e.activation(
                    out=pb[pi][:], in_=ps_s[sbuf_i][:], func=AF.Exp, scale=0.125)))(),
                    [("ps_s", sbuf_i)], [("d_p", pi)])

            def avz(kc):
                pi = kc % NPB
                for c in range(2):
                    MM(ps_av[c][:], dv[hb_][:, kc, :], pb[pi][:, c, :], kc == 0, kc == NCK - 1,
                       vres + [("d_p", pi)], [("ps_av", c)])
                if kc % 2 == 0:
                    MM(ps_z[1][:], ones_b[:], pb[pi][:, 1, :], kc == 0, False,
                       ["ones_b", ("d_p", pi)], [("ps_z", 1)])
                else:
                    TT("pool", acc[:, 1, :], acc[:, 1, :], pb[pi][:, 1, :], ALU.add, [("d_acc", 1), ("d_p", pi)], [("d_acc", 1)])
                TT("dve", acc[:, 0, :], acc[:, 0, :], pb[pi][:, 0, :], ALU.add, [("d_acc", 0), ("d_p", pi)], [("d_acc", 0)])

            scores(0)
            expo(0)
            if NCK > 1:
                scores(1)
            for kc in range(1, NCK):
                expo(kc)
                avz(kc - 1)
                if kc + 1 < NCK:
                    scores(kc + 1)
            avz(NCK - 1)
            MM(ps_z[1][:], ones_f[:, 0:128], acc[:, 1, :], False, True, ["ones_f", ("d_acc", 1)], [("ps_z", 1)])
            MM(ps_z[0][:], ones_f[:, 0:128], acc[:, 0, :], True, True, ["ones_f", ("d_acc", 0)], [("ps_z", 0)])
            for c in range(2):
                RECIP(rz[c][:], ps_z[c][:], [("ps_z", c)], [("d_rz", c)])
                TT("dve", t12[c][:], ps_av[c][:], rz[c][:], ALU.mult, [("ps_av", c), ("d_rz", c)], [("d_t", c)])
            STT(od[:], t12[1][:], neglam, t12[0][:], ALU.mult, ALU.add, [("d_t", 0), ("d_t", 1), "lam_w"], ["d_o"])
            ACT(dsq[:], od[:], AF.Square, ["d_o"], ["d_sq"])
            MM(ps_z[0][:], ones_f[:, 0:128], dsq[:], True, True, ["d_sq", "ones_f"], [("ps_z", 0)])
            ACT(drs[:], ps_z[0][:], AF.Sqrt, [("ps_z", 0), "eps_t"], ["d_rs"], bias=eps_t[:, 0:1], scale=1.0 / 128)
            RECIP(drs[:], drs[:], ["d_rs"], ["d_rs"])
            k = it[0] % 2
            it[0] += 1
            STT(yds[k][:], od[:], dgn[:, h:h + 1], drs[:], ALU.mult, ALU.mult, ["d_o", "d_gn", "d_rs"], [("d_yds", k)])
            DMA("sp", YDT[h * 128:(h + 1) * 128, q0:q0 + QT_], yds[k][:], [("d_yds", k)], [("YDT", h, q0)])
    es.close()

    if stop_after < 5:
        return finish()
    P.barrier()
    es = ExitStack()
    TW5 = 512 if HALF >= 512 else HALF
    wbg = sb("m_wbg", [128, 8, D], BF16)
    wbd = sb("m_wbd", [128, 8, D], BF16)
    wo = sb("m_wo", [128, 8, D], BF16)
    load_w_bf16(wbg, wbg_d.rearrange("(kc p) n -> p kc n", p=128), 8, D, "m_wbg")
    load_w_bf16(wbd, wbd_d.rearrange("(kc p) n -> p kc n", p=128), 8, D, "m_wbd")
    load_w_bf16(wo, wo_d.rearrange("(kc p) n -> p kc n", p=128), 8, D, "m_wo")
    ygt = [sb("m_yg%d" % i, [128, 8, TW5], BF16) for i in range(2)]
    ydt = [sb("m_yd%d" % i, [128, 8, TW5], BF16) for i in range(2)]
    mgt = [sb("m_mg%d" % i, [128, 16, TW5], BF16) for i in range(2)]
    x1t = [sb("m_x1%d" % i, [128, 8, TW5], F32) for i in range(2)]
    zb = sb("m_z", [128, 8, TW5], BF16)
    z1 = [sb("m_z1_%d" % i, [128, TW5], F32) for i in range(2)]
    z2 = [sb("m_z2_%d" % i, [128, TW5], F32) for i in range(2)]
    ps_a = [pst("ps_a%d" % i, [128, TW5]) for i in range(2)]
    ps_b = [pst("ps_b%d" % i, [128, TW5]) for i in range(2)]
    ps_w = [pst("ps_w%d" % i, [128, TW5]) for i in range(2)]
    YDT_v = YDT.rearrange("(c p) n -> p c n", p=128)
    X2T_v = X2T.rearrange("(c p) n -> p c n", p=128)
    for t, q0 in enumerate(range(0, HALF, TW5)):
        k = t % 2
        ygd = [("YGT", hh, qq) for hh in range(4) for qq in range(q0 - q0 % 512, q0 + TW5, 512)]
        ydd = [("YDT", hh, qq) for hh in range(8) for qq in range(q0 - q0 % QT_, q0 + TW5, QT_)]
        DMA("sp", ygt[k][:], YGT_v[:, :, q0:q0 + TW5], ygd, [("m_yg", k)])
        DMA("sp", ydt[k][:], YDT_v[:, :, q0:q0 + TW5], ydd, [("m_yd", k)])
        DMA("sp", mgt[k][:], MGT_v[:, :, q0:q0 + TW5], dram_deps("MGT", q0, q0 + TW5, TW2), [("m_mg", k)])
        DMA("sp", x1t[k][:], X1T_v[:, :, q0:q0 + TW5], dram_deps("X1T", q0, q0 + TW5, TW1), [("m_x1", k, c) for c in range(8)])
        for c in range(8):
            i = c % 2
            for kc in range(8):
                MM(ps_a[i][:], wbg[:, kc, c * 128:(c + 1) * 128], ygt[k][:, kc, :], kc == 0, kc == 7,
                   [("m_wbg", kc), ("m_yg", k)], [("ps_a", i)])
            for kc in range(8):
                MM(ps_b[i][:], wbd[:, kc, c * 128:(c + 1) * 128], ydt[k][:, kc, :], kc == 0, kc == 7,
                   [("m_wbd", kc), ("m_yd", k)], [("ps_b", i)])
            TT("dve", z1[i][:], ps_a[i][:], mgt[k][:, c, :], ALU.mult, [("ps_a", i), ("m_mg", k)], [("m_z1", i)])
            TT("dve", z2[i][:], ps_b[i][:], mgt[k][:, 8 + c, :], ALU.mult, [("ps_b", i), ("m_mg", k)], [("m_z2", i)])
            TT("pool", zb[:, c, :], z1[i][:], z2[i][:], ALU.add, [("m_z1", i), ("m_z2", i)], [("m_z", c)])
        for c in range(8):
            i = c % 2
            for kc in range(8):
                MM(ps_w[i][:], wo[:, kc, c * 128:(c + 1) * 128], zb[:, kc, :], kc == 0, kc == 7,
                   [("m_wo", kc), ("m_z", kc)], [("ps_w", i)])
            STT(x1t[k][:, c, :], ps_w[i][:], gte[:, 1, c, 0:1], x1t[k][:, c, :], ALU.mult, ALU.add,
                [("ps_w", i), "gte", ("m_x1", k, c)], [("m_x1", k, c)])
        DMA("sp", X2T_v[:, :, q0:q0 + TW5], x1t[k][:], [("m_x1", k, c) for c in range(8)], [("X2T", q0)])
    es.close()

    if stop_after < 6:
        return finish()
    P.barrier()
    tiles6 = [(i * TW1, TW1, 0, i * TW1) for i in range(HALF // TW1)]
    outT_v = outT.rearrange("(c p) n -> p c n", p=128)

    def post6(t, x_, W_, sq, ps_stat, rstd2, tmp, h2, tag):
        c0 = tiles6[t][0]
        rms_rstd(tag, tag + "brstd", lambda c: x_[:, c, :W_], 8, W_, sq, ps_stat, rstd2, 1.0 / D,
                 lambda c: [(tag + "xt", t % 2, c)])
        for hh in range(2):
            for c4 in range(4):
                c = hh * 4 + c4
                STT(h2[:, c4, :W_], x_[:, c, :W_], fg[:, c:c + 1], rstd2[:, :W_], ALU.mult, ALU.mult,
                    [(tag + "xt", t % 2, c), "fg", tag + "brstd"], [(tag + "h2", c4)])
            DMA("sp", outT_v[:, hh * 4:hh * 4 + 4, c0:c0 + W_], h2[:, :, :W_], [(tag + "h2", c4) for c4 in range(4)], [("outT", c0, hh)])

    ffn_phase("f2", f2w1, f2w3, f2w2, tiles6, TW1, X2T, 2, post6, lambda t: [("X2T", (tiles6[t][0] // TW5) * TW5)])

    P.emit()
    cons.close()
    return nc


def _const_tables(SEQ):
    half, quarter = 32, 16
    inv = (10000.0 ** (-np.arange(quarter, dtype=np.float32) / quarter)).astype(np.float32)
    t = np.arange(SEQ, dtype=np.int32)
    row = (t // GRID_W).astype(np.float32)
    col = (t % GRID_W).astype(np.float32)
    cosT = np.zeros((128, SEQ), np.float32)
    sinT = np.zeros((128, SEQ), np.float32)
    perm = np.zeros((128, 128), np.float32)
    for p in range(128):
        d = p % 64
        pos = row if d < half else col
        j = d % quarter
        ang = (pos * inv[j]).astype(np.float32)
        cosT[p] = np.cos(ang).astype(np.float32)
        sn = np.sin(ang).astype(np.float32)
        if (d % half) < quarter:
            sinT[p] = -sn
            perm[p + 16, p] = 1.0
        else:
            sinT[p] = sn
            perm[p - 16, p] = 1.0
    s = np.arange(128)
    masks = np.zeros((128, 2, 128), np.float32)
    masks[:, 0, :] = (s[:, None] <= s[None, :])
    masks[:, 1, :] = (s[:, None] >= s[None, :])
    return cosT, sinT, perm, masks


def _prep(inputs, SEQ):
    f = lambda a: np.ascontiguousarray(np.asarray(a, dtype=np.float32))
    x = f(inputs["x"]); c = f(inputs["c"]); ctx = f(inputs["ctx"]); c_ctx = f(inputs["c_ctx"])
    B = x.shape[0]
    cosT, sinT, perm, masks = _const_tables(SEQ)
    w_in0 = f(inputs["w_in"][0])
    w_in1 = w_in0.copy()
    w_in1[:, O_LR:O_LR + 16] = w_in0[:, O_LR + 16:O_LR + 32]
    w_in1[:, O_LR + 16:O_LR + 32] = w_in0[:, O_LR:O_LR + 16]
    gw2 = f(inputs["gla_gate_w2"][0]); gb = f(inputs["gla_gate_b"][0])
    shared = {
        "w_ada": f(inputs["w_ada"][0]),
        "bT_ada": f(inputs["b_ada"][0].reshape(72, 128).T),
        "ngT": f(np.stack([inputs["ffn1_norm"][0], inputs["mix_norm"][0], inputs["ffn2_norm"][0]]).reshape(3, 8, 128).transpose(2, 0, 1)),
        "fgT": f(np.asarray(inputs["final_norm"]).reshape(8, 128).T),
        "gngT": f(np.asarray(inputs["gla_out_norm"][0]).reshape(8, 128).T),
        "dngT": f(np.asarray(inputs["diff_out_norm"][0]).reshape(8, 128).T),
        "f1w1": f(inputs["ffn1_w1"][0]), "f1w3": f(inputs["ffn1_w3"][0]), "f1w2": f(inputs["ffn1_w2"][0]),
        "f2w1": f(inputs["ffn2_w1"][0]), "f2w3": f(inputs["ffn2_w3"][0]), "f2w2": f(inputs["ffn2_w2"][0]),
        "lamT": f(np.tile(np.asarray(inputs["diff_lambda"][0]).reshape(1, 256), (128, 1))),
        "permM": perm, "masks": masks, "ident": np.eye(128, dtype=np.float32),
        "wbg": f(inputs["w_branch_gla"][0]), "wbd": f(inputs["w_branch_diff"][0]), "wo": f(inputs["w_out"][0]),
    }
    in_maps = []
    for core in range(2 * B):
        b, hf = core // 2, core % 2
        xs = x[b]; cs_ = ctx[b]
        if hf == 1:
            xs = xs[::-1]; cs_ = cs_[::-1]
        xT = np.ascontiguousarray(np.concatenate([cs_, xs], axis=0).T)
        order = (0, 1) if hf == 0 else (1, 0)
        m = dict(shared)
        m["xT"] = xT
        m["cT"] = f(np.stack([c[b].reshape(8, 128).T, c_ctx.reshape(8, 128).T], axis=-1))
        m["w_in"] = w_in0 if hf == 0 else w_in1
        m["gw2"] = f(np.stack([gw2[order[0]], gw2[order[1]]]))
        m["gbT"] = f(np.stack([gb[order[0]].reshape(4, 128).T, gb[order[1]].reshape(4, 128).T], axis=1))
        m["cosT"] = cosT if hf == 0 else np.ascontiguousarray(cosT[:, ::-1])
        m["sinT"] = sinT if hf == 0 else np.ascontiguousarray(sinT[:, ::-1])
        in_maps.append(m)
    return in_maps


def _run(inputs, SEQ, dbg=False, stop_after=99):
    B = np.asarray(inputs["x"]).shape[0]
    HALF = SEQ // 2
    in_maps = _prep(inputs, SEQ)
    nc = build_nc(SEQ, dbg=dbg, stop_after=stop_after)
    res = run_bass_kernel_spmd(nc, in_maps, core_ids=list(range(2 * B)))
    out = np.empty((B, SEQ, D), np.float32)
    for core in range(2 * B):
        b, hf = core // 2, core % 2
        o = res.results[core]["outT"].T
        if hf == 0:
            out[b, :HALF] = o
        else:
            out[b, HALF:] = o[::-1]
    return out, res


def kernel(**inputs):
    SEQ = np.asarray(inputs["x"]).shape[1]
    out, _ = _run(inputs, SEQ)
    return out
```
